# Optimizing a Trainium2 kernel written in Bass

```python
import math
import jax, jax.numpy as jnp
from jax import lax
import numpy as np

D_MODEL = 1024
BATCH = 4
SEQ = 4096
DEPTH = 4
DEC_BATCH = 32
DEC_SEQ = 1
PAST_LEN = 8192
PAGE_SIZE = 128

N_MIXERS = 3
N_POOL_LAYERS = (DEPTH + 2) // 3
N_SWA_LAYERS = (DEPTH + 1) // 3
N_CONV_LAYERS = DEPTH // 3
RMS_EPS = 1e-6
LN_EPS = 1e-5
NEG_INF = -1e30
POOL_WINDOWS = (2, 4, 8, 16)
N_POOL_GROUPS = 4
POOL_GROUP_DIM = D_MODEL // N_POOL_GROUPS
POOL_HIST = max(POOL_WINDOWS) - 1
SWA_CONFIGS = ((128, 1), (512, 4), (2048, 16))
N_GROUPS = 3
HEAD_DIM = 64
HEADS_PER_GROUP = D_MODEL // HEAD_DIM
SPAN = 128
BLK = 128
N_BUCKETS = 32
MAX_DISTANCE = 2048
CONV_WIDTH = 31
CONV_HIST = CONV_WIDTH - 1
D_FF = -(-8 * D_MODEL // (3 * 256)) * 256

kernel_name = "hybrid_pool_dilattn_conv_decoder_step"


def _t5_bucket(dist):
    max_exact = N_BUCKETS // 2
    d = np.maximum(np.asarray(dist), 0)
    large = max_exact + (np.log(np.maximum(d, max_exact) / max_exact) / math.log(MAX_DISTANCE / max_exact)
                         * (N_BUCKETS - max_exact)).astype(np.int64)
    large = np.minimum(large, N_BUCKETS - 1)
    return np.where(d < max_exact, d, large).astype(np.int32)


def rmsnorm(x, g):
    xf = x.astype(jnp.float32)
    y = xf * lax.rsqrt(jnp.mean(xf * xf, axis=-1, keepdims=True) + RMS_EPS)
    return (y * g.astype(jnp.float32)).astype(x.dtype)


def swiglu(h, w_in, w_out):
    gu = h @ w_in
    return (jax.nn.silu(gu[..., :D_FF]) * gu[..., D_FF:]) @ w_out


def pool_mixer(h, hist, n_past, w_grp, scale):
    B, T, D = h.shape
    xh = jnp.concatenate([hist.astype(h.dtype), h], axis=1)
    cs = jnp.pad(jnp.cumsum(xh.astype(jnp.float32), axis=1), ((0, 0), (1, 0), (0, 0)))
    pos1 = n_past + 1 + jnp.arange(T)
    hf = h.astype(jnp.float32)
    outs = []
    for g, w in enumerate(POOL_WINDOWS):
        sl = slice(g * POOL_GROUP_DIM, (g + 1) * POOL_GROUP_DIM)
        win = cs[:, POOL_HIST + 1:POOL_HIST + 1 + T, sl] - cs[:, POOL_HIST + 1 - w:POOL_HIST + 1 - w + T, sl]
        cnt = jnp.minimum(pos1, w).astype(jnp.float32)
        outs.append(win / cnt[None, :, None] - hf[..., sl])
    d = jnp.stack(outs, axis=2)
    y = jnp.einsum('btgc,gce->btge', d, w_grp.astype(jnp.float32)).reshape(B, T, D)
    y = y * scale.astype(jnp.float32)
    return y.astype(h.dtype), xh[:, -POOL_HIST:]


def conv_mixer(h, hist, w_in, b_in, w_dw, b_dw, ln_g, ln_b, w_out, b_out):
    a = h @ w_in + b_in
    u = a[..., :D_MODEL] * jax.nn.sigmoid(a[..., D_MODEL:])
    uh = jnp.concatenate([hist.astype(u.dtype), u], axis=1)
    c = lax.conv_general_dilated(uh, w_dw[:, None, :].astype(uh.dtype), window_strides=(1,), padding='VALID',
                                 dimension_numbers=('NWC', 'WIO', 'NWC'), feature_group_count=D_MODEL) + b_dw
    cf = c.astype(jnp.float32)
    mu = jnp.mean(cf, axis=-1, keepdims=True)
    var = jnp.mean(jnp.square(cf - mu), axis=-1, keepdims=True)
    z = (cf - mu) * lax.rsqrt(var + LN_EPS) * ln_g.astype(jnp.float32) + ln_b.astype(jnp.float32)
    y = jax.nn.silu(z).astype(h.dtype) @ w_out + b_out
    return y, uh[:, -CONV_HIST:]


def _dilated_attn_prompt(q, k, v, rel_bias, g):
    B, S, H, Dh = q.shape
    _, dil = SWA_CONFIGS[g]
    L = S // dil
    nblk = -(-L // BLK)
    Lp = nblk * BLK

    def to_sub(t):
        t = t.reshape(B, L, dil, H, Dh).transpose(0, 2, 1, 3, 4)
        return jnp.pad(t, ((0, 0), (0, 0), (0, Lp - L), (0, 0), (0, 0)))

    def band(t):
        t = jnp.pad(t, ((0, 0), (0, 0), (BLK, 0), (0, 0), (0, 0))).reshape(B, dil, nblk + 1, BLK, H, Dh)
        return jnp.concatenate([t[:, :, :-1], t[:, :, 1:]], axis=3)

    def from_sub(t):
        t = t.reshape((B, dil, Lp) + t.shape[4:])[:, :, :L]
        t = jnp.moveaxis(t, 1, 2)
        return t.reshape((B, S) + t.shape[3:])

    qb = to_sub(q).reshape(B, dil, nblk, BLK, H, Dh)
    kb = band(to_sub(k))
    vb = band(to_sub(v))
    qi = np.arange(BLK)[:, None]
    kj = np.arange(2 * BLK)[None, :]
    delta = qi + BLK - kj
    band_ok = (delta >= 0) & (delta <= SPAN)
    first = (np.arange(nblk) == 0)[:, None, None] & (kj < BLK)[None]
    valid = band_ok[None] & ~first
    bucket = _t5_bucket(np.clip(delta, 0, SPAN) * dil)
    bias = jnp.transpose(rel_bias[bucket, g], (2, 0, 1)).astype(jnp.float32)
    logits = jnp.einsum('brnqhd,brnkhd->brnhqk', qb, kb, preferred_element_type=jnp.float32) * (Dh ** -0.5) + bias
    logits = jnp.where(valid[:, None], logits, NEG_INF)
    m = jnp.max(logits, axis=-1, keepdims=True)
    p = jnp.exp(logits - m)
    s = jnp.sum(p, axis=-1, keepdims=True)
    o = jnp.einsum('brnhqk,brnkhd->brnqhd', p, vb.astype(jnp.float32)) / jnp.transpose(s, (0, 1, 2, 4, 3, 5))
    lse = jnp.transpose((m + jnp.log(s))[..., 0], (0, 1, 2, 4, 3))
    return from_sub(o), from_sub(lse)


def _dilated_attn_sample(q, kbuf, vbuf, rel_bias, g):
    _, dil = SWA_CONFIGS[g]
    T = q.shape[1]
    Dh = q.shape[-1]
    n_cache = kbuf.shape[1] - T
    steps = np.arange(SPAN + 1)
    idx = n_cache + np.arange(T)[:, None] - steps[None, :] * dil
    valid = idx >= 0
    idx = np.maximum(idx, 0)
    kg = kbuf[:, idx]
    vg = vbuf[:, idx]
    bias = jnp.transpose(rel_bias[_t5_bucket(steps * dil), g]).astype(jnp.float32)
    logits = jnp.einsum('bthd,btkhd->bhtk', q, kg, preferred_element_type=jnp.float32) * (Dh ** -0.5) + bias[:, None, :]
    logits = jnp.where(valid[None, None], logits, NEG_INF)
    m = jnp.max(logits, axis=-1, keepdims=True)
    p = jnp.exp(logits - m)
    s = jnp.sum(p, axis=-1, keepdims=True)
    o = jnp.einsum('bhtk,btkhd->bthd', p, vg.astype(jnp.float32)) / jnp.transpose(s, (0, 2, 1, 3))
    lse = jnp.transpose((m + jnp.log(s))[..., 0], (0, 2, 1))
    return o, lse


def _merge_groups(outs, lses):
    o = jnp.stack(outs, axis=0)
    wgt = jax.nn.softmax(jnp.stack(lses, axis=0), axis=0)
    return jnp.einsum('gbth,gbthd->bthd', wgt, o)


def swa_prompt(h, w_qkv, w_o, rel_bias):
    B, T, D = h.shape
    qkv = (h @ w_qkv).reshape(B, T, N_GROUPS, 3, HEADS_PER_GROUP, HEAD_DIM)
    outs, lses, kv_new = [], [], []
    for g, (win, _) in enumerate(SWA_CONFIGS):
        o, l = _dilated_attn_prompt(qkv[:, :, g, 0], qkv[:, :, g, 1], qkv[:, :, g, 2], rel_bias, g)
        outs.append(o)
        lses.append(l)
        keep = min(win, T)
        kv_new.append(qkv[:, T - keep:, g, 1:3])
    y = _merge_groups(outs, lses).reshape(B, T, D).astype(h.dtype) @ w_o
    return y, kv_new


def swa_sample(h, caches, w_qkv, w_o, rel_bias):
    B, T, D = h.shape
    qkv = (h @ w_qkv).reshape(B, T, N_GROUPS, 3, HEADS_PER_GROUP, HEAD_DIM)
    outs, lses, kv_new = [], [], []
    for g in range(N_GROUPS):
        cache = caches[g].astype(qkv.dtype)
        kbuf = jnp.concatenate([cache[:, :, 0], qkv[:, :, g, 1]], axis=1)
        vbuf = jnp.concatenate([cache[:, :, 1], qkv[:, :, g, 2]], axis=1)
        o, l = _dilated_attn_sample(qkv[:, :, g, 0], kbuf, vbuf, rel_bias, g)
        outs.append(o)
        lses.append(l)
        kv_new.append(qkv[:, :, g, 1:3])
    y = _merge_groups(outs, lses).reshape(B, T, D).astype(h.dtype) @ w_o
    return y, kv_new


def setup_inputs(seed: int = 0) -> dict:
    key = jax.random.key(seed)
    ks = jax.random.split(key, 24)
    f32 = jnp.float32

    def nrm(k, shape, scale=1.0):
        return jax.random.normal(k, shape, f32) * scale

    D = D_MODEL
    cache_shape = lambda w: (N_SWA_LAYERS, DEC_BATCH, min(w, PAST_LEN), 2, HEADS_PER_GROUP, HEAD_DIM)
    return {
        "x_prompt": nrm(ks[0], (BATCH, SEQ, D)),
        "x_sample": nrm(ks[1], (DEC_BATCH, DEC_SEQ, D)),
        "state_pool": nrm(ks[2], (N_POOL_LAYERS, DEC_BATCH, POOL_HIST, D)),
        "cache_swa_g0": nrm(ks[3], cache_shape(SWA_CONFIGS[0][0])),
        "cache_swa_g1": nrm(ks[4], cache_shape(SWA_CONFIGS[1][0])),
        "cache_swa_g2": nrm(ks[5], cache_shape(SWA_CONFIGS[2][0])),
        "state_conv": nrm(ks[6], (N_CONV_LAYERS, DEC_BATCH, CONV_HIST, D), 0.5),
        "norm_g": 1.0 + nrm(ks[7], (DEPTH, 4, D), 0.1),
        "w_ffn_in": nrm(ks[8], (DEPTH, D, 2 * D_FF), D ** -0.5),
        "w_ffn_out": nrm(ks[9], (DEPTH, D_FF, D), D_FF ** -0.5),
        "pool_w": nrm(ks[10], (N_POOL_LAYERS, N_POOL_GROUPS, POOL_GROUP_DIM, POOL_GROUP_DIM), POOL_GROUP_DIM ** -0.5),
        "pool_scale": 1.0 + nrm(ks[11], (N_POOL_LAYERS, D), 0.1),
        "w_qkv": nrm(ks[12], (N_SWA_LAYERS, D, N_GROUPS * 3 * D), D ** -0.5),
        "w_o": nrm(ks[13], (N_SWA_LAYERS, D, D), D ** -0.5),
        "rel_bias": nrm(ks[14], (N_BUCKETS, N_GROUPS, HEADS_PER_GROUP), 0.5),
        "conv_w_in": nrm(ks[15], (N_CONV_LAYERS, D, 2 * D), D ** -0.5),
        "conv_b_in": nrm(ks[16], (N_CONV_LAYERS, 2 * D), 0.02),
        "conv_w_dw": nrm(ks[17], (N_CONV_LAYERS, CONV_WIDTH, D), CONV_WIDTH ** -0.5),
        "conv_b_dw": nrm(ks[18], (N_CONV_LAYERS, D), 0.02),
        "conv_ln_g": 1.0 + nrm(ks[19], (N_CONV_LAYERS, D), 0.1),
        "conv_ln_b": nrm(ks[20], (N_CONV_LAYERS, D), 0.02),
        "conv_w_out": nrm(ks[21], (N_CONV_LAYERS, D, D), D ** -0.5),
        "conv_b_out": nrm(ks[22], (N_CONV_LAYERS, D), 0.02),
    }


def reference(x_prompt, x_sample, state_pool, cache_swa_g0, cache_swa_g1, cache_swa_g2, state_conv,
              norm_g, w_ffn_in, w_ffn_out, pool_w, pool_scale, w_qkv, w_o, rel_bias,
              conv_w_in, conv_b_in, conv_w_dw, conv_b_dw, conv_ln_g, conv_ln_b, conv_w_out, conv_b_out):
    swa_caches = (cache_swa_g0, cache_swa_g1, cache_swa_g2)
    xp, xs = x_prompt, x_sample
    Bp = xp.shape[0]
    pool_p, pool_s, conv_p, conv_s = [], [], [], []
    swa_p = [[] for _ in range(N_GROUPS)]
    swa_s = [[] for _ in range(N_GROUPS)]
    for i in range(DEPTH):
        kind, j = i % N_MIXERS, i // N_MIXERS
        hp = rmsnorm(xp, norm_g[i, 0])
        hs = rmsnorm(xs, norm_g[i, 0])
        if kind == 0:
            mp, sp = pool_mixer(hp, jnp.zeros((Bp, POOL_HIST, D_MODEL), hp.dtype), 0, pool_w[j], pool_scale[j])
            ms, ss = pool_mixer(hs, state_pool[j], PAST_LEN, pool_w[j], pool_scale[j])
            pool_p.append(sp)
            pool_s.append(ss)
        elif kind == 1:
            mp, kvp = swa_prompt(hp, w_qkv[j], w_o[j], rel_bias)
            ms, kvs = swa_sample(hs, [c[j] for c in swa_caches], w_qkv[j], w_o[j], rel_bias)
            for g in range(N_GROUPS):
                swa_p[g].append(kvp[g])
                swa_s[g].append(kvs[g])
        else:
            cw = (conv_w_in[j], conv_b_in[j], conv_w_dw[j], conv_b_dw[j], conv_ln_g[j], conv_ln_b[j],
                  conv_w_out[j], conv_b_out[j])
            mp, sp = conv_mixer(hp, jnp.zeros((Bp, CONV_HIST, D_MODEL), hp.dtype), *cw)
            ms, ss = conv_mixer(hs, state_conv[j], *cw)
            conv_p.append(sp)
            conv_s.append(ss)
        xp = xp + rmsnorm(mp, norm_g[i, 1])
        xs = xs + rmsnorm(ms, norm_g[i, 1])
        xp = xp + rmsnorm(swiglu(rmsnorm(xp, norm_g[i, 2]), w_ffn_in[i], w_ffn_out[i]), norm_g[i, 3])
        xs = xs + rmsnorm(swiglu(rmsnorm(xs, norm_g[i, 2]), w_ffn_in[i], w_ffn_out[i]), norm_g[i, 3])
    return (xp, xs,
            jnp.stack(pool_p), jnp.stack(pool_s),
            jnp.stack(swa_p[0]), jnp.stack(swa_p[1]), jnp.stack(swa_p[2]),
            jnp.stack(swa_s[0]), jnp.stack(swa_s[1]), jnp.stack(swa_s[2]),
            jnp.stack(conv_p), jnp.stack(conv_s))
```

```python
import contextlib
import numpy as np
import ml_dtypes
import concourse.bass as bass
import concourse.mybir as mybir
from concourse.bass_utils import run_bass_kernel_spmd

F32 = mybir.dt.float32
BF16 = mybir.dt.bfloat16
AF = mybir.ActivationFunctionType
ALU = mybir.AluOpType
ENGS = ["pe", "act", "dve", "pool", "sp"]

D = 1024
KC = 8
DFF = 2816
FC = 22
NE = 2180
SMP0 = 2176
SMAX = 1156
RMS_EPS = 1e-6
LN_EPS = 1e-5
DILS = (1, 4, 16)
WINS = (128, 512, 2048)
NEG = -30000.0
QROW = 6144
VROW = 3 * 16 * 65
UROW = 16 * 65


class Buf:
    __slots__ = ("name", "lw", "rd", "multi")

    def __init__(self, name, alias=(), multi=False):
        self.name = name
        self.lw = [] if multi else None
        self.rd = list(alias)
        self.multi = multi


class Op:
    __slots__ = ("eng", "fn", "waits", "signal", "cnt", "isdma", "slot", "sem", "semval", "prevval", "seq")


class Prog:
    def __init__(self, nc, ksem=8):
        self.nc = nc
        self.ops = {e: [] for e in ENGS}
        self.ksem = ksem
        self.ndma = {e: 0 for e in ENGS}
        self.seq = 0
        self.frontier = []

    def new_phase(self):
        fr = []
        for e in ENGS:
            last = None
            lastd = {}
            for o in self.ops[e]:
                if o.isdma:
                    lastd[o.slot] = o
                else:
                    last = o
            if last is not None:
                fr.append(last)
            fr.extend(lastd.values())
        self.frontier = fr

    def buf(self, name, multi=False):
        return Buf(name, alias=self.frontier, multi=multi)

    def op(self, eng, fn, reads=(), writes=(), dma=False):
        o = Op()
        o.eng, o.fn, o.isdma, o.signal, o.waits, o.cnt = eng, fn, dma, dma, [], 0
        o.seq = self.seq
        self.seq += 1
        if dma:
            o.slot = self.ndma[eng] % self.ksem
            o.semval = 16 * (self.ndma[eng] // self.ksem + 1)
            o.prevval = o.semval - 16
            self.ndma[eng] += 1
        hard, war = set(), set()
        for b in reads:
            if b.multi:
                hard.update(b.lw)
            elif b.lw is not None:
                hard.add(b.lw)
        for b in writes:
            if b.multi:
                war.update(b.rd)
            else:
                if b.lw is not None:
                    hard.add(b.lw)
                war.update(b.rd)
        for d in hard | war:
            if d is o:
                continue
            if (not d.isdma) and (not dma) and d.eng == eng and eng == "pe":
                continue
            d.signal = True
            o.waits.append(d)
        for b in reads:
            if not dma:
                b.rd = [r for r in b.rd if r.isdma or r.eng != eng]
            b.rd.append(o)
        for b in writes:
            if b.multi:
                b.lw.append(o)
            else:
                b.lw = o
                b.rd = []
        self.ops[eng].append(o)
        return o

    def dma(self, eng, out, in_, reads=(), writes=(), **kw):
        return self.op(eng, lambda e: e.dma_start(out=out, in_=in_, **kw), reads, writes, dma=True)

    def finalize(self, st):
        nc = self.nc
        esem = {e: st.enter_context(nc.semaphore("es_" + e)) for e in ENGS}
        dsem = {e: [st.enter_context(nc.semaphore(f"ds_{e}{i}")) for i in range(self.ksem)] for e in ENGS}
        finals = []
        for e in ENGS:
            c = 0
            lastd = {}
            for o in self.ops[e]:
                if o.isdma:
                    o.sem = dsem[e][o.slot]
                    lastd[o.slot] = (o.sem, o.semval)
                elif o.signal:
                    c += 1
                    o.cnt = c
            finals.append((esem[e], c))
            finals.extend(lastd.values())
        block = st.enter_context(nc.Block())

        def run(eo, e):
            seen = {}

            def need(sem, val):
                if val > 0 and seen.get(id(sem), 0) < val:
                    eo.wait_ge(sem, val)
                    seen[id(sem)] = val

            for o in self.ops[e]:
                if o.isdma:
                    need(o.sem, o.prevval)
                for d in sorted(o.waits, key=lambda d: d.seq):
                    if d.isdma:
                        need(d.sem, d.semval)
                    else:
                        need(esem[d.eng], d.cnt)
                inst = o.fn(eo)
                if o.isdma:
                    inst.then_inc(o.sem, 16)
                elif o.signal:
                    inst.then_inc(esem[e], 1)
            if e == "sp":
                for sem, val in finals:
                    need(sem, val)

        @block.tensor
        def _(eo):
            run(eo, "pe")

        @block.scalar
        def _(eo):
            run(eo, "act")

        @block.vector
        def _(eo):
            run(eo, "dve")

        @block.gpsimd
        def _(eo):
            run(eo, "pool")

        @block.sync
        def _(eo):
            run(eo, "sp")


def I(name, *a, **kw):
    return lambda e: getattr(e, name)(*a, **kw)


def ntiles(c0, c1):
    out = []
    e = min(c1, SMP0)
    a = c0
    while a < e:
        n = min(512, e - a)
        out.append((a, n))
        a += n
    if c1 > SMP0:
        out.append((SMP0, c1 - SMP0))
    return out


def t5_bucket(dist):
    import math
    max_exact = 16
    d = np.maximum(np.asarray(dist), 0)
    large = max_exact + (np.log(np.maximum(d, max_exact) / max_exact) / math.log(2048 / max_exact) * (32 - max_exact)).astype(np.int64)
    large = np.minimum(large, 31)
    return np.where(d < max_exact, d, large).astype(np.int32)


def static_tables():
    oh = np.zeros((3, 32, 512), np.float32)
    neg = np.zeros((3, 1, 512), np.float32)
    for g, dil in enumerate(DILS):
        for j in range(256):
            if j <= 127:
                oh[g, t5_bucket((1 + j) * dil), j] = 1.0
            else:
                neg[g, 0, j] = NEG
            if 127 <= j <= 254:
                oh[g, t5_bucket((j - 127) * dil), 256 + j] = 1.0
            else:
                neg[g, 0, 256 + j] = NEG
    return oh, neg


def build(layers=(0, 1, 2, 3), skip_ffn=False, attn_stop=None):
    nc = bass.Bass("TRN2", target_bir_lowering=False)
    P = Prog(nc)
    st = contextlib.ExitStack()

    def din(name, shape, dt=F32):
        return nc.dram_tensor(name, list(shape), dt, kind="ExternalInput").ap()

    def dout(name, shape, dt=F32):
        return nc.dram_tensor(name, list(shape), dt, kind="ExternalOutput").ap()

    def dscr(name, shape, dt):
        return nc.dram_tensor(name, list(shape), dt, kind="Internal").ap()

    xin = din("xin", [4096, D])
    xsm = din("xsm", [4, D])
    flag_d = din("flag", [128, 16])
    invc_d = din("invc", [128, 2 * 4 * 16])
    spool_d = din("spool", [2, 60, D])
    sconv_d = din("sconv", [120, D])
    cache_d = [din(f"cache{g}", [4, WINS[g], 2, D]) for g in range(3)]
    vecs_d = din("vecs", [64, D])
    w_in_d = din("w_ffn_in", [4, D, 2 * DFF])
    w_out_d = din("w_ffn_out", [4, DFF, D])
    pool_w_d = din("pool_w", [2, 4, 256, 256])
    w_qkv_d = din("w_qkv", [D, 9216])
    w_o_d = din("w_o", [D, D])
    relb_d = din("rel_bias", [32, 48])
    cw_in_d = din("conv_w_in", [D, 2 * D])
    cw_out_d = din("conv_w_out", [D, D])
    oh_d = din("oh", [3, 32, 512])
    negr_d = din("negr", [3, 1, 512])

    y_d = dout("y", [2048, D])
    ys_d = dout("ys", [4, D])
    poolp_d = dout("poolp", [2, 15, D])
    pools_d = dout("pools", [2, 4, 15, D])
    swp_d = [dout(f"swp{g}", [WINS[g], 2, D]) for g in range(3)]
    sws_d = [dout(f"sws{g}", [4, 2, D]) for g in range(3)]
    convp_d = dout("convp", [30, D])
    convs_d = dout("convs", [4, 30, D])

    h1c_d = dscr("h1c", [128, KC, 2048], BF16)
    qk_d = dscr("qkd", [4100, QROW], BF16)
    v_d = dscr("vd", [4100, VROW], BF16)
    uz_d = dscr("uzd", [3, 2176, UROW], F32)
    vec_d = dscr("vecd", [3, 16, 512], F32)

    wb_in = dscr("wb_in", [4, D, 2 * DFF], BF16)
    wb_out = dscr("wb_out", [4, DFF, D], BF16)
    wb_pool = dscr("wb_pool", [2, 4, 256, 256], BF16)
    wb_qkv = dscr("wb_qkv", [D, 9216], BF16)
    wb_o = dscr("wb_o", [D, D], BF16)
    wb_cin = dscr("wb_cin", [D, 2 * D], BF16)
    wb_cout = dscr("wb_cout", [D, D], BF16)
    bWIN = [Buf(f"wbin{i}") for i in range(4)]
    bWOUT = [Buf(f"wbout{i}") for i in range(4)]
    bWPOOL = [Buf("wbpool0"), Buf("wbpool1")]
    bWQKV, bWO_, bWCIN, bWCOUT = Buf("wbqkv"), Buf("wbo"), Buf("wbcin"), Buf("wbcout")

    def cast2d(dst, src, buf):
        P.dma("pool", dst.rearrange("(p a) n -> p (a n)", p=128), src.rearrange("(p a) n -> p (a n)", p=128), writes=[buf])

    bANCH = [Buf(f"anchor{i}") for i in range(3)]

    def cast_group(i):
        rd = [] if i == 0 else [bANCH[i - 1]]
        def c2(dst, src, buf):
            P.dma("pool", dst.rearrange("(p a) n -> p (a n)", p=128), src.rearrange("(p a) n -> p (a n)", p=128), reads=rd, writes=[buf])
        if i == 0:
            c2(wb_pool[0].rearrange("g k n -> (g k) n"), pool_w_d[0].rearrange("g k n -> (g k) n"), bWPOOL[0])
            c2(wb_in[0], w_in_d[0], bWIN[0])
            c2(wb_out[0], w_out_d[0], bWOUT[0])
        elif i == 1:
            c2(wb_qkv, w_qkv_d, bWQKV)
            c2(wb_o, w_o_d, bWO_)
            c2(wb_in[1], w_in_d[1], bWIN[1])
            c2(wb_out[1], w_out_d[1], bWOUT[1])
        elif i == 2:
            c2(wb_cin, cw_in_d, bWCIN)
            c2(wb_cout, cw_out_d, bWCOUT)
            c2(wb_in[2], w_in_d[2], bWIN[2])
            c2(wb_out[2], w_out_d[2], bWOUT[2])
        else:
            c2(wb_pool[1].rearrange("g k n -> (g k) n"), pool_w_d[1].rearrange("g k n -> (g k) n"), bWPOOL[1])
            c2(wb_in[3], w_in_d[3], bWIN[3])
            c2(wb_out[3], w_out_d[3], bWOUT[3])

    def anchor(i):
        dve(I("tensor_copy", out=SMALL[:, 200:201], in_=SMALL[:, 201:202]), reads=[bRS], writes=[bANCH[i]])
        cast_group(i + 1)

    def sb(name, shape, dt):
        return st.enter_context(nc.sbuf_tensor(name, list(shape), dt))

    XT = sb("XT", [128, KC, NE], F32)
    AR = sb("AR", [128, 28000], F32)
    VT = sb("VT", [128, KC, 64], F32)
    IDF = sb("IDF", [128, 128], F32)
    IDB = sb("IDB", [128, 128], BF16)
    JF = sb("JF", [128, 128], F32)
    ONB = sb("ONB", [128, 128], BF16)
    ONF = sb("ONF", [128, 128], F32)
    FLG = sb("FLG", [128, 16], F32)
    INVC = sb("INVC", [128, 2, 4, 16], F32)
    SQS = sb("SQS", [128, 4096], BF16)
    SQ8 = SQS[:].rearrange("p (c n) -> p c n", c=KC)
    SIG = SQS[:, 0:2048].bitcast(F32).rearrange("p (s n) -> p s n", s=2)
    RSTD = sb("RSTD", [128, SMAX], F32)
    STG = sb("STG", [128, 2, D], F32)
    HT15 = sb("HT15", [128, KC, 15], F32)
    UT30 = sb("UT30", [128, KC, 30], F32)
    SMALL = sb("SMALL", [128, 256], F32)
    PS = st.enter_context(nc.psum_tensor("PS", [128, 8, 512], F32))

    PB = [Buf(f"ps{b}") for b in range(8)]
    XB = [Buf(f"xt{t}") for t in range(18)]
    bVT, bIDF, bIDB, bJF, bONB, bONF, bNH, bFLG, bINVC = (Buf(n) for n in "VT IDF IDB JF ONB ONF NH FLG INVC".split())
    bSQ2 = Buf("sq2")
    bSIG = [Buf("sig0"), Buf("sig1")]
    bSTG = [Buf("stg0"), Buf("stg1")]
    bHT15, bUT30 = Buf("ht15"), Buf("ut30")
    bRS = Buf("rstd")
    bSMALL = Buf("small")
    bOUT = Buf("out", multi=True)
    bH1C = Buf("h1c", multi=True)
    bQK = Buf("qkd", multi=True)
    bV = Buf("vd", multi=True)
    bUZ = Buf("uzd", multi=True)
    bVEC = Buf("vecd", multi=True)

    def xtb(a, n):
        out = []
        for t in range(17):
            if a < (t + 1) * 128 and a + n > t * 128:
                out.append(XB[t])
        if a + n > SMP0:
            out.append(XB[17])
        return out

    def act(fn, reads=(), writes=()):
        return P.op("act", fn, reads, writes)

    def dve(fn, reads=(), writes=()):
        return P.op("dve", fn, reads, writes)

    def pool(fn, reads=(), writes=()):
        return P.op("pool", fn, reads, writes)

    def pe(fn, reads=(), writes=()):
        return P.op("pe", fn, reads, writes)

    def arv(off, nbytes, dt, pat=None, **kw):
        assert off % 4 == 0 and nbytes % 4 == 0 and off + nbytes <= 28000 * 4, (off, nbytes)
        v = AR[:, off // 4:(off + nbytes) // 4]
        if dt == BF16:
            v = v.bitcast(BF16)
        if pat:
            v = v.rearrange(pat, **kw)
        return v

    stg_ctr = [0]
    bank2_ctr = [0]

    pool(I("memset", IDF[:], 1.0), writes=[bIDF])
    pool(I("affine_select", out=IDF[:], in_=IDF[:], pattern=[[-1, 128]], compare_op=ALU.is_equal, fill=0.0, base=0, channel_multiplier=1), reads=[bIDF], writes=[bIDF])
    pool(I("tensor_copy", out=IDB[:], in_=IDF[:]), reads=[bIDF], writes=[bIDB])
    pool(I("memset", JF[:], 1.0), writes=[bJF])
    pool(I("affine_select", out=JF[:], in_=JF[:], pattern=[[1, 128]], compare_op=ALU.is_equal, fill=0.0, base=-127, channel_multiplier=1), reads=[bJF], writes=[bJF])
    pool(I("memset", ONB[:], 1.0), writes=[bONB])
    pool(I("memset", ONF[:], 1.0), writes=[bONF])
    pool(I("memset", SMALL[:], 0.0), writes=[bSMALL])
    P.dma("sp", FLG[:], flag_d, writes=[bFLG])
    P.dma("sp", INVC[:].rearrange("p a g j -> p (a g j)"), invc_d, writes=[bINVC])

    def from_tok(src_ap, m, dst_half, reads_extra=(), writes=()):
        s = stg_ctr[0] % 2
        stg_ctr[0] += 1
        b0 = 2 * (bank2_ctr[0] % 2)
        bank2_ctr[0] += 1
        P.dma("pool", STG[0:m, s, :], src_ap, reads=list(reads_extra), writes=[bSTG[s]])
        for c in range(KC):
            pe(I("transpose", out=PS[:, b0 + c // 4, (c % 4) * m:(c % 4 + 1) * m], in_=STG[0:m, s, c * 128:(c + 1) * 128], identity=IDF[0:m, 0:m]),
               reads=[bSTG[s], bIDF], writes=[PB[b0 + c // 4]])
        act(I("activation", out=dst_half(0), in_=PS[:, b0, 0:4 * m].rearrange("p (c n) -> p c n", c=4), func=AF.Copy), reads=[PB[b0]], writes=writes)
        dve(I("tensor_copy", out=dst_half(1), in_=PS[:, b0 + 1, 0:4 * m].rearrange("p (c n) -> p c n", c=4)), reads=[PB[b0 + 1]], writes=writes)

    def to_tok(src_chunk, m, dst_ap, reads=()):
        s = stg_ctr[0] % 2
        stg_ctr[0] += 1
        b0 = 2 * (bank2_ctr[0] % 2)
        bank2_ctr[0] += 1
        for c in range(KC):
            pe(I("transpose", out=PS[0:m, b0 + c // 4, (c % 4) * 128:(c % 4 + 1) * 128], in_=src_chunk(c), identity=IDF[:]),
               reads=list(reads) + [bIDF], writes=[PB[b0 + c // 4]])
        act(I("activation", out=STG[0:m, s, 0:512], in_=PS[0:m, b0, :], func=AF.Copy), reads=[PB[b0]], writes=[bSTG[s]])
        dve(I("tensor_copy", out=STG[0:m, s, 512:1024], in_=PS[0:m, b0 + 1, :]), reads=[PB[b0 + 1]], writes=[bSTG[s]])
        P.dma("act", dst_ap, STG[0:m, s, :], reads=[bSTG[s]], writes=[bOUT])

    from_tok(vecs_d, 64, lambda h: VT[:, 4 * h:4 * h + 4, :], writes=[bVT])

    def vt(c, idx):
        return VT[:, c, idx:idx + 1]

    def G(li, k):
        return li * 4 + k

    def stats(src_half, n, rs_ap, src_bufs, rsb, eps=RMS_EPS, bank=6):
        sqb = [bSIG[0], bSIG[1], bSQ2]
        for h in range(2):
            act(I("activation", out=SQ8[:, 4 * h:4 * h + 4, :n], in_=src_half(h), func=AF.Square), reads=src_bufs, writes=sqb)
        for c in range(KC):
            pe(I("matmul", out=PS[:, bank, :n], lhsT=ONB[:], rhs=SQ8[:, c, :n], start=(c == 0), stop=(c == KC - 1)), reads=sqb + [bONB], writes=[PB[bank]])
        dve(I("tensor_scalar", out=rs_ap, in0=PS[:, bank, :n], scalar1=1.0 / D, scalar2=eps, op0=ALU.mult, op1=ALU.add), reads=[PB[bank]], writes=[rsb])
        act(I("activation", out=rs_ap, in_=rs_ap, func=AF.Sqrt), reads=[rsb], writes=[rsb])
        dve(I("reciprocal", out=rs_ap, in_=rs_ap), reads=[rsb], writes=[rsb])

    bRS3 = [Buf(f"rstd{i}") for i in range(4)]

    def prenorm(gi, c0, c1, dst, dst_buf_):
        tl = ntiles(c0, c1)
        for ti_, (a, n) in enumerate(tl):
            la = a - c0
            stats(lambda h: XT[:, 4 * h:4 * h + 4, a:a + n], n, RSTD[:, la:la + n], xtb(a, n), bRS3[ti_])
        for ti_, (a, n) in enumerate(tl):
            la = a - c0
            xb = xtb(a, n)
            dst_buf = dst_buf_[ti_] if isinstance(dst_buf_, list) else dst_buf_
            for c in range(KC):
                dve(I("scalar_tensor_tensor", out=dst(c, la, n), in0=XT[:, c, a:a + n], scalar=vt(c, gi), in1=RSTD[:, la:la + n], op0=ALU.mult, op1=ALU.mult),
                    reads=xb + [bVT, bRS3[ti_]], writes=[dst_buf])

    def postnorm_residual(gi, c0, c1, yT, ybufs):
        tl = ntiles(c0, c1)
        for ti_, (a, n) in enumerate(tl):
            la = a - c0
            stats(lambda h: yT("h%d" % h, la, n), n, RSTD[:, la:la + n], ybufs, bRS3[ti_])
        for ti_, (a, n) in enumerate(tl):
            la = a - c0
            xb = xtb(a, n)
            for c in range(KC):
                dve(I("tensor_tensor", out=yT(c, la, n), in0=yT(c, la, n), in1=RSTD[:, la:la + n], op=ALU.mult), reads=ybufs + [bRS3[ti_]], writes=ybufs)
            for c in range(KC):
                dve(I("scalar_tensor_tensor", out=XT[:, c, a:a + n], in0=yT(c, la, n), scalar=vt(c, gi), in1=XT[:, c, a:a + n], op0=ALU.mult, op1=ALU.add),
                    reads=ybufs + xb + [bVT], writes=xb)

    R1, R2, R3, RW = 0, 18496, 36992, 36992 + 50864
    assert RW + 23552 <= 112000

    def yT_view():
        lo = arv(R1, 18496, F32, "p (c n) -> p c n", c=4)
        hi = arv(R2, 18496, F32, "p (c n) -> p c n", c=4)
        def f(c, la, n):
            if isinstance(c, str):
                return (lo if c == "h0" else hi)[:, :, la:la + n]
            return (lo if c < 4 else hi)[:, c % 4, la:la + n]
        return f

    def ffn(li, c0, c1):
        if skip_ffn:
            return
        P.new_phase()
        bR1, bR2, bA = P.buf("R1"), P.buf("R2"), P.buf("actT")
        bWA = [P.buf(f"wa{i}") for i in range(2)]
        hT = arv(R1, 18496, BF16, "p (c n) -> p c n", c=KC)
        aT = arv(R3, 50864, BF16, "p (c n) -> p c n", c=FC)
        WA = [arv(RW + 8192 * i, 8192, BF16, "p (k t n) -> p k t n", k=KC, t=2) for i in range(2)]
        WB = [arv(RW + 16384, 5632, BF16, "p (j n) -> p j n", j=FC),
              STG[:].rearrange("p s n -> p (s n)")[:, 0:1408].bitcast(BF16).rearrange("p (j n) -> p j n", j=FC)]
        bWB = [[P.buf("wb0")], bSTG]
        yT = yT_view()
        tiles = ntiles(c0, c1)
        bH = [P.buf(f"hT{i}") for i in range(len(tiles))]
        prenorm(G(li, 2), c0, c1, lambda c, la, n: hT[:, c, la:la + n], bH)
        wsrc = wb_in[li].rearrange("(k p) n -> p k n", p=128)
        cnt = 0
        for j in range(FC):
            s = (j // 2) % 2
            jj = j % 2
            if jj == 0:
                jp = j // 2
                P.dma("sp", WA[s][:, :, 0, :], wsrc[:, :, jp * 256:(jp + 1) * 256], reads=[bWIN[li]], writes=[bWA[s]])
                P.dma("sp", WA[s][:, :, 1, :], wsrc[:, :, DFF + jp * 256:DFF + (jp + 1) * 256], reads=[bWIN[li]], writes=[bWA[s]])
            for ti_, (a, n) in enumerate(tiles):
                la = a - c0
                t = cnt % 2
                cnt += 1
                bg, bu = 2 * t, 2 * t + 1
                for k in range(KC):
                    pe(I("matmul", out=PS[:, bg, :n], lhsT=WA[s][:, k, 0, jj * 128:(jj + 1) * 128], rhs=hT[:, k, la:la + n], start=(k == 0), stop=(k == KC - 1)),
                       reads=[bWA[s], bH[ti_]], writes=[PB[bg]])
                for k in range(KC):
                    pe(I("matmul", out=PS[:, bu, :n], lhsT=WA[s][:, k, 1, jj * 128:(jj + 1) * 128], rhs=hT[:, k, la:la + n], start=(k == 0), stop=(k == KC - 1)),
                       reads=[bWA[s], bH[ti_]], writes=[PB[bu]])
                act(I("activation", out=SIG[:, t, :n], in_=PS[:, bg, :n], func=AF.Silu), reads=[PB[bg]], writes=[bSIG[t]])
                dve(I("tensor_tensor", out=aT[:, j, la:la + n], in0=SIG[:, t, :n], in1=PS[:, bu, :n], op=ALU.mult), reads=[bSIG[t], PB[bu]], writes=[bA])
        osrc = wb_out[li].rearrange("(j p) n -> p j n", p=128)
        cnt = 0
        for c in range(KC):
            s = c % 2
            P.dma("sp", WB[s][:, 0:11, :], osrc[:, 0:11, c * 128:(c + 1) * 128], reads=[bWOUT[li]], writes=bWB[s])
            P.dma("sp", WB[s][:, 11:22, :], osrc[:, 11:22, c * 128:(c + 1) * 128], reads=[bWOUT[li]], writes=bWB[s])
            yb = bR1 if c < 4 else bR2
            for (a, n) in tiles:
                la = a - c0
                bk = (4, 5, 7)[cnt % 3]
                cnt += 1
                for j in range(FC):
                    pe(I("matmul", out=PS[:, bk, :n], lhsT=WB[s][:, j, :], rhs=aT[:, j, la:la + n], start=(j == 0), stop=(j == FC - 1)),
                       reads=bWB[s] + [bA], writes=[PB[bk]])
                act(I("activation", out=yT(c, la, n), in_=PS[:, bk, :n], func=AF.Copy), reads=[PB[bk]], writes=[yb] + (bH if c < 4 else []))
        postnorm_residual(G(li, 3), c0, c1, yT, [bR1, bR2])

    def load_tiles(row0, col0, nt):
        for t in range(nt):
            col = col0 + t * 128
            from_tok(xin[row0 + t * 128:row0 + (t + 1) * 128, :], 128, lambda h: XT[:, 4 * h:4 * h + 4, col:col + 128], writes=xtb(col, 128))

    def load_samples():
        from_tok(xsm, 4, lambda h: XT[:, 4 * h:4 * h + 4, SMP0:SMP0 + 4], writes=[XB[17]])

    def pool_layer(li, j, c0, c1, left, fix, flag_halo, save_tail, emit_out, samples):
        P.new_phase()
        S = c1 - c0
        W = 15 + S
        HF = arv(R3, 8 * 1171 * 4, F32, "p (c n) -> p c n", c=KC)
        TA = arv(R3 + 8 * 1171 * 4, 1171 * 4, F32)
        TB_ = arv(R3 + 9 * 1171 * 4, 1171 * 4, F32)
        DG = [arv(RW + 4096 + 4624 * i, 4624, BF16, "p (k n) -> p k n", k=2) for i in range(2)]
        PW = arv(RW, 4096, BF16, "p (g k n) -> p g k n", g=4, k=2)
        bHF, bTA, bTB, bPW = P.buf("HF"), P.buf("TA"), P.buf("TB"), P.buf("PW")
        bDG = [P.buf("dg0"), P.buf("dg1")]
        bR1, bR2 = P.buf("R1"), P.buf("R2")
        yT = yT_view()
        P.dma("sp", PW, wb_pool[j].rearrange("g (k p) n -> p g k n", p=128), reads=[bWPOOL[j]], writes=[bPW])
        prenorm(G(li, 0), c0, c1, lambda c, la, n: HF[:, c, 15 + la:15 + la + n], bHF)
        if left == "zero":
            pool(I("memset", HF[:, :, 0:15], 0.0), writes=[bHF])
        else:
            pool(I("tensor_copy", out=HF[:, :, 0:15], in_=HT15[:]), reads=[bHT15], writes=[bHF])
        if flag_halo:
            pool(I("tensor_scalar", out=HF[:, :, 15:15 + 128], in0=HF[:, :, 15:15 + 128], scalar1=FLG[:, 0:1], scalar2=None, op0=ALU.mult), reads=[bHF, bFLG], writes=[bHF])
        ntok = S - (4 if samples else 0)
        if save_tail:
            pool(I("tensor_copy", out=HT15[:], in_=HF[:, :, ntok:15 + ntok]), reads=[bHF], writes=[bHT15])
        if emit_out:
            to_tok(lambda c: HF[:, c, ntok:15 + ntok], 15, poolp_d[j], reads=[bHF])
        if samples:
            HS = arv(RW + 13344, 8 * 64 * 4, F32, "p (c n) -> p c n", c=KC)
            SA = arv(RW + 13344 + 2048, 256, F32)
            SB_ = arv(RW + 13344 + 2304, 256, F32)
            bHS = P.buf("HS")
            from_tok(spool_d[j], 60, lambda h: HS[:, 4 * h:4 * h + 4, :].rearrange("p c (b t) -> p c b t", t=16)[:, :, :, 0:15], writes=[bHS])
            dve(I("tensor_copy", out=HS[:].rearrange("p c (b t) -> p c b t", t=16)[:, :, :, 15], in_=HF[:, :, 15 + ntok:15 + S]), reads=[bHF], writes=[bHS])
            P.dma("sp", pools_d[j][:, 0:14, :], spool_d[j].rearrange("(b t) d -> b t d", t=15)[:, 1:15, :], writes=[bOUT])
            to_tok(lambda c: HF[:, c, 15 + ntok:15 + S], 4, pools_d[j][:, 14, :], reads=[bHF])
        for g in range(4):
            w = 2 << g
            for kk in range(2):
                c = 2 * g + kk
                src = HF[:, c, :]
                bufs = [bHF]
                shift = 1
                cur, curb = src, bHF
                for lv in range(g + 1):
                    dst, dstb = (TA, bTA) if lv % 2 == 0 else (TB_, bTB)
                    lo = 2 * shift - 1
                    dve(I("tensor_tensor", out=dst[:, lo:W], in0=cur[:, lo:W], in1=cur[:, lo - shift:W - shift], op=ALU.add),
                        reads=[curb], writes=[dstb])
                    cur, curb = dst, dstb
                    shift *= 2
                dve(I("scalar_tensor_tensor", out=DG[g % 2][:, kk, 0:S], in0=cur[:, 15:W], scalar=1.0 / w, in1=HF[:, c, 15:W], op0=ALU.mult, op1=ALU.subtract),
                    reads=[curb, bHF], writes=[bDG[g % 2]])
                for (fc, pos) in fix:
                    dve(I("tensor_tensor", out=SMALL[:, 0:15], in0=cur[:, 15 + fc:30 + fc], in1=INVC[:, pos, g, 0:15], op=ALU.mult),
                        reads=[curb, bINVC], writes=[bSMALL])
                    dve(I("tensor_tensor", out=DG[g % 2][:, kk, fc:fc + 15], in0=SMALL[:, 0:15], in1=HF[:, c, 15 + fc:30 + fc], op=ALU.subtract),
                        reads=[bSMALL, bHF], writes=[bDG[g % 2]])
                if samples:
                    cur, curb = HS[:, c, :], bHS
                    shift = 1
                    for lv in range(g + 1):
                        dst = SA if lv % 2 == 0 else SB_
                        lo = 2 * shift - 1
                        dve(I("tensor_tensor", out=dst[:, lo:64], in0=cur[:, lo:64], in1=cur[:, lo - shift:64 - shift], op=ALU.add),
                            reads=[curb], writes=[bHS])
                        cur = dst
                        shift *= 2
                    dve(I("scalar_tensor_tensor", out=DG[g % 2][:, kk, ntok:S], in0=cur.rearrange("p (b t) -> p b t", t=16)[:, :, 15], scalar=1.0 / w,
                                                                              in1=HF[:, c, 15 + ntok:15 + S], op0=ALU.mult, op1=ALU.subtract),
                        reads=[bHS, bHF], writes=[bDG[g % 2]])
            cnt = 0
            for co in range(2):
                cp = 2 * g + co
                yb = bR1 if cp < 4 else bR2
                for (a, n) in ntiles(c0, c1):
                    la = a - c0
                    bk = 4 + cnt % 2
                    cnt += 1
                    for k in range(2):
                        pe(I("matmul", out=PS[:, bk, :n], lhsT=PW[:, g, k, co * 128:(co + 1) * 128], rhs=DG[g % 2][:, k, la:la + n], start=(k == 0), stop=(k == 1)),
                           reads=[bPW, bDG[g % 2]], writes=[PB[bk]])
                    act(I("activation", out=yT(cp, la, n), in_=PS[:, bk, :n], func=AF.Copy, scale=vt(cp, 16 + j)), reads=[PB[bk], bVT], writes=[yb])
        postnorm_residual(G(li, 1), c0, c1, yT, [bR1, bR2])

    def h1_to_scratch(c0, ncols, dcol0):
        P.new_phase()
        hT = arv(R1, 18496, BF16, "p (c n) -> p c n", c=KC)
        b = P.buf("R1")
        prenorm(G(1, 0), c0, c0 + ncols, lambda c, la, n: hT[:, c, la:la + n], b)
        P.dma("sp", h1c_d[:, :, dcol0:dcol0 + ncols], hT[:, :, 0:ncols], reads=[b], writes=[bH1C])

    if 0 in layers:
        load_tiles(0, 128, 8)
        cast_group(0)
        pool_layer(0, 0, 128, 1152, "zero", [(0, 0)], False, True, False, False)
        anchor(0)
        ffn(0, 128, 1152)
        h1_to_scratch(128, 1024, 0)
        load_tiles(1024, 128, 8)
        pool_layer(0, 0, 128, 1152, "tail", [], False, True, False, False)
        ffn(0, 128, 1152)
        h1_to_scratch(128, 1024, 1024)
        for c in range(KC):
            act(I("activation", out=XT[:, c, 0:128], in_=XT[:, c, 1024:1152], func=AF.Copy), reads=[XB[8]], writes=[XB[0]])
        load_tiles(2048, 128, 8)
        pool_layer(0, 0, 128, 1152, "tail", [(0, 1)], False, True, False, False)
        anchor(1)
        ffn(0, 128, 1152)
        load_tiles(3072, 1152, 8)
        load_samples()
        pool_layer(0, 0, 1152, NE, "tail", [], False, False, True, True)
        ffn(0, 1152, NE)
    else:
        cast_group(0)
        load_tiles(1920, 0, 17)
        load_samples()

    def conv_layer(c0, c1, first):
        P.new_phase()
        S = c1 - c0
        samples = (c1 == NE)
        ntok = S - (4 if samples else 0)
        tiles = ntiles(c0, c1)
        hT = arv(R1, 18496, BF16, "p (c n) -> p c n", c=KC)
        UT = arv(R3, 8 * 1186 * 4, F32, "p (c n) -> p c n", c=KC)
        sT = arv(R3, 18496, BF16, "p (c n) -> p c n", c=KC)
        cT = yT_view()
        WA = [arv(RW + 4096 * i, 4096, BF16, "p (k n) -> p k n", k=KC) for i in range(3)]
        WO = [arv(RW + 12288 + 2048 * i, 2048, BF16, "p (k n) -> p k n", k=KC) for i in range(2)]
        US = arv(RW + 16384, 8 * 4 * 31 * 4, F32, "p (c b k) -> p c b k", c=KC, b=4)
        bR1, bR2, bUT = P.buf("R1"), P.buf("R2"), P.buf("UT")
        bWA = [P.buf(f"wa{i}") for i in range(3)]
        bWO = [P.buf(f"wo{i}") for i in range(2)]
        bUS = P.buf("US")
        prenorm(G(2, 0), c0, c1, lambda c, la, n: hT[:, c, la:la + n], bR1)
        if first:
            pool(I("memset", UT[:, :, 0:30], 0.0), writes=[bUT])
        else:
            pool(I("tensor_copy", out=UT[:, :, 0:30], in_=UT30[:]), reads=[bUT30], writes=[bUT])
        wsrc = wb_cin.rearrange("(k p) n -> p k n", p=128)
        cnt = 0
        for c in range(KC):
            s = c % 3
            P.dma("sp", WA[s][:, :, 0:128], wsrc[:, :, c * 128:(c + 1) * 128], reads=[bWCIN], writes=[bWA[s]])
            P.dma("sp", WA[s][:, :, 128:256], wsrc[:, :, D + c * 128:D + (c + 1) * 128], reads=[bWCIN], writes=[bWA[s]])
            for (a, n) in tiles:
                la = a - c0
                t = cnt % 2
                cnt += 1
                bg, bu = 2 * t, 2 * t + 1
                for k in range(KC):
                    pe(I("matmul", out=PS[:, bg, :n], lhsT=WA[s][:, k, 0:128], rhs=hT[:, k, la:la + n], start=(k == 0), stop=(k == KC - 1)), reads=[bWA[s], bR1], writes=[PB[bg]])
                for k in range(KC):
                    pe(I("matmul", out=PS[:, bu, :n], lhsT=WA[s][:, k, 128:256], rhs=hT[:, k, la:la + n], start=(k == 0), stop=(k == KC - 1)), reads=[bWA[s], bR1], writes=[PB[bu]])
                act(I("activation", out=SIG[:, t, :n], in_=PS[:, bu, :n], func=AF.Sigmoid, bias=vt(c, 19)), reads=[PB[bu], bVT], writes=[bSIG[t]])
                dve(I("scalar_tensor_tensor", out=UT[:, c, 30 + la:30 + la + n], in0=PS[:, bg, :n], scalar=vt(c, 18), in1=SIG[:, t, :n], op0=ALU.add, op1=ALU.mult),
                    reads=[PB[bg], bSIG[t], bVT], writes=[bUT])
        if first:
            pool(I("tensor_scalar", out=UT[:, :, 30:158], in0=UT[:, :, 30:158], scalar1=FLG[:, 0:1], scalar2=None, op0=ALU.mult), reads=[bUT, bFLG], writes=[bUT])
        if not samples:
            pool(I("tensor_copy", out=UT30[:], in_=UT[:, :, ntok:ntok + 30]), reads=[bUT], writes=[bUT30])
        else:
            to_tok(lambda c: UT[:, c, ntok:ntok + 30], 30, convp_d, reads=[bUT])
            from_tok(sconv_d, 120, lambda h: US[:, 4 * h:4 * h + 4, :, 0:30], writes=[bUS])
            dve(I("tensor_copy", out=US[:, :, :, 30], in_=UT[:, :, 30 + ntok:30 + S]), reads=[bUT], writes=[bUS])
            P.dma("sp", convs_d[:, 0:29, :], sconv_d.rearrange("(b t) d -> b t d", t=30)[:, 1:30, :], writes=[bOUT])
            to_tok(lambda c: UT[:, c, 30 + ntok:30 + S], 4, convs_d[:, 29, :], reads=[bUT])
            for c in range(KC):
                dve(I("tensor_tensor", out=US[:, c], in0=US[:, c], in1=VT[:, c, 24:55].unsqueeze(1).broadcast_to([128, 4, 31]), op=ALU.mult), reads=[bUS, bVT], writes=[bUS])
            dve(I("tensor_reduce", out=SMALL[:, 0:32], in_=US[:].rearrange("p c b k -> p (c b) k"), axis=mybir.AxisListType.X, op=ALU.add), reads=[bUS], writes=[bSMALL])
            for h in range(2):
                reg = arv(R1 if h == 0 else R2, 18496, F32, "p (c n) -> p c n", c=4)
                dve(I("tensor_tensor", out=reg[:, :, ntok:S], in0=SMALL[:, 16 * h:16 * h + 16].rearrange("p (c b) -> p c b", b=4),
                      in1=VT[:, 4 * h:4 * h + 4, 20:21].broadcast_to([128, 4, 4]), op=ALU.add), reads=[bSMALL, bVT], writes=[bR1 if h == 0 else bR2])
        UTB = [arv(RW, 2372, BF16), arv(RW + 20352, 2372, BF16)]
        DIAG = arv(RW + 2372, 7936, BF16, "p (k n) -> p k n", k=31)
        bUTB = [P.buf("utb0"), P.buf("utb1")]
        bDIAG = P.buf("diag")
        cnt = 0
        for c in range(KC):
            s = c % 2
            yb = bR1 if c < 4 else bR2
            act(I("activation", out=UTB[s][:, 0:30 + S], in_=UT[:, c, 0:30 + S], func=AF.Copy), reads=[bUT], writes=[bUTB[s]] + (bWA if s == 0 else []))
            dve(I("tensor_tensor", out=DIAG, in0=IDB[:].unsqueeze(1).broadcast_to([128, 31, 128]), in1=VT[:, c, 24:55].unsqueeze(2).broadcast_to([128, 31, 128]), op=ALU.mult),
                reads=[bIDB, bVT], writes=[bDIAG] + bWA)
            for (a, n) in tiles:
                if a >= SMP0:
                    continue
                la = a - c0
                bk = 4 + cnt % 2
                cnt += 1
                for k in range(31):
                    pe(I("matmul", out=PS[:, bk, :n], lhsT=DIAG[:, k, :], rhs=UTB[s][:, la + k:la + k + n], start=(k == 0), stop=(k == 30)), reads=[bDIAG, bUTB[s]], writes=[PB[bk]])
                act(I("activation", out=cT(c, la, n), in_=PS[:, bk, :n], func=AF.Identity, bias=vt(c, 20)), reads=[PB[bk], bVT], writes=[yb])
        for (a, n) in tiles:
            la = a - c0
            for c in range(KC):
                yb = bR1 if c < 4 else bR2
                pe(I("matmul", out=PS[:, 6, :n], lhsT=ONF[:], rhs=cT(c, la, n), start=(c == 0), stop=(c == KC - 1)), reads=[yb, bONF], writes=[PB[6]])
            for c in range(KC):
                yb = bR1 if c < 4 else bR2
                s = c % 2
                act(I("activation", out=STG[:, s, :n], in_=cT(c, la, n), func=AF.Square), reads=[yb], writes=[bSTG[s]])
                pe(I("matmul", out=PS[:, 7, :n], lhsT=ONF[:], rhs=STG[:, s, :n], start=(c == 0), stop=(c == KC - 1)), reads=[bSTG[s], bONF], writes=[PB[7]])
            MU = SIG[:, 0, :n]
            T2 = SIG[:, 1, :n]
            RS = RSTD[:, la:la + n]
            dve(I("tensor_scalar", out=MU, in0=PS[:, 6, :n], scalar1=1.0 / D, scalar2=None, op0=ALU.mult), reads=[PB[6]], writes=[bSIG[0]])
            dve(I("tensor_scalar", out=RS, in0=PS[:, 7, :n], scalar1=1.0 / D, scalar2=LN_EPS, op0=ALU.mult, op1=ALU.add), reads=[PB[7]], writes=[bRS])
            dve(I("tensor_tensor", out=T2, in0=MU, in1=MU, op=ALU.mult), reads=[bSIG[0]], writes=[bSIG[1]])
            dve(I("tensor_tensor", out=RS, in0=RS, in1=T2, op=ALU.subtract), reads=[bRS, bSIG[1]], writes=[bRS])
            act(I("activation", out=RS, in_=RS, func=AF.Sqrt), reads=[bRS], writes=[bRS])
            dve(I("reciprocal", out=RS, in_=RS), reads=[bRS], writes=[bRS])
            for c in range(KC):
                yb = bR1 if c < 4 else bR2
                dve(I("tensor_tensor", out=cT(c, la, n), in0=cT(c, la, n), in1=MU, op=ALU.subtract), reads=[yb, bSIG[0]], writes=[yb])
            for c in range(KC):
                yb = bR1 if c < 4 else bR2
                dve(I("tensor_tensor", out=cT(c, la, n), in0=cT(c, la, n), in1=RS, op=ALU.mult), reads=[yb, bRS], writes=[yb])
            for c in range(KC):
                yb = bR1 if c < 4 else bR2
                act(I("activation", out=sT[:, c, la:la + n], in_=cT(c, la, n), func=AF.Silu, scale=vt(c, 21), bias=vt(c, 22)), reads=[yb, bVT], writes=[bUT])
        osrc = wb_cout.rearrange("(k p) n -> p k n", p=128)
        cnt = 0
        for c in range(KC):
            s = c % 2
            P.dma("sp", WO[s], osrc[:, :, c * 128:(c + 1) * 128], reads=[bWCOUT], writes=[bWO[s]])
            yb = bR1 if c < 4 else bR2
            for (a, n) in tiles:
                la = a - c0
                bk = 4 + cnt % 2
                cnt += 1
                for k in range(KC):
                    pe(I("matmul", out=PS[:, bk, :n], lhsT=WO[s][:, k, :], rhs=sT[:, k, la:la + n], start=(k == 0), stop=(k == KC - 1)), reads=[bWO[s], bUT], writes=[PB[bk]])
                act(I("activation", out=cT(c, la, n), in_=PS[:, bk, :n], func=AF.Identity, bias=vt(c, 23)), reads=[PB[bk], bVT], writes=[yb])
        postnorm_residual(G(2, 1), c0, c1, cT, [bR1, bR2])

    def PSB(b):
        return PS[:, b, :].bitcast(BF16)

    def attn_layer():
        P.new_phase()
        H1E = arv(0, 34880, BF16, "p (c n) -> p c n", c=KC)
        H1C = arv(34880, 32768, BF16, "p (c n) -> p c n", c=KC)
        WQ = [arv(67648 + 16384 * i, 16384, BF16, "p (k n) -> p k n", k=KC) for i in range(2)]
        QS = arv(100416, 4096, BF16, "p (s n) -> p s n", s=2)
        VS = [arv(104512 + 2080 * i, 2080, BF16, "p (h d) -> p h d", h=16) for i in range(3)]
        bH1E, bH1Cs = P.buf("H1E"), P.buf("H1C")
        bWQ = [P.buf("wq0"), P.buf("wq1")]
        bQS = [P.buf("qs0"), P.buf("qs1")]
        bVS = [P.buf(f"vs{i}") for i in range(3)]
        prenorm(G(1, 0), 0, 1152, lambda c, la, n: H1E[:, c, la:la + n], bH1E)
        prenorm(G(1, 0), 1152, NE, lambda c, la, n: H1E[:, c, 1152 + la:1152 + la + n], bH1E)
        P.dma("sp", H1C, h1c_d, reads=[bH1C], writes=[bH1Cs])
        pool(I("tensor_copy", out=VS[0][:, :, 64], in_=FLG[:, 0:16]), reads=[bFLG], writes=[bVS[0]])
        for i in (1, 2):
            pool(I("memset", VS[i][:, :, 64:65], 1.0), writes=[bVS[i]])
        wsrc = wb_qkv.rearrange("(k p) n -> p k n", p=128)
        tl_ctx = [("c", t, H1C, t * 128, 128, t * 128) for t in range(15)]
        tl_e = [("e", t, H1E, t * 128, 128, 1920 + t * 128) for t in range(17)]
        tl_s = [("s", 0, H1E, SMP0, 4, 4096)]
        cntb = cq = cv = 0
        for cg in range(9):
            g, typ = cg // 3, cg % 3
            s = cg % 2
            P.dma("sp", WQ[s], wsrc[:, :, cg * 1024:(cg + 1) * 1024], reads=[bWQKV], writes=[bWQ[s]])
            tls = (tl_ctx if typ > 0 else []) + tl_e + tl_s
            for (kind, t, H, col, M, row0) in tls:
                hb = bH1Cs if kind == "c" else bH1E
                b0 = 2 * (cntb % 2)
                cntb += 1
                for hf in range(2):
                    for k in range(KC):
                        pe(I("matmul", out=PS[0:M, b0 + hf, :], lhsT=H[:, k, col:col + M], rhs=WQ[s][:, k, hf * 512:(hf + 1) * 512], start=(k == 0), stop=(k == KC - 1)),
                           reads=[hb, bWQ[s]], writes=[PB[b0 + hf]])
                if typ < 2:
                    q = cq % 2
                    cq += 1
                    act(I("activation", out=QS[0:M, q, 0:512], in_=PS[0:M, b0, :], func=AF.Copy), reads=[PB[b0]], writes=[bQS[q]])
                    dve(I("tensor_copy", out=QS[0:M, q, 512:1024], in_=PS[0:M, b0 + 1, :]), reads=[PB[b0 + 1]], writes=[bQS[q]])
                    o0 = g * 2048 + typ * 1024
                    P.dma("pool", qk_d[row0:row0 + M, o0:o0 + 1024], QS[0:M, q, :], reads=[bQS[q]], writes=[bQK])
                else:
                    isctx = kind == "c" or (kind == "e" and t == 0)
                    vi = 0 if isctx else 1 + cv % 2
                    cv += 1
                    act(I("activation", out=VS[vi][0:M, 0:8, 0:64], in_=PS[0:M, b0, :].rearrange("p (h d) -> p h d", h=8), func=AF.Copy), reads=[PB[b0]], writes=[bVS[vi]])
                    dve(I("tensor_copy", out=VS[vi][0:M, 8:16, 0:64], in_=PS[0:M, b0 + 1, :].rearrange("p (h d) -> p h d", h=8)), reads=[PB[b0 + 1]], writes=[bVS[vi]])
                    P.dma("pool", v_d[row0:row0 + M, g * 1040:(g + 1) * 1040], VS[vi][0:M].rearrange("p h d -> p (h d)"), reads=[bVS[vi]], writes=[bV])
                if typ > 0 and (kind == "s" or (kind == "e" and t >= 1)):
                    if kind == "s":
                        dst = sws_d[g][0:4, typ - 1, :]
                    else:
                        orow = (t - 1) * 128 - (2048 - WINS[g])
                        dst = None if orow < 0 else swp_d[g][orow:orow + 128, typ - 1, :]
                    if dst is not None:
                        f = stg_ctr[0] % 2
                        stg_ctr[0] += 1
                        act(I("activation", out=STG[0:M, f, 0:512], in_=PS[0:M, b0, :], func=AF.Copy), reads=[PB[b0]], writes=[bSTG[f]])
                        dve(I("tensor_copy", out=STG[0:M, f, 512:1024], in_=PS[0:M, b0 + 1, :]), reads=[PB[b0 + 1]], writes=[bSTG[f]])
                        P.dma("act", dst, STG[0:M, f, :], reads=[bSTG[f]], writes=[bOUT])

        if attn_stop == 'A':
            return
        P.new_phase()
        TB = [arv(8192 * g, 8192, BF16, "p (h t q) -> p h t q", h=16, t=2) for g in range(3)]
        TBf = [arv(8192 * g, 8192, BF16) for g in range(3)]
        HK = arv(24576, 16384, F32)
        VEC = arv(40960, 2048, F32)
        RB = arv(43008, 192, F32)
        OH = arv(43264, 2048, F32)
        NR = arv(45312, 2048, F32)
        bTB = [P.buf(f"tb{g}") for g in range(3)]
        bHK, bVECs, bRB, bOH, bNR = P.buf("HK"), P.buf("VEC"), P.buf("RB"), P.buf("OH"), P.buf("NR")
        P.dma("sp", RB[0:32, :], relb_d, writes=[bRB])
        for g in range(3):
            P.dma("sp", OH[0:32, :], oh_d[g], writes=[bOH])
            P.dma("sp", NR[0:1, :], negr_d[g], writes=[bNR])
            pe(I("matmul", out=PS[0:16, 0, :], lhsT=RB[0:32, g * 16:(g + 1) * 16], rhs=OH[0:32, :], start=True, stop=False), reads=[bRB, bOH], writes=[PB[0]])
            pe(I("matmul", out=PS[0:16, 0, :], lhsT=ONF[0:1, 0:16], rhs=NR[0:1, :], start=False, stop=True), reads=[bONF, bNR], writes=[PB[0]])
            act(I("activation", out=VEC[0:16, :], in_=PS[0:16, 0, :], func=AF.Copy), reads=[PB[0]], writes=[bVECs])
            P.dma("sp", vec_d[g], VEC[0:16, :], reads=[bVECs], writes=[bVEC])
            for h4 in range(4):
                hsrc = bass.AP(vec_d.tensor, g * 16 * 512 + h4 * 4 * 512, [[1, 128], [512, 4], [256, 2], [1, 128]])
                P.dma("sp", HK.rearrange("p (h t q) -> p h t q", h=16, t=2)[:, 4 * h4:4 * h4 + 4], hsrc, reads=[bVEC], writes=[bHK])
            for i in range(8):
                bk = 2 + i % 2
                pe(I("matmul", out=PS[:, bk, :], lhsT=JF[:], rhs=HK[:, i * 512:(i + 1) * 512], start=True, stop=True), reads=[bJF, bHK], writes=[PB[bk]])
                act(I("activation", out=TBf[g][:, i * 512:(i + 1) * 512], in_=PS[:, bk, :], func=AF.Exp), reads=[PB[bk]], writes=[bTB[g]])

        if attn_stop == 'B':
            return
        P.new_phase()
        QTOK = [arv(24576 + 2048 * i, 2048, BF16) for i in range(2)]
        KTOK = [arv(28672 + 2048 * i, 2048, BF16) for i in range(2)]
        QT = [arv(32768 + 2048 * i, 2048, BF16, "p (c n) -> p c n", c=KC) for i in range(2)]
        KT = [arv(36864 + 2048 * i, 2048, BF16, "p (c n) -> p c n", c=KC) for i in range(3)]
        VA = [arv(43008 + 2080 * i, 2080, BF16) for i in range(3)]
        PT = [arv(49248 + 1024 * i, 1024, BF16, "p (h t q) -> p h t q", h=2, t=2) for i in range(3)]
        UZS = [arv(52320 + 4160 * i, 4160, F32) for i in range(2)]
        bQTOK = [P.buf("qtok0"), P.buf("qtok1")]
        bKTOK = [P.buf("ktok0"), P.buf("ktok1")]
        bQT = [P.buf("qt0"), P.buf("qt1")]
        bKT = [P.buf(f"kt{i}") for i in range(3)]
        bVA = [P.buf(f"va{i}") for i in range(3)]
        bPT = [P.buf(f"pt{i}") for i in range(3)]
        bUZS = [P.buf("uzs0"), P.buf("uzs1")]
        blocks = []
        qi = 0
        for g, dil in enumerate(DILS):
            NB = 32 // dil
            QB0 = 16 // dil - 1
            for r in range(dil):
                first = True
                for kb in range(max(QB0 - 1, 0), NB):
                    isq = kb >= QB0
                    blocks.append(dict(g=g, dil=dil, r=r, kb=kb, isq=isq, first=first, qb=(qi % 2) if isq else None,
                                       qlo=((128 - 128 // dil) if kb == QB0 else 0)))
                    if isq:
                        qi += 1
                    first = False

        def issue_loads(i):
            B_ = blocks[i]
            g, dil, r, kb = B_["g"], B_["dil"], B_["r"], B_["kb"]
            sl, kti = i % 3, i % 2
            row0 = kb * 128 * dil + r
            P.dma("act", KTOK[kti], bass.AP(qk_d.tensor, row0 * QROW + g * 2048 + 1024, [[dil * QROW, 128], [1, 1024]]), reads=[bQK], writes=[bKTOK[kti]])
            P.dma("act", VA[sl], bass.AP(v_d.tensor, row0 * VROW + g * 1040, [[dil * VROW, 128], [1, 1040]]), reads=[bV], writes=[bVA[sl]])
            if B_["isq"]:
                qb, qlo = B_["qb"], B_["qlo"]
                P.dma("act", QTOK[qb][qlo:128, :], bass.AP(qk_d.tensor, (row0 + qlo * dil) * QROW + g * 2048, [[dil * QROW, 128 - qlo], [1, 1024]]), reads=[bQK], writes=[bQTOK[qb]])

        ui = 0
        issue_loads(0)
        for i, B_ in enumerate(blocks):
            g, dil, r, kb = B_["g"], B_["dil"], B_["r"], B_["kb"]
            if i + 1 < len(blocks):
                issue_loads(i + 1)
            sl, kti = i % 3, i % 2
            for c in range(KC):
                pe(I("transpose", out=PSB(0)[:, c * 128:(c + 1) * 128], in_=KTOK[kti][:, c * 128:(c + 1) * 128], identity=IDB[:]), reads=[bKTOK[kti], bIDB], writes=[PB[0]])
            dve(I("tensor_copy", out=KT[sl], in_=PSB(0).rearrange("p (c n) -> p c n", c=KC)), reads=[PB[0]], writes=[bKT[sl]])
            if not B_["isq"]:
                continue
            qlo, qb = B_["qlo"], B_["qb"]
            nq = 128 - qlo
            for c in range(KC):
                pe(I("transpose", out=PSB(1)[:, c * 128:(c + 1) * 128], in_=QTOK[qb][:, c * 128:(c + 1) * 128], identity=IDB[:]), reads=[bQTOK[qb], bIDB], writes=[PB[1]])
            act(I("activation", out=QT[qb], in_=PSB(1).rearrange("p (c n) -> p c n", c=KC), func=AF.Copy), reads=[PB[1]], writes=[bQT[qb]])
            has_prev = (kb >= 1) and not B_["first"]
            pc0 = 0 if has_prev else 1
            kts = ([(0, (i - 1) % 3)] if has_prev else []) + [(1, sl)]

            def heads_of(hp):
                base = 4 * (hp // 2) + (hp % 2)
                return (base, base + 2)

            def s_stage(hp):
                bank = 2 + hp % 3
                hA, hB = heads_of(hp)
                p0 = (hA % 2) * 64
                for hh, h in enumerate((hA, hB)):
                    for (pc, ks) in kts:
                        o0 = hh * 256 + pc * 128
                        pe(I("matmul", out=PS[:, bank, o0 + qlo:o0 + 128], lhsT=KT[ks][p0:p0 + 64, h // 2, :], rhs=QT[qb][p0:p0 + 64, h // 2, qlo:128], start=True, stop=True),
                           reads=[bKT[ks], bQT[qb]], writes=[PB[bank]])
                pt = hp % 3
                act(I("activation", out=PT[pt][:, :, pc0:2, qlo:128], in_=PS[:, bank, :].rearrange("p (h t q) -> p h t q", h=2, t=2)[:, :, pc0:2, qlo:128], func=AF.Exp, scale=0.125),
                    reads=[PB[bank]], writes=[bPT[pt]])
                dve(I("tensor_tensor", out=PT[pt][:, :, pc0:2, qlo:128], in0=PT[pt][:, :, pc0:2, qlo:128], in1=TB[g][:, hA:hB + 1:2, pc0:2, qlo:128], op=ALU.mult),
                    reads=[bPT[pt], bTB[g]], writes=[bPT[pt]])

            def pv_stage(hp):
                pt = hp % 3
                for hh, h in enumerate(heads_of(hp)):
                    bank = 5 + h // 7
                    off = (h % 7) * 65
                    for jx, (pc, ks) in enumerate(kts):
                        pe(I("matmul", out=PS[0:nq, bank, off:off + 65], lhsT=PT[pt][:, hh, pc, qlo:128], rhs=VA[ks][:, h * 65:(h + 1) * 65], start=(jx == 0), stop=(jx == len(kts) - 1)),
                           reads=[bPT[pt], bVA[ks]], writes=[PB[bank]])

            for ii in range(10):
                if ii < 8:
                    s_stage(ii)
                if ii >= 2:
                    pv_stage(ii - 2)
            u = ui % 2
            ui += 1
            act(I("activation", out=UZS[u][0:nq, 0:455], in_=PS[0:nq, 5, 0:455], func=AF.Copy), reads=[PB[5]], writes=[bUZS[u]])
            dve(I("tensor_copy", out=UZS[u][0:nq, 455:910], in_=PS[0:nq, 6, 0:455]), reads=[PB[6]], writes=[bUZS[u]])
            act(I("activation", out=UZS[u][0:nq, 910:1040], in_=PS[0:nq, 7, 0:130], func=AF.Copy), reads=[PB[7]], writes=[bUZS[u]])
            e0 = (kb * 128 + qlo) * dil + r - 1920
            P.dma("act", bass.AP(uz_d.tensor, (g * 2176 + e0) * UROW, [[dil * UROW, nq], [1, 1040]]), UZS[u][0:nq, :], reads=[bUZS[u]], writes=[bUZ])

        if attn_stop == 'C':
            return
        P.new_phase()
        SQK = arv(24576, 12288, BF16)
        SV = arv(36864, 6240, BF16)
        CK = [arv(43104 + 4096 * i, 4096, F32) for i in range(2)]
        CV = [arv(51296 + 4096 * i, 4096, F32) for i in range(2)]
        CVA = arv(59488, 2080, BF16, "p (h d) -> p h d", h=16)
        CVAf = arv(59488, 2080, BF16)
        PR = arv(61568, 4096, F32)
        LG = arv(65664, 64, F32)
        PEX = arv(65728, 64, F32)
        PM = arv(65792, 32, BF16)
        MSK = arv(65920, 4160, F32)
        UZA = arv(70080, 4160, F32)
        SEL = arv(74240, 1024, BF16, "p (b m) -> p b m", b=4)
        OHB = arv(75264, 64, F32, "p (b m) -> p b m", b=4)
        BD = arv(75328, 4160, F32, "p (h d) -> p h d", h=16)
        BDf = arv(75328, 4160, F32)
        RB0 = arv(79488, 192, F32)
        PRS = arv(79680, 4160, F32)
        LGS = arv(83840, 64, F32)
        PSS = arv(83904, 64, F32)
        bS = {n: P.buf(n) for n in "SQK SV CVA PR LG PEX PM MSK UZA SEL OHB BD RB0 PRS LGS PSS".split()}
        bCK = [P.buf("ck0"), P.buf("ck1")]
        bCV = [P.buf("cv0"), P.buf("cv1")]
        pool(I("memset", SEL[0:4], 1.0), writes=[bS["SEL"]])
        pool(I("affine_select", out=SEL[0:4], in_=SEL[0:4], pattern=[[-1, 4], [0, 128]], compare_op=ALU.is_equal, fill=0.0, base=0, channel_multiplier=1), reads=[bS["SEL"]], writes=[bS["SEL"]])
        pool(I("memset", OHB[0:16], 1.0), writes=[bS["OHB"]])
        pool(I("affine_select", out=OHB[0:16], in_=OHB[0:16], pattern=[[1, 4], [-1, 4]], compare_op=ALU.is_equal, fill=0.0, base=0, channel_multiplier=0), reads=[bS["OHB"]], writes=[bS["OHB"]])
        pool(I("memset", BD[0:16], 1.0), writes=[bS["BD"]])
        pool(I("affine_select", out=BD[0:16], in_=BD[0:16], pattern=[[-1, 16], [0, 65]], compare_op=ALU.is_equal, fill=0.0, base=0, channel_multiplier=1), reads=[bS["BD"]], writes=[bS["BD"]])
        pool(I("memset", CVA[:, :, 64:65], 1.0), writes=[bS["CVA"]])
        pool(I("memset", UZA[0:4, :], 0.0), writes=[bS["UZA"]])
        P.dma("sp", RB0[0:4, :], bass.AP(relb_d.tensor, 0, [[0, 4], [1, 48]]), writes=[bS["RB0"]])
        P.dma("sp", SQK[0:4, :], qk_d[4096:4100, :], reads=[bQK], writes=[bS["SQK"]])
        P.dma("sp", SV[0:4, :], v_d[4096:4100, :], reads=[bV], writes=[bS["SV"]])
        chunks = [(0, 455), (455, 455), (910, 130)]
        ci = 0
        for b in range(4):
            for g, dil in enumerate(DILS):
                cc = ci % 2
                ci += 1
                P.dma("sp", CK[cc], bass.AP(cache_d[g].tensor, b * WINS[g] * 2048, [[dil * 2048, 128], [1, 1024]]), writes=[bCK[cc]])
                P.dma("sp", CV[cc], bass.AP(cache_d[g].tensor, b * WINS[g] * 2048 + 1024, [[dil * 2048, 128], [1, 1024]]), writes=[bCV[cc]])
                for hf in range(2):
                    pe(I("matmul", out=PS[:, hf, :], lhsT=SEL[0:4, b, :], rhs=SQK[0:4, g * 2048 + hf * 512:g * 2048 + (hf + 1) * 512], start=True, stop=True), reads=[bS["SEL"], bS["SQK"]], writes=[PB[hf]])
                    dve(I("tensor_tensor", out=PR[:, hf * 512:(hf + 1) * 512], in0=CK[cc][:, hf * 512:(hf + 1) * 512], in1=PS[:, hf, :], op=ALU.mult), reads=[bCK[cc], PB[hf]], writes=[bS["PR"]])
                dve(I("tensor_reduce", out=LG[:, 0:16], in_=PR.rearrange("p (h d) -> p h d", h=16), axis=mybir.AxisListType.X, op=ALU.add), reads=[bS["PR"]], writes=[bS["LG"]])
                act(I("activation", out=PEX[:, 0:16], in_=LG[:, 0:16], func=AF.Exp, scale=0.125), reads=[bS["LG"]], writes=[bS["PEX"]])
                dve(I("tensor_tensor", out=PM[:, 0:16], in0=PEX[:, 0:16], in1=TB[g][:, :, 0, 0], op=ALU.mult), reads=[bS["PEX"], bTB[g]], writes=[bS["PM"]])
                dve(I("tensor_copy", out=CVA[:, :, 0:64], in_=CV[cc].rearrange("p (h d) -> p h d", h=16)), reads=[bCV[cc]], writes=[bS["CVA"]])
                for i3, (off, ncol) in enumerate(chunks):
                    pe(I("matmul", out=PS[0:16, 2 + i3, 0:ncol], lhsT=PM[:, 0:16], rhs=CVAf[:, off:off + ncol], start=True, stop=True), reads=[bS["PM"], bS["CVA"]], writes=[PB[2 + i3]])
                    dve(I("tensor_tensor", out=MSK[0:16, off:off + ncol], in0=PS[0:16, 2 + i3, 0:ncol], in1=BDf[0:16, off:off + ncol], op=ALU.mult), reads=[PB[2 + i3], bS["BD"]], writes=[bS["MSK"]])
                for i3, (off, ncol) in enumerate(chunks):
                    pe(I("matmul", out=PS[0:4, 5 + i3, 0:ncol], lhsT=OHB[0:16, b, :], rhs=MSK[0:16, off:off + ncol], start=True, stop=True), reads=[bS["OHB"], bS["MSK"]], writes=[PB[5 + i3]])
                    dve(I("tensor_tensor", out=UZA[0:4, off:off + ncol], in0=UZA[0:4, off:off + ncol], in1=PS[0:4, 5 + i3, 0:ncol], op=ALU.add), reads=[bS["UZA"], PB[5 + i3]], writes=[bS["UZA"]])
        for g in range(3):
            dve(I("tensor_tensor", out=PRS[0:4, 0:1024], in0=SQK[0:4, g * 2048:g * 2048 + 1024], in1=SQK[0:4, g * 2048 + 1024:g * 2048 + 2048], op=ALU.mult), reads=[bS["SQK"]], writes=[bS["PRS"]])
            dve(I("tensor_reduce", out=LGS[0:4, 0:16], in_=PRS[0:4, 0:1024].rearrange("p (h d) -> p h d", h=16), axis=mybir.AxisListType.X, op=ALU.add), reads=[bS["PRS"]], writes=[bS["LGS"]])
            dve(I("scalar_tensor_tensor", out=LGS[0:4, 0:16], in0=LGS[0:4, 0:16], scalar=0.125, in1=RB0[0:4, g * 16:(g + 1) * 16], op0=ALU.mult, op1=ALU.add), reads=[bS["LGS"], bS["RB0"]], writes=[bS["LGS"]])
            act(I("activation", out=PSS[0:4, 0:16], in_=LGS[0:4, 0:16], func=AF.Exp), reads=[bS["LGS"]], writes=[bS["PSS"]])
            prs3 = PRS[0:4, 0:1040].rearrange("p (h d) -> p h d", h=16)
            sv3 = SV[0:4, g * 1040:(g + 1) * 1040].rearrange("p (h d) -> p h d", h=16)
            dve(I("tensor_tensor", out=prs3[:, :, 0:64], in0=sv3[:, :, 0:64], in1=PSS[0:4, 0:16].unsqueeze(2).broadcast_to([4, 16, 64]), op=ALU.mult), reads=[bS["SV"], bS["PSS"], bS["PRS"]], writes=[bS["PRS"]])
            dve(I("tensor_copy", out=prs3[:, :, 64], in_=PSS[0:4, 0:16]), reads=[bS["PSS"]], writes=[bS["PRS"]])
            dve(I("tensor_tensor", out=UZA[0:4, :], in0=UZA[0:4, :], in1=PRS[0:4, 0:1040], op=ALU.add), reads=[bS["UZA"], bS["PRS"]], writes=[bS["UZA"]])

        if attn_stop == 'D1':
            return
        P.new_phase()
        ATT = arv(0, 34880, BF16, "p (c n) -> p c n", c=KC)
        UZ3 = [arv(74240 + 12480 * i, 12480, F32, "p (g n) -> p g n", g=3) for i in range(2)]
        ATK = [arv(99200 + 2048 * i, 2048, BF16) for i in range(2)]
        ZR = arv(103296, 64, F32)
        ATS = arv(103360, 2048, BF16)
        bATT, bZR, bATS = P.buf("ATT"), P.buf("ZR"), P.buf("ATS")
        bUZ3 = [P.buf("uz30"), P.buf("uz31")]
        bATK = [P.buf("atk0"), P.buf("atk1")]
        bUZA2 = P.buf("UZA2")
        uza3 = UZA[0:4, :].rearrange("p (h d) -> p h d", h=16)
        dve(I("tensor_scalar", out=ZR[0:4, 0:16], in0=uza3[:, :, 64], scalar1=1e-30, scalar2=None, op0=ALU.add), reads=[bS["UZA"]], writes=[bZR])
        dve(I("reciprocal", out=ZR[0:4, 0:16], in_=ZR[0:4, 0:16]), reads=[bZR], writes=[bZR])
        dve(I("tensor_tensor", out=ATS[0:4, 0:1024].rearrange("p (h d) -> p h d", h=16), in0=uza3[:, :, 0:64], in1=ZR[0:4, 0:16].unsqueeze(2).broadcast_to([4, 16, 64]), op=ALU.mult),
            reads=[bS["UZA"], bZR], writes=[bATS])
        for c in range(KC):
            pe(I("transpose", out=PSB(0)[:, c * 4:(c + 1) * 4], in_=ATS[0:4, c * 128:(c + 1) * 128], identity=IDB[0:4, 0:4]), reads=[bATS, bIDB], writes=[PB[0]])
        act(I("activation", out=ATT[:, :, SMP0:NE], in_=PSB(0)[:, 0:32].rearrange("p (c n) -> p c n", c=KC), func=AF.Copy), reads=[PB[0]], writes=[bATT])
        def merge_load(e):
            for gg in range(3):
                P.dma("act", UZ3[e % 2][:, gg, :], uz_d[gg, e * 128:(e + 1) * 128, :], reads=[bUZ], writes=[bUZ3[e % 2]])

        merge_load(0)
        for e in range(17):
            u3 = e % 2
            if e + 1 < 17:
                merge_load(e + 1)
            dve(I("tensor_tensor", out=UZ3[u3][:, 0, :], in0=UZ3[u3][:, 0, :], in1=UZ3[u3][:, 1, :], op=ALU.add), reads=[bUZ3[u3]], writes=[bUZ3[u3]])
            dve(I("tensor_tensor", out=UZ3[u3][:, 0, :], in0=UZ3[u3][:, 0, :], in1=UZ3[u3][:, 2, :], op=ALU.add), reads=[bUZ3[u3]], writes=[bUZ3[u3]])
            s3 = UZ3[u3][:, 0, :].rearrange("p (h d) -> p h d", h=16)
            dve(I("tensor_scalar", out=ZR[:, 0:16], in0=s3[:, :, 64], scalar1=1e-30, scalar2=None, op0=ALU.add), reads=[bUZ3[u3]], writes=[bZR])
            dve(I("reciprocal", out=ZR[:, 0:16], in_=ZR[:, 0:16]), reads=[bZR], writes=[bZR])
            dve(I("tensor_tensor", out=ATK[u3].rearrange("p (h d) -> p h d", h=16), in0=s3[:, :, 0:64], in1=ZR[:, 0:16].unsqueeze(2).broadcast_to([128, 16, 64]), op=ALU.mult),
                reads=[bUZ3[u3], bZR], writes=[bATK[u3]])
            for c in range(KC):
                pe(I("transpose", out=PSB(1)[:, c * 128:(c + 1) * 128], in_=ATK[u3][:, c * 128:(c + 1) * 128], identity=IDB[:]), reads=[bATK[u3], bIDB], writes=[PB[1]])
            act(I("activation", out=ATT[:, :, e * 128:(e + 1) * 128], in_=PSB(1).rearrange("p (c n) -> p c n", c=KC), func=AF.Copy), reads=[PB[1]], writes=[bATT])

        if attn_stop == 'D2':
            return
        P.new_phase()
        WO = [arv(107904 + 2048 * i, 2048, BF16, "p (k n) -> p k n", k=KC) for i in range(2)]
        ylo = arv(34880, 18496, F32, "p (c n) -> p c n", c=4)
        yhi = arv(53376, 18496, F32, "p (c n) -> p c n", c=4)
        bATT2, bYL, bYH = P.buf("ATT2"), P.buf("YL"), P.buf("YH")
        bWO = [P.buf("wo0"), P.buf("wo1")]
        osrc = wb_o.rearrange("(k p) n -> p k n", p=128)
        passes = [(0, 1152), (1152, NE)]

        def Y(c, col, n):
            p0 = 0 if col < 1152 else 1152
            if isinstance(c, str):
                return (ylo if c == "h0" else yhi)[:, :, col - p0:col - p0 + n]
            return (ylo if c < 4 else yhi)[:, c % 4, col - p0:col - p0 + n]

        for (c0, c1) in passes:
            cnt = 0
            for c in range(KC):
                s = c % 2
                P.dma("sp", WO[s], osrc[:, :, c * 128:(c + 1) * 128], reads=[bWO_], writes=[bWO[s]])
                yb = bYL if c < 4 else bYH
                for (a, n) in ntiles(c0, c1):
                    bk = 4 + cnt % 2
                    cnt += 1
                    for k in range(KC):
                        pe(I("matmul", out=PS[:, bk, :n], lhsT=WO[s][:, k, :], rhs=ATT[:, k, a:a + n], start=(k == 0), stop=(k == KC - 1)), reads=[bWO[s], bATT], writes=[PB[bk]])
                    act(I("activation", out=Y(c, a, n), in_=PS[:, bk, :n], func=AF.Copy), reads=[PB[bk]], writes=[yb])
            postnorm_residual(G(1, 1), c0, c1, lambda c, la, n, c0=c0: Y(c, c0 + la, n), [bYL, bYH])

    if 0 not in layers:
        anchor(0)
        anchor(1)
    if 1 in layers:
        attn_layer()
        anchor(2)
        ffn(1, 0, 1152)
        ffn(1, 1152, NE)
    if 2 in layers:
        conv_layer(0, 1152, True)
        ffn(2, 0, 1152)
        conv_layer(1152, NE, False)
        ffn(2, 1152, NE)
    if 3 in layers:
        pool_layer(3, 1, 0, 1152, "zero", [(128, 1)], True, True, False, False)
        ffn(3, 0, 1152)
        pool_layer(3, 1, 1152, NE, "tail", [], False, False, True, True)
        ffn(3, 1152, NE)

    P.new_phase()
    for t in range(16):
        col = 128 + t * 128
        to_tok(lambda c, col=col: XT[:, c, col:col + 128], 128, y_d[t * 128:(t + 1) * 128, :], reads=xtb(col, 128))
    to_tok(lambda c: XT[:, c, SMP0:NE], 4, ys_d, reads=[XB[17]])
    global LASTP
    LASTP = P
    P.finalize(st)
    st.close()
    return nc


def _in_maps(inp):
    oh, negr = static_tables()
    f32 = np.float32
    xp = np.asarray(inp["x_prompt"], f32)
    vecs = np.zeros((64, D), f32)
    vecs[0:16] = np.asarray(inp["norm_g"], f32).reshape(16, D)
    vecs[16:18] = np.asarray(inp["pool_scale"], f32)
    vecs[18:20] = np.asarray(inp["conv_b_in"], f32).reshape(2, D)
    vecs[20] = np.asarray(inp["conv_b_dw"], f32)[0]
    vecs[21] = np.asarray(inp["conv_ln_g"], f32)[0]
    vecs[22] = np.asarray(inp["conv_ln_b"], f32)[0]
    vecs[23] = np.asarray(inp["conv_b_out"], f32)[0]
    vecs[24:55] = np.asarray(inp["conv_w_dw"], f32)[0]
    shared = {
        "vecs": vecs,
        "w_ffn_in": np.ascontiguousarray(inp["w_ffn_in"], f32),
        "w_ffn_out": np.ascontiguousarray(inp["w_ffn_out"], f32),
        "pool_w": np.ascontiguousarray(inp["pool_w"], f32),
        "w_qkv": np.ascontiguousarray(np.asarray(inp["w_qkv"], f32)[0]),
        "w_o": np.ascontiguousarray(np.asarray(inp["w_o"], f32)[0]),
        "rel_bias": np.ascontiguousarray(np.asarray(inp["rel_bias"], f32).reshape(32, 48)),
        "conv_w_in": np.ascontiguousarray(np.asarray(inp["conv_w_in"], f32)[0]),
        "conv_w_out": np.ascontiguousarray(np.asarray(inp["conv_w_out"], f32)[0]),
        "oh": oh, "negr": negr,
    }
    maps = []
    for i in range(8):
        seq, half = i // 2, i % 2
        b0 = 4 * i
        if half == 0:
            xin = np.concatenate([np.zeros((2048, D), f32), xp[seq, 0:2048]], axis=0)
        else:
            xin = np.ascontiguousarray(xp[seq])
        invc = np.zeros((2, 4, 16), f32)
        for pos in range(2):
            start = (half == 1) if pos == 0 else (half == 0)
            for g in range(4):
                w = 2 << g
                for j in range(16):
                    invc[pos, g, j] = 1.0 / (min(j + 1, w) if start else w)
        m = dict(shared)
        m["xin"] = xin
        m["xsm"] = np.ascontiguousarray(np.asarray(inp["x_sample"], f32)[b0:b0 + 4, 0])
        m["flag"] = np.full((128, 16), float(half), f32)
        m["invc"] = np.ascontiguousarray(np.broadcast_to(invc.reshape(1, 128), (128, 128)))
        m["spool"] = np.ascontiguousarray(np.asarray(inp["state_pool"], f32)[:, b0:b0 + 4].reshape(2, 60, D))
        m["sconv"] = np.ascontiguousarray(np.asarray(inp["state_conv"], f32)[0, b0:b0 + 4].reshape(120, D))
        for g in range(3):
            m[f"cache{g}"] = np.ascontiguousarray(np.asarray(inp[f"cache_swa_g{g}"], f32)[0, b0:b0 + 4].reshape(4, WINS[g], 2, D))
        maps.append(m)
    return maps


def _assemble(res):
    f32 = np.float32
    y = np.zeros((4, 4096, D), f32)
    ys = np.zeros((32, 1, D), f32)
    poolp = np.zeros((2, 4, 15, D), f32)
    pools = np.zeros((2, 32, 15, D), f32)
    swp = [np.zeros((1, 4, WINS[g], 2, 16, 64), f32) for g in range(3)]
    sws = [np.zeros((1, 32, 1, 2, 16, 64), f32) for g in range(3)]
    convp = np.zeros((1, 4, 30, D), f32)
    convs = np.zeros((1, 32, 30, D), f32)
    for i in range(8):
        r = res[i]
        seq, half = i // 2, i % 2
        b0 = 4 * i
        y[seq, half * 2048:(half + 1) * 2048] = r["y"]
        ys[b0:b0 + 4, 0] = r["ys"]
        pools[:, b0:b0 + 4] = r["pools"]
        convs[0, b0:b0 + 4] = r["convs"]
        for g in range(3):
            sws[g][0, b0:b0 + 4, 0] = r[f"sws{g}"].reshape(4, 2, 16, 64)
        if half == 1:
            poolp[:, seq] = r["poolp"]
            convp[0, seq] = r["convp"]
            for g in range(3):
                swp[g][0, seq] = r[f"swp{g}"].reshape(WINS[g], 2, 16, 64)
    return (y, ys, poolp, pools, swp[0], swp[1], swp[2], sws[0], sws[1], sws[2], convp, convs)


def kernel(**inputs):
    nc = build()
    res = run_bass_kernel_spmd(nc, _in_maps(inputs), core_ids=list(range(8)))
    return _assemble(res.results)
```

```python
import contextlib
import numpy as np
import ml_dtypes
import concourse.bass as bass
import concourse.mybir as mybir
from concourse.bass_utils import run_bass_kernel_spmd

F32 = mybir.dt.float32
BF16 = mybir.dt.bfloat16
AF = mybir.ActivationFunctionType
ALU = mybir.AluOpType
ENGS = ["pe", "act", "dve", "pool", "sp"]

D = 1024
KC = 8
DFF = 2816
FC = 22
NE = 2180
SMP0 = 2176
SMAX = 1156
RMS_EPS = 1e-6
LN_EPS = 1e-5
DILS = (1, 4, 16)
WINS = (128, 512, 2048)
NEG = -30000.0
QROW = 6144
VROW = 3 * 16 * 65
UROW = 16 * 65


class Buf:
    __slots__ = ("name", "lw", "rd", "multi")

    def __init__(self, name, alias=(), multi=False):
        self.name = name
        self.lw = [] if multi else None
        self.rd = list(alias)
        self.multi = multi


class Op:
    __slots__ = ("eng", "fn", "waits", "signal", "cnt", "isdma", "slot", "sem", "semval", "prevval", "seq")


class Prog:
    def __init__(self, nc, ksem=8):
        self.nc = nc
        self.ops = {e: [] for e in ENGS}
        self.ksem = ksem
        self.ndma = {e: 0 for e in ENGS}
        self.seq = 0
        self.frontier = []

    def new_phase(self):
        fr = []
        for e in ENGS:
            last = None
            lastd = {}
            for o in self.ops[e]:
                if o.isdma:
                    lastd[o.slot] = o
                else:
                    last = o
            if last is not None:
                fr.append(last)
            fr.extend(lastd.values())
        self.frontier = fr

    def buf(self, name, multi=False):
        return Buf(name, alias=self.frontier, multi=multi)

    def op(self, eng, fn, reads=(), writes=(), dma=False):
        o = Op()
        o.eng, o.fn, o.isdma, o.signal, o.waits, o.cnt = eng, fn, dma, dma, [], 0
        o.seq = self.seq
        self.seq += 1
        if dma:
            o.slot = self.ndma[eng] % self.ksem
            o.semval = 16 * (self.ndma[eng] // self.ksem + 1)
            o.prevval = o.semval - 16
            self.ndma[eng] += 1
        hard, war = set(), set()
        for b in reads:
            if b.multi:
                hard.update(b.lw)
            elif b.lw is not None:
                hard.add(b.lw)
        for b in writes:
            if b.multi:
                war.update(b.rd)
            else:
                if b.lw is not None:
                    hard.add(b.lw)
                war.update(b.rd)
        for d in hard | war:
            if d is o:
                continue
            if (not d.isdma) and (not dma) and d.eng == eng and eng == "pe":
                continue
            d.signal = True
            o.waits.append(d)
        for b in reads:
            if not dma:
                b.rd = [r for r in b.rd if r.isdma or r.eng != eng]
            b.rd.append(o)
        for b in writes:
            if b.multi:
                b.lw.append(o)
            else:
                b.lw = o
                b.rd = []
        self.ops[eng].append(o)
        return o

    def dma(self, eng, out, in_, reads=(), writes=(), **kw):
        return self.op(eng, lambda e: e.dma_start(out=out, in_=in_, **kw), reads, writes, dma=True)

    def finalize(self, st):
        nc = self.nc
        esem = {e: st.enter_context(nc.semaphore("es_" + e)) for e in ENGS}
        dsem = {e: [st.enter_context(nc.semaphore(f"ds_{e}{i}")) for i in range(self.ksem)] for e in ENGS}
        finals = []
        for e in ENGS:
            c = 0
            lastd = {}
            for o in self.ops[e]:
                if o.isdma:
                    o.sem = dsem[e][o.slot]
                    lastd[o.slot] = (o.sem, o.semval)
                elif o.signal:
                    c += 1
                    o.cnt = c
            finals.append((esem[e], c))
            finals.extend(lastd.values())
        block = st.enter_context(nc.Block())

        def run(eo, e):
            seen = {}

            def need(sem, val):
                if val > 0 and seen.get(id(sem), 0) < val:
                    eo.wait_ge(sem, val)
                    seen[id(sem)] = val

            for o in self.ops[e]:
                if o.isdma:
                    need(o.sem, o.prevval)
                for d in sorted(o.waits, key=lambda d: d.seq):
                    if d.isdma:
                        need(d.sem, d.semval)
                    else:
                        need(esem[d.eng], d.cnt)
                inst = o.fn(eo)
                if o.isdma:
                    inst.then_inc(o.sem, 16)
                elif o.signal:
                    inst.then_inc(esem[e], 1)
            if e == "sp":
                for sem, val in finals:
                    need(sem, val)

        @block.tensor
        def _(eo):
            run(eo, "pe")

        @block.scalar
        def _(eo):
            run(eo, "act")

        @block.vector
        def _(eo):
            run(eo, "dve")

        @block.gpsimd
        def _(eo):
            run(eo, "pool")

        @block.sync
        def _(eo):
            run(eo, "sp")


def I(name, *a, **kw):
    return lambda e: getattr(e, name)(*a, **kw)


def ntiles(c0, c1):
    out = []
    e = min(c1, SMP0)
    a = c0
    while a < e:
        n = min(512, e - a)
        out.append((a, n))
        a += n
    if c1 > SMP0:
        out.append((SMP0, c1 - SMP0))
    return out


def t5_bucket(dist):
    import math
    max_exact = 16
    d = np.maximum(np.asarray(dist), 0)
    large = max_exact + (np.log(np.maximum(d, max_exact) / max_exact) / math.log(2048 / max_exact) * (32 - max_exact)).astype(np.int64)
    large = np.minimum(large, 31)
    return np.where(d < max_exact, d, large).astype(np.int32)


def static_tables():
    oh = np.zeros((3, 32, 512), np.float32)
    neg = np.zeros((3, 1, 512), np.float32)
    for g, dil in enumerate(DILS):
        for j in range(256):
            if j <= 127:
                oh[g, t5_bucket((1 + j) * dil), j] = 1.0
            else:
                neg[g, 0, j] = NEG
            if 127 <= j <= 254:
                oh[g, t5_bucket((j - 127) * dil), 256 + j] = 1.0
            else:
                neg[g, 0, 256 + j] = NEG
    return oh, neg


def build(layers=(0, 1, 2, 3), skip_ffn=False, attn_stop=None):
    nc = bass.Bass("TRN2", target_bir_lowering=False)
    P = Prog(nc)
    st = contextlib.ExitStack()

    def din(name, shape, dt=F32):
        return nc.dram_tensor(name, list(shape), dt, kind="ExternalInput").ap()

    def dout(name, shape, dt=F32):
        return nc.dram_tensor(name, list(shape), dt, kind="ExternalOutput").ap()

    def dscr(name, shape, dt):
        return nc.dram_tensor(name, list(shape), dt, kind="Internal").ap()

    xin = din("xin", [4096, D])
    xsm = din("xsm", [4, D])
    flag_d = din("flag", [128, 16])
    invc_d = din("invc", [128, 2 * 4 * 16])
    spool_d = din("spool", [2, 60, D])
    sconv_d = din("sconv", [120, D])
    cache_d = [din(f"cache{g}", [4, WINS[g], 2, D]) for g in range(3)]
    vecs_d = din("vecs", [64, D])
    w_in_d = din("w_ffn_in", [4, D, 2 * DFF])
    w_out_d = din("w_ffn_out", [4, DFF, D])
    pool_w_d = din("pool_w", [2, 4, 256, 256])
    w_qkv_d = din("w_qkv", [D, 9216])
    w_o_d = din("w_o", [D, D])
    relb_d = din("rel_bias", [32, 48])
    cw_in_d = din("conv_w_in", [D, 2 * D])
    cw_out_d = din("conv_w_out", [D, D])
    oh_d = din("oh", [3, 32, 512])
    negr_d = din("negr", [3, 1, 512])

    y_d = dout("y", [2048, D])
    ys_d = dout("ys", [4, D])
    poolp_d = dout("poolp", [2, 15, D])
    pools_d = dout("pools", [2, 4, 15, D])
    swp_d = [dout(f"swp{g}", [WINS[g], 2, D]) for g in range(3)]
    sws_d = [dout(f"sws{g}", [4, 2, D]) for g in range(3)]
    convp_d = dout("convp", [30, D])
    convs_d = dout("convs", [4, 30, D])

    h1c_d = dscr("h1c", [128, KC, 2048], BF16)
    qk_d = dscr("qkd", [4100, QROW], BF16)
    v_d = dscr("vd", [4100, VROW], BF16)
    uz_d = dscr("uzd", [3, 2176, UROW], F32)
    vec_d = dscr("vecd", [3, 16, 512], F32)

    wb_in = dscr("wb_in", [4, D, 2 * DFF], BF16)
    wb_out = dscr("wb_out", [4, DFF, D], BF16)
    wb_pool = dscr("wb_pool", [2, 4, 256, 256], BF16)
    wb_qkv = dscr("wb_qkv", [D, 9216], BF16)
    wb_o = dscr("wb_o", [D, D], BF16)
    wb_cin = dscr("wb_cin", [D, 2 * D], BF16)
    wb_cout = dscr("wb_cout", [D, D], BF16)
    bWIN = [Buf(f"wbin{i}") for i in range(4)]
    bWOUT = [Buf(f"wbout{i}") for i in range(4)]
    bWPOOL = [Buf("wbpool0"), Buf("wbpool1")]
    bWQKV, bWO_, bWCIN, bWCOUT = Buf("wbqkv"), Buf("wbo"), Buf("wbcin"), Buf("wbcout")

    def cast2d(dst, src, buf):
        P.dma("pool", dst.rearrange("(p a) n -> p (a n)", p=128), src.rearrange("(p a) n -> p (a n)", p=128), writes=[buf])

    bANCH = [Buf(f"anchor{i}") for i in range(3)]

    def cast_group(i):
        rd = [] if i == 0 else [bANCH[i - 1]]
        def c2(dst, src, buf):
            P.dma("pool", dst.rearrange("(p a) n -> p (a n)", p=128), src.rearrange("(p a) n -> p (a n)", p=128), reads=rd, writes=[buf])
        if i == 0:
            c2(wb_pool[0].rearrange("g k n -> (g k) n"), pool_w_d[0].rearrange("g k n -> (g k) n"), bWPOOL[0])
            c2(wb_in[0], w_in_d[0], bWIN[0])
            c2(wb_out[0], w_out_d[0], bWOUT[0])
        elif i == 1:
            c2(wb_qkv, w_qkv_d, bWQKV)
            c2(wb_o, w_o_d, bWO_)
            c2(wb_in[1], w_in_d[1], bWIN[1])
            c2(wb_out[1], w_out_d[1], bWOUT[1])
        elif i == 2:
            c2(wb_cin, cw_in_d, bWCIN)
            c2(wb_cout, cw_out_d, bWCOUT)
            c2(wb_in[2], w_in_d[2], bWIN[2])
            c2(wb_out[2], w_out_d[2], bWOUT[2])
        else:
            c2(wb_pool[1].rearrange("g k n -> (g k) n"), pool_w_d[1].rearrange("g k n -> (g k) n"), bWPOOL[1])
            c2(wb_in[3], w_in_d[3], bWIN[3])
            c2(wb_out[3], w_out_d[3], bWOUT[3])

    def anchor(i):
        dve(I("tensor_copy", out=SMALL[:, 200:201], in_=SMALL[:, 201:202]), reads=[bRS], writes=[bANCH[i]])
        cast_group(i + 1)

    def sb(name, shape, dt):
        return st.enter_context(nc.sbuf_tensor(name, list(shape), dt))

    XT = sb("XT", [128, KC, NE], F32)
    AR = sb("AR", [128, 28000], F32)
    VT = sb("VT", [128, KC, 64], F32)
    IDF = sb("IDF", [128, 128], F32)
    IDB = sb("IDB", [128, 128], BF16)
    JF = sb("JF", [128, 128], F32)
    ONB = sb("ONB", [128, 128], BF16)
    ONF = sb("ONF", [128, 128], F32)
    FLG = sb("FLG", [128, 16], F32)
    INVC = sb("INVC", [128, 2, 4, 16], F32)
    SQS = sb("SQS", [128, 4096], BF16)
    SQ8 = SQS[:].rearrange("p (c n) -> p c n", c=KC)
    SIG = SQS[:, 0:2048].bitcast(F32).rearrange("p (s n) -> p s n", s=2)
    RSTD = sb("RSTD", [128, SMAX], F32)
    STG = sb("STG", [128, 2, D], F32)
    HT15 = sb("HT15", [128, KC, 15], F32)
    UT30 = sb("UT30", [128, KC, 30], F32)
    SMALL = sb("SMALL", [128, 256], F32)
    PS = st.enter_context(nc.psum_tensor("PS", [128, 8, 512], F32))

    PB = [Buf(f"ps{b}") for b in range(8)]
    XB = [Buf(f"xt{t}") for t in range(18)]
    bVT, bIDF, bIDB, bJF, bONB, bONF, bNH, bFLG, bINVC = (Buf(n) for n in "VT IDF IDB JF ONB ONF NH FLG INVC".split())
    bSQ2 = Buf("sq2")
    bSIG = [Buf("sig0"), Buf("sig1")]
    bSTG = [Buf("stg0"), Buf("stg1")]
    bHT15, bUT30 = Buf("ht15"), Buf("ut30")
    bRS = Buf("rstd")
    bSMALL = Buf("small")
    bOUT = Buf("out", multi=True)
    bH1C = Buf("h1c", multi=True)
    bQK = Buf("qkd", multi=True)
    bV = Buf("vd", multi=True)
    bUZ = Buf("uzd", multi=True)
    bVEC = Buf("vecd", multi=True)

    def xtb(a, n):
        out = []
        for t in range(17):
            if a < (t + 1) * 128 and a + n > t * 128:
                out.append(XB[t])
        if a + n > SMP0:
            out.append(XB[17])
        return out

    def act(fn, reads=(), writes=()):
        return P.op("act", fn, reads, writes)

    def dve(fn, reads=(), writes=()):
        return P.op("dve", fn, reads, writes)

    def pool(fn, reads=(), writes=()):
        return P.op("pool", fn, reads, writes)

    def pe(fn, reads=(), writes=()):
        return P.op("pe", fn, reads, writes)

    def arv(off, nbytes, dt, pat=None, **kw):
        assert off % 4 == 0 and nbytes % 4 == 0 and off + nbytes <= 28000 * 4, (off, nbytes)
        v = AR[:, off // 4:(off + nbytes) // 4]
        if dt == BF16:
            v = v.bitcast(BF16)
        if pat:
            v = v.rearrange(pat, **kw)
        return v

    stg_ctr = [0]
    bank2_ctr = [0]

    pool(I("memset", IDF[:], 1.0), writes=[bIDF])
    pool(I("affine_select", out=IDF[:], in_=IDF[:], pattern=[[-1, 128]], compare_op=ALU.is_equal, fill=0.0, base=0, channel_multiplier=1), reads=[bIDF], writes=[bIDF])
    pool(I("tensor_copy", out=IDB[:], in_=IDF[:]), reads=[bIDF], writes=[bIDB])
    pool(I("memset", JF[:], 1.0), writes=[bJF])
    pool(I("affine_select", out=JF[:], in_=JF[:], pattern=[[1, 128]], compare_op=ALU.is_equal, fill=0.0, base=-127, channel_multiplier=1), reads=[bJF], writes=[bJF])
    pool(I("memset", ONB[:], 1.0), writes=[bONB])
    pool(I("memset", ONF[:], 1.0), writes=[bONF])
    pool(I("memset", SMALL[:], 0.0), writes=[bSMALL])
    P.dma("sp", FLG[:], flag_d, writes=[bFLG])
    P.dma("sp", INVC[:].rearrange("p a g j -> p (a g j)"), invc_d, writes=[bINVC])

    def from_tok(src_ap, m, dst_half, reads_extra=(), writes=()):
        s = stg_ctr[0] % 2
        stg_ctr[0] += 1
        b0 = 2 * (bank2_ctr[0] % 2)
        bank2_ctr[0] += 1
        P.dma("pool", STG[0:m, s, :], src_ap, reads=list(reads_extra), writes=[bSTG[s]])
        for c in range(KC):
            pe(I("transpose", out=PS[:, b0 + c // 4, (c % 4) * m:(c % 4 + 1) * m], in_=STG[0:m, s, c * 128:(c + 1) * 128], identity=IDF[0:m, 0:m]),
               reads=[bSTG[s], bIDF], writes=[PB[b0 + c // 4]])
        act(I("activation", out=dst_half(0), in_=PS[:, b0, 0:4 * m].rearrange("p (c n) -> p c n", c=4), func=AF.Copy), reads=[PB[b0]], writes=writes)
        dve(I("tensor_copy", out=dst_half(1), in_=PS[:, b0 + 1, 0:4 * m].rearrange("p (c n) -> p c n", c=4)), reads=[PB[b0 + 1]], writes=writes)

    def to_tok(src_chunk, m, dst_ap, reads=()):
        s = stg_ctr[0] % 2
        stg_ctr[0] += 1
        b0 = 2 * (bank2_ctr[0] % 2)
        bank2_ctr[0] += 1
        for c in range(KC):
            pe(I("transpose", out=PS[0:m, b0 + c // 4, (c % 4) * 128:(c % 4 + 1) * 128], in_=src_chunk(c), identity=IDF[:]),
               reads=list(reads) + [bIDF], writes=[PB[b0 + c // 4]])
        act(I("activation", out=STG[0:m, s, 0:512], in_=PS[0:m, b0, :], func=AF.Copy), reads=[PB[b0]], writes=[bSTG[s]])
        dve(I("tensor_copy", out=STG[0:m, s, 512:1024], in_=PS[0:m, b0 + 1, :]), reads=[PB[b0 + 1]], writes=[bSTG[s]])
        P.dma("act", dst_ap, STG[0:m, s, :], reads=[bSTG[s]], writes=[bOUT])

    from_tok(vecs_d, 64, lambda h: VT[:, 4 * h:4 * h + 4, :], writes=[bVT])

    def vt(c, idx):
        return VT[:, c, idx:idx + 1]

    def G(li, k):
        return li * 4 + k

    def stats(src_half, n, rs_ap, src_bufs, rsb, eps=RMS_EPS, bank=6):
        sqb = [bSIG[0], bSIG[1], bSQ2]
        for h in range(2):
            act(I("activation", out=SQ8[:, 4 * h:4 * h + 4, :n], in_=src_half(h), func=AF.Square), reads=src_bufs, writes=sqb)
        for c in range(KC):
            pe(I("matmul", out=PS[:, bank, :n], lhsT=ONB[:], rhs=SQ8[:, c, :n], start=(c == 0), stop=(c == KC - 1)), reads=sqb + [bONB], writes=[PB[bank]])
        dve(I("tensor_scalar", out=rs_ap, in0=PS[:, bank, :n], scalar1=1.0 / D, scalar2=eps, op0=ALU.mult, op1=ALU.add), reads=[PB[bank]], writes=[rsb])
        act(I("activation", out=rs_ap, in_=rs_ap, func=AF.Sqrt), reads=[rsb], writes=[rsb])
        dve(I("reciprocal", out=rs_ap, in_=rs_ap), reads=[rsb], writes=[rsb])

    bRS3 = [Buf(f"rstd{i}") for i in range(4)]

    def prenorm(gi, c0, c1, dst, dst_buf_, pipelined=True):
        tl = ntiles(c0, c1)
        if pipelined:
            for ti_, (a, n) in enumerate(tl):
                la = a - c0
                stats(lambda h: XT[:, 4 * h:4 * h + 4, a:a + n], n, RSTD[:, la:la + n], xtb(a, n), bRS3[ti_])
        for ti_, (a, n) in enumerate(tl):
            la = a - c0
            if not pipelined:
                stats(lambda h: XT[:, 4 * h:4 * h + 4, a:a + n], n, RSTD[:, la:la + n], xtb(a, n), bRS3[ti_])
            xb = xtb(a, n)
            dst_buf = dst_buf_[ti_] if isinstance(dst_buf_, list) else dst_buf_
            for c in range(KC):
                dve(I("scalar_tensor_tensor", out=dst(c, la, n), in0=XT[:, c, a:a + n], scalar=vt(c, gi), in1=RSTD[:, la:la + n], op0=ALU.mult, op1=ALU.mult),
                    reads=xb + [bVT, bRS3[ti_]], writes=[dst_buf])

    def postnorm_residual(gi, c0, c1, yT, ybufs):
        tl = ntiles(c0, c1)
        for ti_, (a, n) in enumerate(tl):
            la = a - c0
            stats(lambda h: yT("h%d" % h, la, n), n, RSTD[:, la:la + n], ybufs, bRS3[ti_])
        for ti_, (a, n) in enumerate(tl):
            la = a - c0
            xb = xtb(a, n)
            for c in range(KC):
                dve(I("tensor_tensor", out=yT(c, la, n), in0=yT(c, la, n), in1=RSTD[:, la:la + n], op=ALU.mult), reads=ybufs + [bRS3[ti_]], writes=ybufs)
            for c in range(KC):
                dve(I("scalar_tensor_tensor", out=XT[:, c, a:a + n], in0=yT(c, la, n), scalar=vt(c, gi), in1=XT[:, c, a:a + n], op0=ALU.mult, op1=ALU.add),
                    reads=ybufs + xb + [bVT], writes=xb)

    R1, R2, R3, RW = 0, 18496, 36992, 36992 + 50864
    assert RW + 23552 <= 112000

    def yT_view():
        lo = arv(R1, 18496, F32, "p (c n) -> p c n", c=4)
        hi = arv(R2, 18496, F32, "p (c n) -> p c n", c=4)
        def f(c, la, n):
            if isinstance(c, str):
                return (lo if c == "h0" else hi)[:, :, la:la + n]
            return (lo if c < 4 else hi)[:, c % 4, la:la + n]
        return f

    def ffn(li, c0, c1):
        if skip_ffn:
            return
        P.new_phase()
        bR1, bR2, bA = P.buf("R1"), P.buf("R2"), P.buf("actT")
        bWA = [P.buf(f"wa{i}") for i in range(2)]
        hT = arv(R1, 18496, BF16, "p (c n) -> p c n", c=KC)
        aT = arv(R3, 50864, BF16, "p (c n) -> p c n", c=FC)
        WA = [arv(RW + 8192 * i, 8192, BF16, "p (k t n) -> p k t n", k=KC, t=2) for i in range(2)]
        WB = [arv(RW + 16384, 5632, BF16, "p (j n) -> p j n", j=FC),
              STG[:].rearrange("p s n -> p (s n)")[:, 0:1408].bitcast(BF16).rearrange("p (j n) -> p j n", j=FC)]
        bWB = [[P.buf("wb0")], bSTG]
        yT = yT_view()
        tiles = ntiles(c0, c1)
        bH = [P.buf(f"hT{i}") for i in range(len(tiles))]
        prenorm(G(li, 2), c0, c1, lambda c, la, n: hT[:, c, la:la + n], bH, pipelined=False)
        wsrc = wb_in[li].rearrange("(k p) n -> p k n", p=128)
        cnt = 0
        for j in range(FC):
            s = (j // 2) % 2
            jj = j % 2
            if jj == 0:
                jp = j // 2
                P.dma("sp", WA[s][:, :, 0, :], wsrc[:, :, jp * 256:(jp + 1) * 256], reads=[bWIN[li]], writes=[bWA[s]])
                P.dma("sp", WA[s][:, :, 1, :], wsrc[:, :, DFF + jp * 256:DFF + (jp + 1) * 256], reads=[bWIN[li]], writes=[bWA[s]])
            for ti_, (a, n) in enumerate(tiles):
                la = a - c0
                t = cnt % 2
                cnt += 1
                bg, bu = 2 * t, 2 * t + 1
                for k in range(KC):
                    pe(I("matmul", out=PS[:, bg, :n], lhsT=WA[s][:, k, 0, jj * 128:(jj + 1) * 128], rhs=hT[:, k, la:la + n], start=(k == 0), stop=(k == KC - 1)),
                       reads=[bWA[s], bH[ti_]], writes=[PB[bg]])
                for k in range(KC):
                    pe(I("matmul", out=PS[:, bu, :n], lhsT=WA[s][:, k, 1, jj * 128:(jj + 1) * 128], rhs=hT[:, k, la:la + n], start=(k == 0), stop=(k == KC - 1)),
                       reads=[bWA[s], bH[ti_]], writes=[PB[bu]])
                act(I("activation", out=SIG[:, t, :n], in_=PS[:, bg, :n], func=AF.Silu), reads=[PB[bg]], writes=[bSIG[t]])
                dve(I("tensor_tensor", out=aT[:, j, la:la + n], in0=SIG[:, t, :n], in1=PS[:, bu, :n], op=ALU.mult), reads=[bSIG[t], PB[bu]], writes=[bA])
        osrc = wb_out[li].rearrange("(j p) n -> p j n", p=128)
        cnt = 0
        for c in range(KC):
            s = c % 2
            P.dma("sp", WB[s][:, 0:11, :], osrc[:, 0:11, c * 128:(c + 1) * 128], reads=[bWOUT[li]], writes=bWB[s])
            P.dma("sp", WB[s][:, 11:22, :], osrc[:, 11:22, c * 128:(c + 1) * 128], reads=[bWOUT[li]], writes=bWB[s])
            yb = bR1 if c < 4 else bR2
            for (a, n) in tiles:
                la = a - c0
                bk = (4, 5, 7)[cnt % 3]
                cnt += 1
                for j in range(FC):
                    pe(I("matmul", out=PS[:, bk, :n], lhsT=WB[s][:, j, :], rhs=aT[:, j, la:la + n], start=(j == 0), stop=(j == FC - 1)),
                       reads=bWB[s] + [bA], writes=[PB[bk]])
                act(I("activation", out=yT(c, la, n), in_=PS[:, bk, :n], func=AF.Copy), reads=[PB[bk]], writes=[yb] + (bH if c < 4 else []))
        postnorm_residual(G(li, 3), c0, c1, yT, [bR1, bR2])

    def load_tiles(row0, col0, nt):
        for t in range(nt):
            col = col0 + t * 128
            from_tok(xin[row0 + t * 128:row0 + (t + 1) * 128, :], 128, lambda h: XT[:, 4 * h:4 * h + 4, col:col + 128], writes=xtb(col, 128))

    def load_samples():
        from_tok(xsm, 4, lambda h: XT[:, 4 * h:4 * h + 4, SMP0:SMP0 + 4], writes=[XB[17]])

    def pool_layer(li, j, c0, c1, left, fix, flag_halo, save_tail, emit_out, samples):
        P.new_phase()
        S = c1 - c0
        W = 15 + S
        HF = arv(R3, 8 * 1171 * 4, F32, "p (c n) -> p c n", c=KC)
        TA = arv(R3 + 8 * 1171 * 4, 1171 * 4, F32)
        TB_ = arv(R3 + 9 * 1171 * 4, 1171 * 4, F32)
        DG = [arv(RW + 4096 + 4624 * i, 4624, BF16, "p (k n) -> p k n", k=2) for i in range(2)]
        PW = arv(RW, 4096, BF16, "p (g k n) -> p g k n", g=4, k=2)
        bHF, bTA, bTB, bPW = P.buf("HF"), P.buf("TA"), P.buf("TB"), P.buf("PW")
        bDG = [P.buf("dg0"), P.buf("dg1")]
        bR1, bR2 = P.buf("R1"), P.buf("R2")
        yT = yT_view()
        P.dma("sp", PW, wb_pool[j].rearrange("g (k p) n -> p g k n", p=128), reads=[bWPOOL[j]], writes=[bPW])
        prenorm(G(li, 0), c0, c1, lambda c, la, n: HF[:, c, 15 + la:15 + la + n], bHF)
        if left == "zero":
            pool(I("memset", HF[:, :, 0:15], 0.0), writes=[bHF])
        else:
            pool(I("tensor_copy", out=HF[:, :, 0:15], in_=HT15[:]), reads=[bHT15], writes=[bHF])
        if flag_halo:
            pool(I("tensor_scalar", out=HF[:, :, 15:15 + 128], in0=HF[:, :, 15:15 + 128], scalar1=FLG[:, 0:1], scalar2=None, op0=ALU.mult), reads=[bHF, bFLG], writes=[bHF])
        ntok = S - (4 if samples else 0)
        if save_tail:
            pool(I("tensor_copy", out=HT15[:], in_=HF[:, :, ntok:15 + ntok]), reads=[bHF], writes=[bHT15])
        if emit_out:
            to_tok(lambda c: HF[:, c, ntok:15 + ntok], 15, poolp_d[j], reads=[bHF])
        if samples:
            HS = arv(RW + 13344, 8 * 64 * 4, F32, "p (c n) -> p c n", c=KC)
            SA = arv(RW + 13344 + 2048, 256, F32)
            SB_ = arv(RW + 13344 + 2304, 256, F32)
            bHS = P.buf("HS")
            from_tok(spool_d[j], 60, lambda h: HS[:, 4 * h:4 * h + 4, :].rearrange("p c (b t) -> p c b t", t=16)[:, :, :, 0:15], writes=[bHS])
            dve(I("tensor_copy", out=HS[:].rearrange("p c (b t) -> p c b t", t=16)[:, :, :, 15], in_=HF[:, :, 15 + ntok:15 + S]), reads=[bHF], writes=[bHS])
            P.dma("sp", pools_d[j][:, 0:14, :], spool_d[j].rearrange("(b t) d -> b t d", t=15)[:, 1:15, :], writes=[bOUT])
            to_tok(lambda c: HF[:, c, 15 + ntok:15 + S], 4, pools_d[j][:, 14, :], reads=[bHF])
        for g in range(4):
            w = 2 << g
            for kk in range(2):
                c = 2 * g + kk
                src = HF[:, c, :]
                bufs = [bHF]
                shift = 1
                cur, curb = src, bHF
                for lv in range(g + 1):
                    dst, dstb = (TA, bTA) if lv % 2 == 0 else (TB_, bTB)
                    lo = 2 * shift - 1
                    dve(I("tensor_tensor", out=dst[:, lo:W], in0=cur[:, lo:W], in1=cur[:, lo - shift:W - shift], op=ALU.add),
                        reads=[curb], writes=[dstb])
                    cur, curb = dst, dstb
                    shift *= 2
                dve(I("scalar_tensor_tensor", out=DG[g % 2][:, kk, 0:S], in0=cur[:, 15:W], scalar=1.0 / w, in1=HF[:, c, 15:W], op0=ALU.mult, op1=ALU.subtract),
                    reads=[curb, bHF], writes=[bDG[g % 2]])
                for (fc, pos) in fix:
                    dve(I("tensor_tensor", out=SMALL[:, 0:15], in0=cur[:, 15 + fc:30 + fc], in1=INVC[:, pos, g, 0:15], op=ALU.mult),
                        reads=[curb, bINVC], writes=[bSMALL])
                    dve(I("tensor_tensor", out=DG[g % 2][:, kk, fc:fc + 15], in0=SMALL[:, 0:15], in1=HF[:, c, 15 + fc:30 + fc], op=ALU.subtract),
                        reads=[bSMALL, bHF], writes=[bDG[g % 2]])
                if samples:
                    cur, curb = HS[:, c, :], bHS
                    shift = 1
                    for lv in range(g + 1):
                        dst = SA if lv % 2 == 0 else SB_
                        lo = 2 * shift - 1
                        dve(I("tensor_tensor", out=dst[:, lo:64], in0=cur[:, lo:64], in1=cur[:, lo - shift:64 - shift], op=ALU.add),
                            reads=[curb], writes=[bHS])
                        cur = dst
                        shift *= 2
                    dve(I("scalar_tensor_tensor", out=DG[g % 2][:, kk, ntok:S], in0=cur.rearrange("p (b t) -> p b t", t=16)[:, :, 15], scalar=1.0 / w,
                                                                              in1=HF[:, c, 15 + ntok:15 + S], op0=ALU.mult, op1=ALU.subtract),
                        reads=[bHS, bHF], writes=[bDG[g % 2]])
            cnt = 0
            for co in range(2):
                cp = 2 * g + co
                yb = bR1 if cp < 4 else bR2
                for (a, n) in ntiles(c0, c1):
                    la = a - c0
                    bk = 4 + cnt % 2
                    cnt += 1
                    for k in range(2):
                        pe(I("matmul", out=PS[:, bk, :n], lhsT=PW[:, g, k, co * 128:(co + 1) * 128], rhs=DG[g % 2][:, k, la:la + n], start=(k == 0), stop=(k == 1)),
                           reads=[bPW, bDG[g % 2]], writes=[PB[bk]])
                    act(I("activation", out=yT(cp, la, n), in_=PS[:, bk, :n], func=AF.Copy, scale=vt(cp, 16 + j)), reads=[PB[bk], bVT], writes=[yb])
        postnorm_residual(G(li, 1), c0, c1, yT, [bR1, bR2])

    def h1_to_scratch(c0, ncols, dcol0):
        P.new_phase()
        hT = arv(R1, 18496, BF16, "p (c n) -> p c n", c=KC)
        b = P.buf("R1")
        prenorm(G(1, 0), c0, c0 + ncols, lambda c, la, n: hT[:, c, la:la + n], b)
        P.dma("sp", h1c_d[:, :, dcol0:dcol0 + ncols], hT[:, :, 0:ncols], reads=[b], writes=[bH1C])

    if 0 in layers:
        load_tiles(0, 128, 8)
        cast_group(0)
        pool_layer(0, 0, 128, 1152, "zero", [(0, 0)], False, True, False, False)
        anchor(0)
        ffn(0, 128, 1152)
        h1_to_scratch(128, 1024, 0)
        load_tiles(1024, 128, 8)
        pool_layer(0, 0, 128, 1152, "tail", [], False, True, False, False)
        ffn(0, 128, 1152)
        h1_to_scratch(128, 1024, 1024)
        for c in range(KC):
            act(I("activation", out=XT[:, c, 0:128], in_=XT[:, c, 1024:1152], func=AF.Copy), reads=[XB[8]], writes=[XB[0]])
        load_tiles(2048, 128, 8)
        pool_layer(0, 0, 128, 1152, "tail", [(0, 1)], False, True, False, False)
        anchor(1)
        ffn(0, 128, 1152)
        load_tiles(3072, 1152, 8)
        load_samples()
        pool_layer(0, 0, 1152, NE, "tail", [], False, False, True, True)
        ffn(0, 1152, NE)
    else:
        cast_group(0)
        load_tiles(1920, 0, 17)
        load_samples()

    def conv_layer(c0, c1, first):
        P.new_phase()
        S = c1 - c0
        samples = (c1 == NE)
        ntok = S - (4 if samples else 0)
        tiles = ntiles(c0, c1)
        hT = arv(R1, 18496, BF16, "p (c n) -> p c n", c=KC)
        UT = arv(R3, 8 * 1186 * 4, F32, "p (c n) -> p c n", c=KC)
        sT = arv(R3, 18496, BF16, "p (c n) -> p c n", c=KC)
        cT = yT_view()
        WA = [arv(RW + 4096 * i, 4096, BF16, "p (k n) -> p k n", k=KC) for i in range(3)]
        WO = [arv(RW + 12288 + 2048 * i, 2048, BF16, "p (k n) -> p k n", k=KC) for i in range(2)]
        US = arv(RW + 16384, 8 * 4 * 31 * 4, F32, "p (c b k) -> p c b k", c=KC, b=4)
        bR1, bR2, bUT = P.buf("R1"), P.buf("R2"), P.buf("UT")
        bWA = [P.buf(f"wa{i}") for i in range(3)]
        bWO = [P.buf(f"wo{i}") for i in range(2)]
        bUS = P.buf("US")
        prenorm(G(2, 0), c0, c1, lambda c, la, n: hT[:, c, la:la + n], bR1)
        if first:
            pool(I("memset", UT[:, :, 0:30], 0.0), writes=[bUT])
        else:
            pool(I("tensor_copy", out=UT[:, :, 0:30], in_=UT30[:]), reads=[bUT30], writes=[bUT])
        wsrc = wb_cin.rearrange("(k p) n -> p k n", p=128)
        cnt = 0
        for c in range(KC):
            s = c % 3
            P.dma("sp", WA[s][:, :, 0:128], wsrc[:, :, c * 128:(c + 1) * 128], reads=[bWCIN], writes=[bWA[s]])
            P.dma("sp", WA[s][:, :, 128:256], wsrc[:, :, D + c * 128:D + (c + 1) * 128], reads=[bWCIN], writes=[bWA[s]])
            for (a, n) in tiles:
                la = a - c0
                t = cnt % 2
                cnt += 1
                bg, bu = 2 * t, 2 * t + 1
                for k in range(KC):
                    pe(I("matmul", out=PS[:, bg, :n], lhsT=WA[s][:, k, 0:128], rhs=hT[:, k, la:la + n], start=(k == 0), stop=(k == KC - 1)), reads=[bWA[s], bR1], writes=[PB[bg]])
                for k in range(KC):
                    pe(I("matmul", out=PS[:, bu, :n], lhsT=WA[s][:, k, 128:256], rhs=hT[:, k, la:la + n], start=(k == 0), stop=(k == KC - 1)), reads=[bWA[s], bR1], writes=[PB[bu]])
                act(I("activation", out=SIG[:, t, :n], in_=PS[:, bu, :n], func=AF.Sigmoid, bias=vt(c, 19)), reads=[PB[bu], bVT], writes=[bSIG[t]])
                dve(I("scalar_tensor_tensor", out=UT[:, c, 30 + la:30 + la + n], in0=PS[:, bg, :n], scalar=vt(c, 18), in1=SIG[:, t, :n], op0=ALU.add, op1=ALU.mult),
                    reads=[PB[bg], bSIG[t], bVT], writes=[bUT])
        if first:
            pool(I("tensor_scalar", out=UT[:, :, 30:158], in0=UT[:, :, 30:158], scalar1=FLG[:, 0:1], scalar2=None, op0=ALU.mult), reads=[bUT, bFLG], writes=[bUT])
        if not samples:
            pool(I("tensor_copy", out=UT30[:], in_=UT[:, :, ntok:ntok + 30]), reads=[bUT], writes=[bUT30])
        else:
            to_tok(lambda c: UT[:, c, ntok:ntok + 30], 30, convp_d, reads=[bUT])
            from_tok(sconv_d, 120, lambda h: US[:, 4 * h:4 * h + 4, :, 0:30], writes=[bUS])
            dve(I("tensor_copy", out=US[:, :, :, 30], in_=UT[:, :, 30 + ntok:30 + S]), reads=[bUT], writes=[bUS])
            P.dma("sp", convs_d[:, 0:29, :], sconv_d.rearrange("(b t) d -> b t d", t=30)[:, 1:30, :], writes=[bOUT])
            to_tok(lambda c: UT[:, c, 30 + ntok:30 + S], 4, convs_d[:, 29, :], reads=[bUT])
            for c in range(KC):
                dve(I("tensor_tensor", out=US[:, c], in0=US[:, c], in1=VT[:, c, 24:55].unsqueeze(1).broadcast_to([128, 4, 31]), op=ALU.mult), reads=[bUS, bVT], writes=[bUS])
            dve(I("tensor_reduce", out=SMALL[:, 0:32], in_=US[:].rearrange("p c b k -> p (c b) k"), axis=mybir.AxisListType.X, op=ALU.add), reads=[bUS], writes=[bSMALL])
            for h in range(2):
                reg = arv(R1 if h == 0 else R2, 18496, F32, "p (c n) -> p c n", c=4)
                dve(I("tensor_tensor", out=reg[:, :, ntok:S], in0=SMALL[:, 16 * h:16 * h + 16].rearrange("p (c b) -> p c b", b=4),
                      in1=VT[:, 4 * h:4 * h + 4, 20:21].broadcast_to([128, 4, 4]), op=ALU.add), reads=[bSMALL, bVT], writes=[bR1 if h == 0 else bR2])
        UTB = [arv(RW, 2372, BF16), arv(RW + 20352, 2372, BF16)]
        DIAG = arv(RW + 2372, 7936, BF16, "p (k n) -> p k n", k=31)
        bUTB = [P.buf("utb0"), P.buf("utb1")]
        bDIAG = P.buf("diag")
        cnt = 0
        for c in range(KC):
            s = c % 2
            yb = bR1 if c < 4 else bR2
            act(I("activation", out=UTB[s][:, 0:30 + S], in_=UT[:, c, 0:30 + S], func=AF.Copy), reads=[bUT], writes=[bUTB[s]] + (bWA if s == 0 else []))
            dve(I("tensor_tensor", out=DIAG, in0=IDB[:].unsqueeze(1).broadcast_to([128, 31, 128]), in1=VT[:, c, 24:55].unsqueeze(2).broadcast_to([128, 31, 128]), op=ALU.mult),
                reads=[bIDB, bVT], writes=[bDIAG] + bWA)
            for (a, n) in tiles:
                if a >= SMP0:
                    continue
                la = a - c0
                bk = 4 + cnt % 2
                cnt += 1
                for k in range(31):
                    pe(I("matmul", out=PS[:, bk, :n], lhsT=DIAG[:, k, :], rhs=UTB[s][:, la + k:la + k + n], start=(k == 0), stop=(k == 30)), reads=[bDIAG, bUTB[s]], writes=[PB[bk]])
                act(I("activation", out=cT(c, la, n), in_=PS[:, bk, :n], func=AF.Identity, bias=vt(c, 20)), reads=[PB[bk], bVT], writes=[yb])
        for (a, n) in tiles:
            la = a - c0
            for c in range(KC):
                yb = bR1 if c < 4 else bR2
                pe(I("matmul", out=PS[:, 6, :n], lhsT=ONF[:], rhs=cT(c, la, n), start=(c == 0), stop=(c == KC - 1)), reads=[yb, bONF], writes=[PB[6]])
            for c in range(KC):
                yb = bR1 if c < 4 else bR2
                s = c % 2
                act(I("activation", out=STG[:, s, :n], in_=cT(c, la, n), func=AF.Square), reads=[yb], writes=[bSTG[s]])
                pe(I("matmul", out=PS[:, 7, :n], lhsT=ONF[:], rhs=STG[:, s, :n], start=(c == 0), stop=(c == KC - 1)), reads=[bSTG[s], bONF], writes=[PB[7]])
            MU = SIG[:, 0, :n]
            T2 = SIG[:, 1, :n]
            RS = RSTD[:, la:la + n]
            dve(I("tensor_scalar", out=MU, in0=PS[:, 6, :n], scalar1=1.0 / D, scalar2=None, op0=ALU.mult), reads=[PB[6]], writes=[bSIG[0]])
            dve(I("tensor_scalar", out=RS, in0=PS[:, 7, :n], scalar1=1.0 / D, scalar2=LN_EPS, op0=ALU.mult, op1=ALU.add), reads=[PB[7]], writes=[bRS])
            dve(I("tensor_tensor", out=T2, in0=MU, in1=MU, op=ALU.mult), reads=[bSIG[0]], writes=[bSIG[1]])
            dve(I("tensor_tensor", out=RS, in0=RS, in1=T2, op=ALU.subtract), reads=[bRS, bSIG[1]], writes=[bRS])
            act(I("activation", out=RS, in_=RS, func=AF.Sqrt), reads=[bRS], writes=[bRS])
            dve(I("reciprocal", out=RS, in_=RS), reads=[bRS], writes=[bRS])
            for c in range(KC):
                yb = bR1 if c < 4 else bR2
                dve(I("tensor_tensor", out=cT(c, la, n), in0=cT(c, la, n), in1=MU, op=ALU.subtract), reads=[yb, bSIG[0]], writes=[yb])
            for c in range(KC):
                yb = bR1 if c < 4 else bR2
                dve(I("tensor_tensor", out=cT(c, la, n), in0=cT(c, la, n), in1=RS, op=ALU.mult), reads=[yb, bRS], writes=[yb])
            for c in range(KC):
                yb = bR1 if c < 4 else bR2
                act(I("activation", out=sT[:, c, la:la + n], in_=cT(c, la, n), func=AF.Silu, scale=vt(c, 21), bias=vt(c, 22)), reads=[yb, bVT], writes=[bUT])
        osrc = wb_cout.rearrange("(k p) n -> p k n", p=128)
        cnt = 0
        for c in range(KC):
            s = c % 2
            P.dma("sp", WO[s], osrc[:, :, c * 128:(c + 1) * 128], reads=[bWCOUT], writes=[bWO[s]])
            yb = bR1 if c < 4 else bR2
            for (a, n) in tiles:
                la = a - c0
                bk = 4 + cnt % 2
                cnt += 1
                for k in range(KC):
                    pe(I("matmul", out=PS[:, bk, :n], lhsT=WO[s][:, k, :], rhs=sT[:, k, la:la + n], start=(k == 0), stop=(k == KC - 1)), reads=[bWO[s], bUT], writes=[PB[bk]])
                act(I("activation", out=cT(c, la, n), in_=PS[:, bk, :n], func=AF.Identity, bias=vt(c, 23)), reads=[PB[bk], bVT], writes=[yb])
        postnorm_residual(G(2, 1), c0, c1, cT, [bR1, bR2])

    def PSB(b):
        return PS[:, b, :].bitcast(BF16)

    def attn_layer():
        P.new_phase()
        H1E = arv(0, 34880, BF16, "p (c n) -> p c n", c=KC)
        H1C = arv(34880, 32768, BF16, "p (c n) -> p c n", c=KC)
        WQ = [arv(67648 + 16384 * i, 16384, BF16, "p (k n) -> p k n", k=KC) for i in range(2)]
        QS = arv(100416, 4096, BF16, "p (s n) -> p s n", s=2)
        VS = [arv(104512 + 2080 * i, 2080, BF16, "p (h d) -> p h d", h=16) for i in range(3)]
        bH1E, bH1Cs = P.buf("H1E"), P.buf("H1C")
        bWQ = [P.buf("wq0"), P.buf("wq1")]
        bQS = [P.buf("qs0"), P.buf("qs1")]
        bVS = [P.buf(f"vs{i}") for i in range(3)]
        prenorm(G(1, 0), 0, 1152, lambda c, la, n: H1E[:, c, la:la + n], bH1E)
        prenorm(G(1, 0), 1152, NE, lambda c, la, n: H1E[:, c, 1152 + la:1152 + la + n], bH1E)
        P.dma("sp", H1C, h1c_d, reads=[bH1C], writes=[bH1Cs])
        pool(I("tensor_copy", out=VS[0][:, :, 64], in_=FLG[:, 0:16]), reads=[bFLG], writes=[bVS[0]])
        for i in (1, 2):
            pool(I("memset", VS[i][:, :, 64:65], 1.0), writes=[bVS[i]])
        wsrc = wb_qkv.rearrange("(k p) n -> p k n", p=128)
        tl_ctx = [("c", t, H1C, t * 128, 128, t * 128) for t in range(15)]
        tl_e = [("e", t, H1E, t * 128, 128, 1920 + t * 128) for t in range(17)]
        tl_s = [("s", 0, H1E, SMP0, 4, 4096)]
        cntb = cq = cv = 0
        for cg in range(9):
            g, typ = cg // 3, cg % 3
            s = cg % 2
            P.dma("sp", WQ[s], wsrc[:, :, cg * 1024:(cg + 1) * 1024], reads=[bWQKV], writes=[bWQ[s]])
            tls = (tl_ctx if typ > 0 else []) + tl_e + tl_s
            for (kind, t, H, col, M, row0) in tls:
                hb = bH1Cs if kind == "c" else bH1E
                b0 = 2 * (cntb % 2)
                cntb += 1
                for hf in range(2):
                    for k in range(KC):
                        pe(I("matmul", out=PS[0:M, b0 + hf, :], lhsT=H[:, k, col:col + M], rhs=WQ[s][:, k, hf * 512:(hf + 1) * 512], start=(k == 0), stop=(k == KC - 1)),
                           reads=[hb, bWQ[s]], writes=[PB[b0 + hf]])
                if typ < 2:
                    q = cq % 2
                    cq += 1
                    act(I("activation", out=QS[0:M, q, 0:512], in_=PS[0:M, b0, :], func=AF.Copy), reads=[PB[b0]], writes=[bQS[q]])
                    dve(I("tensor_copy", out=QS[0:M, q, 512:1024], in_=PS[0:M, b0 + 1, :]), reads=[PB[b0 + 1]], writes=[bQS[q]])
                    o0 = g * 2048 + typ * 1024
                    P.dma("pool", qk_d[row0:row0 + M, o0:o0 + 1024], QS[0:M, q, :], reads=[bQS[q]], writes=[bQK])
                else:
                    isctx = kind == "c" or (kind == "e" and t == 0)
                    vi = 0 if isctx else 1 + cv % 2
                    cv += 1
                    act(I("activation", out=VS[vi][0:M, 0:8, 0:64], in_=PS[0:M, b0, :].rearrange("p (h d) -> p h d", h=8), func=AF.Copy), reads=[PB[b0]], writes=[bVS[vi]])
                    dve(I("tensor_copy", out=VS[vi][0:M, 8:16, 0:64], in_=PS[0:M, b0 + 1, :].rearrange("p (h d) -> p h d", h=8)), reads=[PB[b0 + 1]], writes=[bVS[vi]])
                    P.dma("pool", v_d[row0:row0 + M, g * 1040:(g + 1) * 1040], VS[vi][0:M].rearrange("p h d -> p (h d)"), reads=[bVS[vi]], writes=[bV])
                if typ > 0 and (kind == "s" or (kind == "e" and t >= 1)):
                    if kind == "s":
                        dst = sws_d[g][0:4, typ - 1, :]
                    else:
                        orow = (t - 1) * 128 - (2048 - WINS[g])
                        dst = None if orow < 0 else swp_d[g][orow:orow + 128, typ - 1, :]
                    if dst is not None:
                        f = stg_ctr[0] % 2
                        stg_ctr[0] += 1
                        act(I("activation", out=STG[0:M, f, 0:512], in_=PS[0:M, b0, :], func=AF.Copy), reads=[PB[b0]], writes=[bSTG[f]])
                        dve(I("tensor_copy", out=STG[0:M, f, 512:1024], in_=PS[0:M, b0 + 1, :]), reads=[PB[b0 + 1]], writes=[bSTG[f]])
                        P.dma("act", dst, STG[0:M, f, :], reads=[bSTG[f]], writes=[bOUT])

        if attn_stop == 'A':
            return
        P.new_phase()
        TB = [arv(8192 * g, 8192, BF16, "p (h t q) -> p h t q", h=16, t=2) for g in range(3)]
        TBf = [arv(8192 * g, 8192, BF16) for g in range(3)]
        HK = arv(24576, 16384, F32)
        VEC = arv(40960, 2048, F32)
        RB = arv(43008, 192, F32)
        OH = arv(43264, 2048, F32)
        NR = arv(45312, 2048, F32)
        bTB = [P.buf(f"tb{g}") for g in range(3)]
        bHK, bVECs, bRB, bOH, bNR = P.buf("HK"), P.buf("VEC"), P.buf("RB"), P.buf("OH"), P.buf("NR")
        P.dma("sp", RB[0:32, :], relb_d, writes=[bRB])
        for g in range(3):
            P.dma("sp", OH[0:32, :], oh_d[g], writes=[bOH])
            P.dma("sp", NR[0:1, :], negr_d[g], writes=[bNR])
            pe(I("matmul", out=PS[0:16, 0, :], lhsT=RB[0:32, g * 16:(g + 1) * 16], rhs=OH[0:32, :], start=True, stop=False), reads=[bRB, bOH], writes=[PB[0]])
            pe(I("matmul", out=PS[0:16, 0, :], lhsT=ONF[0:1, 0:16], rhs=NR[0:1, :], start=False, stop=True), reads=[bONF, bNR], writes=[PB[0]])
            act(I("activation", out=VEC[0:16, :], in_=PS[0:16, 0, :], func=AF.Copy), reads=[PB[0]], writes=[bVECs])
            P.dma("sp", vec_d[g], VEC[0:16, :], reads=[bVECs], writes=[bVEC])
            for h4 in range(4):
                hsrc = bass.AP(vec_d.tensor, g * 16 * 512 + h4 * 4 * 512, [[1, 128], [512, 4], [256, 2], [1, 128]])
                P.dma("sp", HK.rearrange("p (h t q) -> p h t q", h=16, t=2)[:, 4 * h4:4 * h4 + 4], hsrc, reads=[bVEC], writes=[bHK])
            for i in range(8):
                bk = 2 + i % 2
                pe(I("matmul", out=PS[:, bk, :], lhsT=JF[:], rhs=HK[:, i * 512:(i + 1) * 512], start=True, stop=True), reads=[bJF, bHK], writes=[PB[bk]])
                act(I("activation", out=TBf[g][:, i * 512:(i + 1) * 512], in_=PS[:, bk, :], func=AF.Exp), reads=[PB[bk]], writes=[bTB[g]])

        if attn_stop == 'B':
            return
        P.new_phase()
        QTOK = [arv(24576 + 2048 * i, 2048, BF16) for i in range(2)]
        KTOK = [arv(28672 + 2048 * i, 2048, BF16) for i in range(2)]
        QT = [arv(32768 + 2048 * i, 2048, BF16, "p (c n) -> p c n", c=KC) for i in range(2)]
        KT = [arv(36864 + 2048 * i, 2048, BF16, "p (c n) -> p c n", c=KC) for i in range(3)]
        VA = [arv(43008 + 2080 * i, 2080, BF16) for i in range(3)]
        PT = [arv(49248 + 1024 * i, 1024, BF16, "p (h t q) -> p h t q", h=2, t=2) for i in range(3)]
        UZS = [arv(52320 + 4160 * i, 4160, F32) for i in range(2)]
        bQTOK = [P.buf("qtok0"), P.buf("qtok1")]
        bKTOK = [P.buf("ktok0"), P.buf("ktok1")]
        bQT = [P.buf("qt0"), P.buf("qt1")]
        bKT = [P.buf(f"kt{i}") for i in range(3)]
        bVA = [P.buf(f"va{i}") for i in range(3)]
        bPT = [P.buf(f"pt{i}") for i in range(3)]
        bUZS = [P.buf("uzs0"), P.buf("uzs1")]
        blocks = []
        qi = 0
        for g, dil in enumerate(DILS):
            NB = 32 // dil
            QB0 = 16 // dil - 1
            for r in range(dil):
                first = True
                for kb in range(max(QB0 - 1, 0), NB):
                    isq = kb >= QB0
                    blocks.append(dict(g=g, dil=dil, r=r, kb=kb, isq=isq, first=first, qb=(qi % 2) if isq else None,
                                       qlo=((128 - 128 // dil) if kb == QB0 else 0)))
                    if isq:
                        qi += 1
                    first = False

        def issue_loads(i):
            B_ = blocks[i]
            g, dil, r, kb = B_["g"], B_["dil"], B_["r"], B_["kb"]
            sl, kti = i % 3, i % 2
            row0 = kb * 128 * dil + r
            P.dma("act", KTOK[kti], bass.AP(qk_d.tensor, row0 * QROW + g * 2048 + 1024, [[dil * QROW, 128], [1, 1024]]), reads=[bQK], writes=[bKTOK[kti]])
            P.dma("act", VA[sl], bass.AP(v_d.tensor, row0 * VROW + g * 1040, [[dil * VROW, 128], [1, 1040]]), reads=[bV], writes=[bVA[sl]])
            if B_["isq"]:
                qb, qlo = B_["qb"], B_["qlo"]
                P.dma("act", QTOK[qb][qlo:128, :], bass.AP(qk_d.tensor, (row0 + qlo * dil) * QROW + g * 2048, [[dil * QROW, 128 - qlo], [1, 1024]]), reads=[bQK], writes=[bQTOK[qb]])

        ui = 0
        issue_loads(0)
        for i, B_ in enumerate(blocks):
            g, dil, r, kb = B_["g"], B_["dil"], B_["r"], B_["kb"]
            if i + 1 < len(blocks):
                issue_loads(i + 1)
            sl, kti = i % 3, i % 2
            for c in range(KC):
                pe(I("transpose", out=PSB(0)[:, c * 128:(c + 1) * 128], in_=KTOK[kti][:, c * 128:(c + 1) * 128], identity=IDB[:]), reads=[bKTOK[kti], bIDB], writes=[PB[0]])
            dve(I("tensor_copy", out=KT[sl], in_=PSB(0).rearrange("p (c n) -> p c n", c=KC)), reads=[PB[0]], writes=[bKT[sl]])
            if not B_["isq"]:
                continue
            qlo, qb = B_["qlo"], B_["qb"]
            nq = 128 - qlo
            for c in range(KC):
                pe(I("transpose", out=PSB(1)[:, c * 128:(c + 1) * 128], in_=QTOK[qb][:, c * 128:(c + 1) * 128], identity=IDB[:]), reads=[bQTOK[qb], bIDB], writes=[PB[1]])
            act(I("activation", out=QT[qb], in_=PSB(1).rearrange("p (c n) -> p c n", c=KC), func=AF.Copy), reads=[PB[1]], writes=[bQT[qb]])
            has_prev = (kb >= 1) and not B_["first"]
            pc0 = 0 if has_prev else 1
            kts = ([(0, (i - 1) % 3)] if has_prev else []) + [(1, sl)]

            def heads_of(hp):
                base = 4 * (hp // 2) + (hp % 2)
                return (base, base + 2)

            def s_stage(hp):
                bank = 2 + hp % 3
                hA, hB = heads_of(hp)
                p0 = (hA % 2) * 64
                for hh, h in enumerate((hA, hB)):
                    for (pc, ks) in kts:
                        o0 = hh * 256 + pc * 128
                        pe(I("matmul", out=PS[:, bank, o0 + qlo:o0 + 128], lhsT=KT[ks][p0:p0 + 64, h // 2, :], rhs=QT[qb][p0:p0 + 64, h // 2, qlo:128], start=True, stop=True),
                           reads=[bKT[ks], bQT[qb]], writes=[PB[bank]])
                pt = hp % 3
                act(I("activation", out=PT[pt][:, :, pc0:2, qlo:128], in_=PS[:, bank, :].rearrange("p (h t q) -> p h t q", h=2, t=2)[:, :, pc0:2, qlo:128], func=AF.Exp, scale=0.125),
                    reads=[PB[bank]], writes=[bPT[pt]])
                dve(I("tensor_tensor", out=PT[pt][:, :, pc0:2, qlo:128], in0=PT[pt][:, :, pc0:2, qlo:128], in1=TB[g][:, hA:hB + 1:2, pc0:2, qlo:128], op=ALU.mult),
                    reads=[bPT[pt], bTB[g]], writes=[bPT[pt]])

            def pv_stage(hp):
                pt = hp % 3
                for hh, h in enumerate(heads_of(hp)):
                    bank = 5 + h // 7
                    off = (h % 7) * 65
                    for jx, (pc, ks) in enumerate(kts):
                        pe(I("matmul", out=PS[0:nq, bank, off:off + 65], lhsT=PT[pt][:, hh, pc, qlo:128], rhs=VA[ks][:, h * 65:(h + 1) * 65], start=(jx == 0), stop=(jx == len(kts) - 1)),
                           reads=[bPT[pt], bVA[ks]], writes=[PB[bank]])

            for ii in range(10):
                if ii < 8:
                    s_stage(ii)
                if ii >= 2:
                    pv_stage(ii - 2)
            u = ui % 2
            ui += 1
            act(I("activation", out=UZS[u][0:nq, 0:455], in_=PS[0:nq, 5, 0:455], func=AF.Copy), reads=[PB[5]], writes=[bUZS[u]])
            dve(I("tensor_copy", out=UZS[u][0:nq, 455:910], in_=PS[0:nq, 6, 0:455]), reads=[PB[6]], writes=[bUZS[u]])
            act(I("activation", out=UZS[u][0:nq, 910:1040], in_=PS[0:nq, 7, 0:130], func=AF.Copy), reads=[PB[7]], writes=[bUZS[u]])
            e0 = (kb * 128 + qlo) * dil + r - 1920
            P.dma("act", bass.AP(uz_d.tensor, (g * 2176 + e0) * UROW, [[dil * UROW, nq], [1, 1040]]), UZS[u][0:nq, :], reads=[bUZS[u]], writes=[bUZ])

        if attn_stop == 'C':
            return
        P.new_phase()
        SQK = arv(24576, 12288, BF16)
        SV = arv(36864, 6240, BF16)
        CK = [arv(43104 + 4096 * i, 4096, F32) for i in range(2)]
        CV = [arv(51296 + 4096 * i, 4096, F32) for i in range(2)]
        CVA = arv(59488, 2080, BF16, "p (h d) -> p h d", h=16)
        CVAf = arv(59488, 2080, BF16)
        PR = arv(61568, 4096, F32)
        LG = arv(65664, 64, F32)
        PEX = arv(65728, 64, F32)
        PM = arv(65792, 32, BF16)
        MSK = arv(65920, 4160, F32)
        UZA = arv(70080, 4160, F32)
        SEL = arv(74240, 1024, BF16, "p (b m) -> p b m", b=4)
        OHB = arv(75264, 64, F32, "p (b m) -> p b m", b=4)
        BD = arv(75328, 4160, F32, "p (h d) -> p h d", h=16)
        BDf = arv(75328, 4160, F32)
        RB0 = arv(79488, 192, F32)
        PRS = arv(79680, 4160, F32)
        LGS = arv(83840, 64, F32)
        PSS = arv(83904, 64, F32)
        bS = {n: P.buf(n) for n in "SQK SV CVA PR LG PEX PM MSK UZA SEL OHB BD RB0 PRS LGS PSS".split()}
        bCK = [P.buf("ck0"), P.buf("ck1")]
        bCV = [P.buf("cv0"), P.buf("cv1")]
        pool(I("memset", SEL[0:4], 1.0), writes=[bS["SEL"]])
        pool(I("affine_select", out=SEL[0:4], in_=SEL[0:4], pattern=[[-1, 4], [0, 128]], compare_op=ALU.is_equal, fill=0.0, base=0, channel_multiplier=1), reads=[bS["SEL"]], writes=[bS["SEL"]])
        pool(I("memset", OHB[0:16], 1.0), writes=[bS["OHB"]])
        pool(I("affine_select", out=OHB[0:16], in_=OHB[0:16], pattern=[[1, 4], [-1, 4]], compare_op=ALU.is_equal, fill=0.0, base=0, channel_multiplier=0), reads=[bS["OHB"]], writes=[bS["OHB"]])
        pool(I("memset", BD[0:16], 1.0), writes=[bS["BD"]])
        pool(I("affine_select", out=BD[0:16], in_=BD[0:16], pattern=[[-1, 16], [0, 65]], compare_op=ALU.is_equal, fill=0.0, base=0, channel_multiplier=1), reads=[bS["BD"]], writes=[bS["BD"]])
        pool(I("memset", CVA[:, :, 64:65], 1.0), writes=[bS["CVA"]])
        pool(I("memset", UZA[0:4, :], 0.0), writes=[bS["UZA"]])
        P.dma("sp", RB0[0:4, :], bass.AP(relb_d.tensor, 0, [[0, 4], [1, 48]]), writes=[bS["RB0"]])
        P.dma("sp", SQK[0:4, :], qk_d[4096:4100, :], reads=[bQK], writes=[bS["SQK"]])
        P.dma("sp", SV[0:4, :], v_d[4096:4100, :], reads=[bV], writes=[bS["SV"]])
        chunks = [(0, 455), (455, 455), (910, 130)]
        ci = 0
        for b in range(4):
            for g, dil in enumerate(DILS):
                cc = ci % 2
                ci += 1
                P.dma("sp", CK[cc], bass.AP(cache_d[g].tensor, b * WINS[g] * 2048, [[dil * 2048, 128], [1, 1024]]), writes=[bCK[cc]])
                P.dma("sp", CV[cc], bass.AP(cache_d[g].tensor, b * WINS[g] * 2048 + 1024, [[dil * 2048, 128], [1, 1024]]), writes=[bCV[cc]])
                for hf in range(2):
                    pe(I("matmul", out=PS[:, hf, :], lhsT=SEL[0:4, b, :], rhs=SQK[0:4, g * 2048 + hf * 512:g * 2048 + (hf + 1) * 512], start=True, stop=True), reads=[bS["SEL"], bS["SQK"]], writes=[PB[hf]])
                    dve(I("tensor_tensor", out=PR[:, hf * 512:(hf + 1) * 512], in0=CK[cc][:, hf * 512:(hf + 1) * 512], in1=PS[:, hf, :], op=ALU.mult), reads=[bCK[cc], PB[hf]], writes=[bS["PR"]])
                dve(I("tensor_reduce", out=LG[:, 0:16], in_=PR.rearrange("p (h d) -> p h d", h=16), axis=mybir.AxisListType.X, op=ALU.add), reads=[bS["PR"]], writes=[bS["LG"]])
                act(I("activation", out=PEX[:, 0:16], in_=LG[:, 0:16], func=AF.Exp, scale=0.125), reads=[bS["LG"]], writes=[bS["PEX"]])
                dve(I("tensor_tensor", out=PM[:, 0:16], in0=PEX[:, 0:16], in1=TB[g][:, :, 0, 0], op=ALU.mult), reads=[bS["PEX"], bTB[g]], writes=[bS["PM"]])
                dve(I("tensor_copy", out=CVA[:, :, 0:64], in_=CV[cc].rearrange("p (h d) -> p h d", h=16)), reads=[bCV[cc]], writes=[bS["CVA"]])
                for i3, (off, ncol) in enumerate(chunks):
                    pe(I("matmul", out=PS[0:16, 2 + i3, 0:ncol], lhsT=PM[:, 0:16], rhs=CVAf[:, off:off + ncol], start=True, stop=True), reads=[bS["PM"], bS["CVA"]], writes=[PB[2 + i3]])
                    dve(I("tensor_tensor", out=MSK[0:16, off:off + ncol], in0=PS[0:16, 2 + i3, 0:ncol], in1=BDf[0:16, off:off + ncol], op=ALU.mult), reads=[PB[2 + i3], bS["BD"]], writes=[bS["MSK"]])
                for i3, (off, ncol) in enumerate(chunks):
                    pe(I("matmul", out=PS[0:4, 5 + i3, 0:ncol], lhsT=OHB[0:16, b, :], rhs=MSK[0:16, off:off + ncol], start=True, stop=True), reads=[bS["OHB"], bS["MSK"]], writes=[PB[5 + i3]])
                    dve(I("tensor_tensor", out=UZA[0:4, off:off + ncol], in0=UZA[0:4, off:off + ncol], in1=PS[0:4, 5 + i3, 0:ncol], op=ALU.add), reads=[bS["UZA"], PB[5 + i3]], writes=[bS["UZA"]])
        for g in range(3):
            dve(I("tensor_tensor", out=PRS[0:4, 0:1024], in0=SQK[0:4, g * 2048:g * 2048 + 1024], in1=SQK[0:4, g * 2048 + 1024:g * 2048 + 2048], op=ALU.mult), reads=[bS["SQK"]], writes=[bS["PRS"]])
            dve(I("tensor_reduce", out=LGS[0:4, 0:16], in_=PRS[0:4, 0:1024].rearrange("p (h d) -> p h d", h=16), axis=mybir.AxisListType.X, op=ALU.add), reads=[bS["PRS"]], writes=[bS["LGS"]])
            dve(I("scalar_tensor_tensor", out=LGS[0:4, 0:16], in0=LGS[0:4, 0:16], scalar=0.125, in1=RB0[0:4, g * 16:(g + 1) * 16], op0=ALU.mult, op1=ALU.add), reads=[bS["LGS"], bS["RB0"]], writes=[bS["LGS"]])
            act(I("activation", out=PSS[0:4, 0:16], in_=LGS[0:4, 0:16], func=AF.Exp), reads=[bS["LGS"]], writes=[bS["PSS"]])
            prs3 = PRS[0:4, 0:1040].rearrange("p (h d) -> p h d", h=16)
            sv3 = SV[0:4, g * 1040:(g + 1) * 1040].rearrange("p (h d) -> p h d", h=16)
            dve(I("tensor_tensor", out=prs3[:, :, 0:64], in0=sv3[:, :, 0:64], in1=PSS[0:4, 0:16].unsqueeze(2).broadcast_to([4, 16, 64]), op=ALU.mult), reads=[bS["SV"], bS["PSS"], bS["PRS"]], writes=[bS["PRS"]])
            dve(I("tensor_copy", out=prs3[:, :, 64], in_=PSS[0:4, 0:16]), reads=[bS["PSS"]], writes=[bS["PRS"]])
            dve(I("tensor_tensor", out=UZA[0:4, :], in0=UZA[0:4, :], in1=PRS[0:4, 0:1040], op=ALU.add), reads=[bS["UZA"], bS["PRS"]], writes=[bS["UZA"]])

        if attn_stop == 'D1':
            return
        P.new_phase()
        ATT = arv(0, 34880, BF16, "p (c n) -> p c n", c=KC)
        UZ3 = [arv(74240 + 12480 * i, 12480, F32, "p (g n) -> p g n", g=3) for i in range(2)]
        ATK = [arv(99200 + 2048 * i, 2048, BF16) for i in range(2)]
        ZR = arv(103296, 64, F32)
        ATS = arv(103360, 2048, BF16)
        bATT, bZR, bATS = P.buf("ATT"), P.buf("ZR"), P.buf("ATS")
        bUZ3 = [P.buf("uz30"), P.buf("uz31")]
        bATK = [P.buf("atk0"), P.buf("atk1")]
        bUZA2 = P.buf("UZA2")
        uza3 = UZA[0:4, :].rearrange("p (h d) -> p h d", h=16)
        dve(I("tensor_scalar", out=ZR[0:4, 0:16], in0=uza3[:, :, 64], scalar1=1e-30, scalar2=None, op0=ALU.add), reads=[bS["UZA"]], writes=[bZR])
        dve(I("reciprocal", out=ZR[0:4, 0:16], in_=ZR[0:4, 0:16]), reads=[bZR], writes=[bZR])
        dve(I("tensor_tensor", out=ATS[0:4, 0:1024].rearrange("p (h d) -> p h d", h=16), in0=uza3[:, :, 0:64], in1=ZR[0:4, 0:16].unsqueeze(2).broadcast_to([4, 16, 64]), op=ALU.mult),
            reads=[bS["UZA"], bZR], writes=[bATS])
        for c in range(KC):
            pe(I("transpose", out=PSB(0)[:, c * 4:(c + 1) * 4], in_=ATS[0:4, c * 128:(c + 1) * 128], identity=IDB[0:4, 0:4]), reads=[bATS, bIDB], writes=[PB[0]])
        act(I("activation", out=ATT[:, :, SMP0:NE], in_=PSB(0)[:, 0:32].rearrange("p (c n) -> p c n", c=KC), func=AF.Copy), reads=[PB[0]], writes=[bATT])
        def merge_load(e):
            for gg in range(3):
                P.dma("act", UZ3[e % 2][:, gg, :], uz_d[gg, e * 128:(e + 1) * 128, :], reads=[bUZ], writes=[bUZ3[e % 2]])

        merge_load(0)
        for e in range(17):
            u3 = e % 2
            if e + 1 < 17:
                merge_load(e + 1)
            dve(I("tensor_tensor", out=UZ3[u3][:, 0, :], in0=UZ3[u3][:, 0, :], in1=UZ3[u3][:, 1, :], op=ALU.add), reads=[bUZ3[u3]], writes=[bUZ3[u3]])
            dve(I("tensor_tensor", out=UZ3[u3][:, 0, :], in0=UZ3[u3][:, 0, :], in1=UZ3[u3][:, 2, :], op=ALU.add), reads=[bUZ3[u3]], writes=[bUZ3[u3]])
            s3 = UZ3[u3][:, 0, :].rearrange("p (h d) -> p h d", h=16)
            dve(I("tensor_scalar", out=ZR[:, 0:16], in0=s3[:, :, 64], scalar1=1e-30, scalar2=None, op0=ALU.add), reads=[bUZ3[u3]], writes=[bZR])
            dve(I("reciprocal", out=ZR[:, 0:16], in_=ZR[:, 0:16]), reads=[bZR], writes=[bZR])
            dve(I("tensor_tensor", out=ATK[u3].rearrange("p (h d) -> p h d", h=16), in0=s3[:, :, 0:64], in1=ZR[:, 0:16].unsqueeze(2).broadcast_to([128, 16, 64]), op=ALU.mult),
                reads=[bUZ3[u3], bZR], writes=[bATK[u3]])
            for c in range(KC):
                pe(I("transpose", out=PSB(1)[:, c * 128:(c + 1) * 128], in_=ATK[u3][:, c * 128:(c + 1) * 128], identity=IDB[:]), reads=[bATK[u3], bIDB], writes=[PB[1]])
            act(I("activation", out=ATT[:, :, e * 128:(e + 1) * 128], in_=PSB(1).rearrange("p (c n) -> p c n", c=KC), func=AF.Copy), reads=[PB[1]], writes=[bATT])

        if attn_stop == 'D2':
            return
        P.new_phase()
        WO = [arv(107904 + 2048 * i, 2048, BF16, "p (k n) -> p k n", k=KC) for i in range(2)]
        ylo = arv(34880, 18496, F32, "p (c n) -> p c n", c=4)
        yhi = arv(53376, 18496, F32, "p (c n) -> p c n", c=4)
        bATT2, bYL, bYH = P.buf("ATT2"), P.buf("YL"), P.buf("YH")
        bWO = [P.buf("wo0"), P.buf("wo1")]
        osrc = wb_o.rearrange("(k p) n -> p k n", p=128)
        passes = [(0, 1152), (1152, NE)]

        def Y(c, col, n):
            p0 = 0 if col < 1152 else 1152
            if isinstance(c, str):
                return (ylo if c == "h0" else yhi)[:, :, col - p0:col - p0 + n]
            return (ylo if c < 4 else yhi)[:, c % 4, col - p0:col - p0 + n]

        for (c0, c1) in passes:
            cnt = 0
            for c in range(KC):
                s = c % 2
                P.dma("sp", WO[s], osrc[:, :, c * 128:(c + 1) * 128], reads=[bWO_], writes=[bWO[s]])
                yb = bYL if c < 4 else bYH
                for (a, n) in ntiles(c0, c1):
                    bk = 4 + cnt % 2
                    cnt += 1
                    for k in range(KC):
                        pe(I("matmul", out=PS[:, bk, :n], lhsT=WO[s][:, k, :], rhs=ATT[:, k, a:a + n], start=(k == 0), stop=(k == KC - 1)), reads=[bWO[s], bATT], writes=[PB[bk]])
                    act(I("activation", out=Y(c, a, n), in_=PS[:, bk, :n], func=AF.Copy), reads=[PB[bk]], writes=[yb])
            postnorm_residual(G(1, 1), c0, c1, lambda c, la, n, c0=c0: Y(c, c0 + la, n), [bYL, bYH])

    if 0 not in layers:
        anchor(0)
        anchor(1)
    if 1 in layers:
        attn_layer()
        anchor(2)
        ffn(1, 0, 1152)
        ffn(1, 1152, NE)
    if 2 in layers:
        conv_layer(0, 1152, True)
        ffn(2, 0, 1152)
        conv_layer(1152, NE, False)
        ffn(2, 1152, NE)
    if 3 in layers:
        pool_layer(3, 1, 0, 1152, "zero", [(128, 1)], True, True, False, False)
        ffn(3, 0, 1152)
        pool_layer(3, 1, 1152, NE, "tail", [], False, False, True, True)
        ffn(3, 1152, NE)

    P.new_phase()
    for t in range(16):
        col = 128 + t * 128
        to_tok(lambda c, col=col: XT[:, c, col:col + 128], 128, y_d[t * 128:(t + 1) * 128, :], reads=xtb(col, 128))
    to_tok(lambda c: XT[:, c, SMP0:NE], 4, ys_d, reads=[XB[17]])
    global LASTP
    LASTP = P
    P.finalize(st)
    st.close()
    return nc


def _in_maps(inp):
    oh, negr = static_tables()
    f32 = np.float32
    xp = np.asarray(inp["x_prompt"], f32)
    vecs = np.zeros((64, D), f32)
    vecs[0:16] = np.asarray(inp["norm_g"], f32).reshape(16, D)
    vecs[16:18] = np.asarray(inp["pool_scale"], f32)
    vecs[18:20] = np.asarray(inp["conv_b_in"], f32).reshape(2, D)
    vecs[20] = np.asarray(inp["conv_b_dw"], f32)[0]
    vecs[21] = np.asarray(inp["conv_ln_g"], f32)[0]
    vecs[22] = np.asarray(inp["conv_ln_b"], f32)[0]
    vecs[23] = np.asarray(inp["conv_b_out"], f32)[0]
    vecs[24:55] = np.asarray(inp["conv_w_dw"], f32)[0]
    shared = {
        "vecs": vecs,
        "w_ffn_in": np.ascontiguousarray(inp["w_ffn_in"], f32),
        "w_ffn_out": np.ascontiguousarray(inp["w_ffn_out"], f32),
        "pool_w": np.ascontiguousarray(inp["pool_w"], f32),
        "w_qkv": np.ascontiguousarray(np.asarray(inp["w_qkv"], f32)[0]),
        "w_o": np.ascontiguousarray(np.asarray(inp["w_o"], f32)[0]),
        "rel_bias": np.ascontiguousarray(np.asarray(inp["rel_bias"], f32).reshape(32, 48)),
        "conv_w_in": np.ascontiguousarray(np.asarray(inp["conv_w_in"], f32)[0]),
        "conv_w_out": np.ascontiguousarray(np.asarray(inp["conv_w_out"], f32)[0]),
        "oh": oh, "negr": negr,
    }
    maps = []
    for i in range(8):
        seq, half = i // 2, i % 2
        b0 = 4 * i
        if half == 0:
            xin = np.concatenate([np.zeros((2048, D), f32), xp[seq, 0:2048]], axis=0)
        else:
            xin = np.ascontiguousarray(xp[seq])
        invc = np.zeros((2, 4, 16), f32)
        for pos in range(2):
            start = (half == 1) if pos == 0 else (half == 0)
            for g in range(4):
                w = 2 << g
                for j in range(16):
                    invc[pos, g, j] = 1.0 / (min(j + 1, w) if start else w)
        m = dict(shared)
        m["xin"] = xin
        m["xsm"] = np.ascontiguousarray(np.asarray(inp["x_sample"], f32)[b0:b0 + 4, 0])
        m["flag"] = np.full((128, 16), float(half), f32)
        m["invc"] = np.ascontiguousarray(np.broadcast_to(invc.reshape(1, 128), (128, 128)))
        m["spool"] = np.ascontiguousarray(np.asarray(inp["state_pool"], f32)[:, b0:b0 + 4].reshape(2, 60, D))
        m["sconv"] = np.ascontiguousarray(np.asarray(inp["state_conv"], f32)[0, b0:b0 + 4].reshape(120, D))
        for g in range(3):
            m[f"cache{g}"] = np.ascontiguousarray(np.asarray(inp[f"cache_swa_g{g}"], f32)[0, b0:b0 + 4].reshape(4, WINS[g], 2, D))
        maps.append(m)
    return maps


def _assemble(res):
    f32 = np.float32
    y = np.zeros((4, 4096, D), f32)
    ys = np.zeros((32, 1, D), f32)
    poolp = np.zeros((2, 4, 15, D), f32)
    pools = np.zeros((2, 32, 15, D), f32)
    swp = [np.zeros((1, 4, WINS[g], 2, 16, 64), f32) for g in range(3)]
    sws = [np.zeros((1, 32, 1, 2, 16, 64), f32) for g in range(3)]
    convp = np.zeros((1, 4, 30, D), f32)
    convs = np.zeros((1, 32, 30, D), f32)
    for i in range(8):
        r = res[i]
        seq, half = i // 2, i % 2
        b0 = 4 * i
        y[seq, half * 2048:(half + 1) * 2048] = r["y"]
        ys[b0:b0 + 4, 0] = r["ys"]
        pools[:, b0:b0 + 4] = r["pools"]
        convs[0, b0:b0 + 4] = r["convs"]
        for g in range(3):
            sws[g][0, b0:b0 + 4, 0] = r[f"sws{g}"].reshape(4, 2, 16, 64)
        if half == 1:
            poolp[:, seq] = r["poolp"]
            convp[0, seq] = r["convp"]
            for g in range(3):
                swp[g][0, seq] = r[f"swp{g}"].reshape(WINS[g], 2, 16, 64)
    return (y, ys, poolp, pools, swp[0], swp[1], swp[2], sws[0], sws[1], sws[2], convp, convs)


def kernel(**inputs):
    nc = build()
    res = run_bass_kernel_spmd(nc, _in_maps(inputs), core_ids=list(range(8)))
    return _assemble(res.results)
```

```python
import contextlib
import numpy as np
import ml_dtypes
import concourse.bass as bass
import concourse.mybir as mybir
from concourse.bass_utils import run_bass_kernel_spmd

F32 = mybir.dt.float32
BF16 = mybir.dt.bfloat16
AF = mybir.ActivationFunctionType
ALU = mybir.AluOpType
ENGS = ["pe", "act", "dve", "pool", "sp"]

D = 1024
KC = 8
DFF = 2816
FC = 22
NE = 2180
SMP0 = 2176
SMAX = 1156
RMS_EPS = 1e-6
LN_EPS = 1e-5
DILS = (1, 4, 16)
WINS = (128, 512, 2048)
NEG = -30000.0
QROW = 6144
VROW = 3 * 16 * 65
UROW = 16 * 65


class Buf:
    __slots__ = ("name", "lw", "rd", "multi")

    def __init__(self, name, alias=(), multi=False):
        self.name = name
        self.lw = [] if multi else None
        self.rd = list(alias)
        self.multi = multi


class Op:
    __slots__ = ("eng", "fn", "waits", "signal", "cnt", "isdma", "slot", "sem", "semval", "prevval", "seq")


class Prog:
    def __init__(self, nc, ksem=8):
        self.nc = nc
        self.ops = {e: [] for e in ENGS}
        self.ksem = ksem
        self.ndma = {e: 0 for e in ENGS}
        self.seq = 0
        self.frontier = []

    def new_phase(self):
        fr = []
        for e in ENGS:
            last = None
            lastd = {}
            for o in self.ops[e]:
                if o.isdma:
                    lastd[o.slot] = o
                else:
                    last = o
            if last is not None:
                fr.append(last)
            fr.extend(lastd.values())
        self.frontier = fr

    def buf(self, name, multi=False):
        return Buf(name, alias=self.frontier, multi=multi)

    def op(self, eng, fn, reads=(), writes=(), dma=False):
        o = Op()
        o.eng, o.fn, o.isdma, o.signal, o.waits, o.cnt = eng, fn, dma, dma, [], 0
        o.seq = self.seq
        self.seq += 1
        if dma:
            o.slot = self.ndma[eng] % self.ksem
            o.semval = 16 * (self.ndma[eng] // self.ksem + 1)
            o.prevval = o.semval - 16
            self.ndma[eng] += 1
        hard, war = set(), set()
        for b in reads:
            if b.multi:
                hard.update(b.lw)
            elif b.lw is not None:
                hard.add(b.lw)
        for b in writes:
            if b.multi:
                war.update(b.rd)
            else:
                if b.lw is not None:
                    hard.add(b.lw)
                war.update(b.rd)
        for d in hard | war:
            if d is o:
                continue
            if (not d.isdma) and (not dma) and d.eng == eng and eng == "pe":
                continue
            d.signal = True
            o.waits.append(d)
        for b in reads:
            if not dma:
                b.rd = [r for r in b.rd if r.isdma or r.eng != eng]
            b.rd.append(o)
        for b in writes:
            if b.multi:
                b.lw.append(o)
            else:
                b.lw = o
                b.rd = []
        self.ops[eng].append(o)
        return o

    def dma(self, eng, out, in_, reads=(), writes=(), **kw):
        return self.op(eng, lambda e: e.dma_start(out=out, in_=in_, **kw), reads, writes, dma=True)

    def finalize(self, st):
        nc = self.nc
        esem = {e: st.enter_context(nc.semaphore("es_" + e)) for e in ENGS}
        dsem = {e: [st.enter_context(nc.semaphore(f"ds_{e}{i}")) for i in range(self.ksem)] for e in ENGS}
        finals = []
        for e in ENGS:
            c = 0
            lastd = {}
            for o in self.ops[e]:
                if o.isdma:
                    o.sem = dsem[e][o.slot]
                    lastd[o.slot] = (o.sem, o.semval)
                elif o.signal:
                    c += 1
                    o.cnt = c
            finals.append((esem[e], c))
            finals.extend(lastd.values())
        block = st.enter_context(nc.Block())

        def run(eo, e):
            seen = {}

            def need(sem, val):
                if val > 0 and seen.get(id(sem), 0) < val:
                    eo.wait_ge(sem, val)
                    seen[id(sem)] = val

            for o in self.ops[e]:
                if o.isdma:
                    need(o.sem, o.prevval)
                for d in sorted(o.waits, key=lambda d: d.seq):
                    if d.isdma:
                        need(d.sem, d.semval)
                    else:
                        need(esem[d.eng], d.cnt)
                inst = o.fn(eo)
                if o.isdma:
                    inst.then_inc(o.sem, 16)
                elif o.signal:
                    inst.then_inc(esem[e], 1)
            if e == "sp":
                for sem, val in finals:
                    need(sem, val)

        @block.tensor
        def _(eo):
            run(eo, "pe")

        @block.scalar
        def _(eo):
            run(eo, "act")

        @block.vector
        def _(eo):
            run(eo, "dve")

        @block.gpsimd
        def _(eo):
            run(eo, "pool")

        @block.sync
        def _(eo):
            run(eo, "sp")


def I(name, *a, **kw):
    return lambda e: getattr(e, name)(*a, **kw)


def ntiles(c0, c1):
    out = []
    e = min(c1, SMP0)
    a = c0
    while a < e:
        n = min(512, e - a)
        out.append((a, n))
        a += n
    if c1 > SMP0:
        out.append((SMP0, c1 - SMP0))
    return out


def t5_bucket(dist):
    import math
    max_exact = 16
    d = np.maximum(np.asarray(dist), 0)
    large = max_exact + (np.log(np.maximum(d, max_exact) / max_exact) / math.log(2048 / max_exact) * (32 - max_exact)).astype(np.int64)
    large = np.minimum(large, 31)
    return np.where(d < max_exact, d, large).astype(np.int32)


def static_tables():
    oh = np.zeros((3, 32, 512), np.float32)
    neg = np.zeros((3, 1, 512), np.float32)
    for g, dil in enumerate(DILS):
        for j in range(256):
            if j <= 127:
                oh[g, t5_bucket((1 + j) * dil), j] = 1.0
            else:
                neg[g, 0, j] = NEG
            if 127 <= j <= 254:
                oh[g, t5_bucket((j - 127) * dil), 256 + j] = 1.0
            else:
                neg[g, 0, 256 + j] = NEG
    return oh, neg


def build(layers=(0, 1, 2, 3), skip_ffn=False, attn_stop=None):
    nc = bass.Bass("TRN2", target_bir_lowering=False)
    P = Prog(nc)
    st = contextlib.ExitStack()

    def din(name, shape, dt=F32):
        return nc.dram_tensor(name, list(shape), dt, kind="ExternalInput").ap()

    def dout(name, shape, dt=F32):
        return nc.dram_tensor(name, list(shape), dt, kind="ExternalOutput").ap()

    def dscr(name, shape, dt):
        return nc.dram_tensor(name, list(shape), dt, kind="Internal").ap()

    xin = din("xin", [4096, D])
    xsm = din("xsm", [4, D])
    flag_d = din("flag", [128, 16])
    invc_d = din("invc", [128, 2 * 4 * 16])
    spool_d = din("spool", [2, 60, D])
    sconv_d = din("sconv", [120, D])
    cache_d = [din(f"cache{g}", [4, WINS[g], 2, D]) for g in range(3)]
    vecs_d = din("vecs", [64, D])
    w_in_d = din("w_ffn_in", [4, D, 2 * DFF])
    w_out_d = din("w_ffn_out", [4, DFF, D])
    pool_w_d = din("pool_w", [2, 4, 256, 256])
    w_qkv_d = din("w_qkv", [D, 9216])
    w_o_d = din("w_o", [D, D])
    relb_d = din("rel_bias", [32, 48])
    cw_in_d = din("conv_w_in", [D, 2 * D])
    cw_out_d = din("conv_w_out", [D, D])
    oh_d = din("oh", [3, 32, 512])
    negr_d = din("negr", [3, 1, 512])

    y_d = dout("y", [2048, D])
    ys_d = dout("ys", [4, D])
    poolp_d = dout("poolp", [2, 15, D])
    pools_d = dout("pools", [2, 4, 15, D])
    swp_d = [dout(f"swp{g}", [WINS[g], 2, D]) for g in range(3)]
    sws_d = [dout(f"sws{g}", [4, 2, D]) for g in range(3)]
    convp_d = dout("convp", [30, D])
    convs_d = dout("convs", [4, 30, D])

    h1c_d = dscr("h1c", [128, KC, 2048], BF16)
    qk_d = dscr("qkd", [4100, QROW], BF16)
    v_d = dscr("vd", [4100, VROW], BF16)
    uz_d = dscr("uzd", [3, 2176, UROW], F32)
    vec_d = dscr("vecd", [3, 16, 512], F32)

    wb_in = dscr("wb_in", [4, D, 2 * DFF], BF16)
    wb_out = dscr("wb_out", [4, DFF, D], BF16)
    wb_pool = dscr("wb_pool", [2, 4, 256, 256], BF16)
    wb_qkv = dscr("wb_qkv", [D, 9216], BF16)
    wb_o = dscr("wb_o", [D, D], BF16)
    wb_cin = dscr("wb_cin", [D, 2 * D], BF16)
    wb_cout = dscr("wb_cout", [D, D], BF16)
    bWIN = [Buf(f"wbin{i}") for i in range(4)]
    bWOUT = [Buf(f"wbout{i}") for i in range(4)]
    bWPOOL = [Buf("wbpool0"), Buf("wbpool1")]
    bWQKV, bWO_, bWCIN, bWCOUT = Buf("wbqkv"), Buf("wbo"), Buf("wbcin"), Buf("wbcout")

    def cast2d(dst, src, buf):
        P.dma("pool", dst.rearrange("(p a) n -> p (a n)", p=128), src.rearrange("(p a) n -> p (a n)", p=128), writes=[buf])

    bANCH = [Buf(f"anchor{i}") for i in range(3)]

    def cast_group(i):
        rd = [] if i == 0 else [bANCH[i - 1]]
        def c2(dst, src, buf):
            P.dma("pool", dst.rearrange("(p a) n -> p (a n)", p=128), src.rearrange("(p a) n -> p (a n)", p=128), reads=rd, writes=[buf])
        if i == 0:
            c2(wb_pool[0].rearrange("g k n -> (g k) n"), pool_w_d[0].rearrange("g k n -> (g k) n"), bWPOOL[0])
            c2(wb_in[0], w_in_d[0], bWIN[0])
            c2(wb_out[0], w_out_d[0], bWOUT[0])
        elif i == 1:
            c2(wb_qkv, w_qkv_d, bWQKV)
            c2(wb_o, w_o_d, bWO_)
            c2(wb_in[1], w_in_d[1], bWIN[1])
            c2(wb_out[1], w_out_d[1], bWOUT[1])
        elif i == 2:
            c2(wb_cin, cw_in_d, bWCIN)
            c2(wb_cout, cw_out_d, bWCOUT)
            c2(wb_in[2], w_in_d[2], bWIN[2])
            c2(wb_out[2], w_out_d[2], bWOUT[2])
        else:
            c2(wb_pool[1].rearrange("g k n -> (g k) n"), pool_w_d[1].rearrange("g k n -> (g k) n"), bWPOOL[1])
            c2(wb_in[3], w_in_d[3], bWIN[3])
            c2(wb_out[3], w_out_d[3], bWOUT[3])

    def anchor(i):
        dve(I("tensor_copy", out=SMALL[:, 200:201], in_=SMALL[:, 201:202]), reads=[bRS], writes=[bANCH[i]])
        cast_group(i + 1)

    def sb(name, shape, dt):
        return st.enter_context(nc.sbuf_tensor(name, list(shape), dt))

    XT = sb("XT", [128, KC, NE], F32)
    AR = sb("AR", [128, 28000], F32)
    VT = sb("VT", [128, KC, 64], F32)
    IDF = sb("IDF", [128, 128], F32)
    IDB = sb("IDB", [128, 128], BF16)
    JF = sb("JF", [128, 128], F32)
    ONB = sb("ONB", [128, 128], BF16)
    ONF = sb("ONF", [128, 128], F32)
    FLG = sb("FLG", [128, 16], F32)
    INVC = sb("INVC", [128, 2, 4, 16], F32)
    SQS = sb("SQS", [128, 4096], BF16)
    SQ8 = SQS[:].rearrange("p (c n) -> p c n", c=KC)
    SIG = SQS[:, 0:2048].bitcast(F32).rearrange("p (s n) -> p s n", s=2)
    RSTD = sb("RSTD", [128, SMAX], F32)
    STG = sb("STG", [128, 2, D], F32)
    HT15 = sb("HT15", [128, KC, 15], F32)
    UT30 = sb("UT30", [128, KC, 30], F32)
    SMALL = sb("SMALL", [128, 256], F32)
    PS = st.enter_context(nc.psum_tensor("PS", [128, 8, 512], F32))

    PB = [Buf(f"ps{b}") for b in range(8)]
    XB = [Buf(f"xt{t}") for t in range(18)]
    bVT, bIDF, bIDB, bJF, bONB, bONF, bNH, bFLG, bINVC = (Buf(n) for n in "VT IDF IDB JF ONB ONF NH FLG INVC".split())
    bSQ2 = Buf("sq2")
    bSIG = [Buf("sig0"), Buf("sig1")]
    bSTG = [Buf("stg0"), Buf("stg1")]
    bHT15, bUT30 = Buf("ht15"), Buf("ut30")
    bRS = Buf("rstd")
    bSMALL = Buf("small")
    bOUT = Buf("out", multi=True)
    bH1C = Buf("h1c", multi=True)
    bQK = Buf("qkd", multi=True)
    bV = Buf("vd", multi=True)
    bUZ = Buf("uzd", multi=True)
    bVEC = Buf("vecd", multi=True)

    def xtb(a, n):
        out = []
        for t in range(17):
            if a < (t + 1) * 128 and a + n > t * 128:
                out.append(XB[t])
        if a + n > SMP0:
            out.append(XB[17])
        return out

    def act(fn, reads=(), writes=()):
        return P.op("act", fn, reads, writes)

    def dve(fn, reads=(), writes=()):
        return P.op("dve", fn, reads, writes)

    def pool(fn, reads=(), writes=()):
        return P.op("pool", fn, reads, writes)

    def pe(fn, reads=(), writes=()):
        return P.op("pe", fn, reads, writes)

    def arv(off, nbytes, dt, pat=None, **kw):
        assert off % 4 == 0 and nbytes % 4 == 0 and off + nbytes <= 28000 * 4, (off, nbytes)
        v = AR[:, off // 4:(off + nbytes) // 4]
        if dt == BF16:
            v = v.bitcast(BF16)
        if pat:
            v = v.rearrange(pat, **kw)
        return v

    stg_ctr = [0]
    bank2_ctr = [0]

    pool(I("memset", IDF[:], 1.0), writes=[bIDF])
    pool(I("affine_select", out=IDF[:], in_=IDF[:], pattern=[[-1, 128]], compare_op=ALU.is_equal, fill=0.0, base=0, channel_multiplier=1), reads=[bIDF], writes=[bIDF])
    pool(I("tensor_copy", out=IDB[:], in_=IDF[:]), reads=[bIDF], writes=[bIDB])
    pool(I("memset", JF[:], 1.0), writes=[bJF])
    pool(I("affine_select", out=JF[:], in_=JF[:], pattern=[[1, 128]], compare_op=ALU.is_equal, fill=0.0, base=-127, channel_multiplier=1), reads=[bJF], writes=[bJF])
    pool(I("memset", ONB[:], 1.0), writes=[bONB])
    pool(I("memset", ONF[:], 1.0), writes=[bONF])
    pool(I("memset", SMALL[:], 0.0), writes=[bSMALL])
    P.dma("sp", FLG[:], flag_d, writes=[bFLG])
    P.dma("sp", INVC[:].rearrange("p a g j -> p (a g j)"), invc_d, writes=[bINVC])

    def from_tok(src_ap, m, dst_half, reads_extra=(), writes=()):
        s = stg_ctr[0] % 2
        stg_ctr[0] += 1
        b0 = 2 * (bank2_ctr[0] % 2)
        bank2_ctr[0] += 1
        P.dma("pool", STG[0:m, s, :], src_ap, reads=list(reads_extra), writes=[bSTG[s]])
        for c in range(KC):
            pe(I("transpose", out=PS[:, b0 + c // 4, (c % 4) * m:(c % 4 + 1) * m], in_=STG[0:m, s, c * 128:(c + 1) * 128], identity=IDF[0:m, 0:m]),
               reads=[bSTG[s], bIDF], writes=[PB[b0 + c // 4]])
        act(I("activation", out=dst_half(0), in_=PS[:, b0, 0:4 * m].rearrange("p (c n) -> p c n", c=4), func=AF.Copy), reads=[PB[b0]], writes=writes)
        dve(I("tensor_copy", out=dst_half(1), in_=PS[:, b0 + 1, 0:4 * m].rearrange("p (c n) -> p c n", c=4)), reads=[PB[b0 + 1]], writes=writes)

    def to_tok(src_chunk, m, dst_ap, reads=()):
        s = stg_ctr[0] % 2
        stg_ctr[0] += 1
        b0 = 2 * (bank2_ctr[0] % 2)
        bank2_ctr[0] += 1
        for c in range(KC):
            pe(I("transpose", out=PS[0:m, b0 + c // 4, (c % 4) * 128:(c % 4 + 1) * 128], in_=src_chunk(c), identity=IDF[:]),
               reads=list(reads) + [bIDF], writes=[PB[b0 + c // 4]])
        act(I("activation", out=STG[0:m, s, 0:512], in_=PS[0:m, b0, :], func=AF.Copy), reads=[PB[b0]], writes=[bSTG[s]])
        dve(I("tensor_copy", out=STG[0:m, s, 512:1024], in_=PS[0:m, b0 + 1, :]), reads=[PB[b0 + 1]], writes=[bSTG[s]])
        P.dma("act", dst_ap, STG[0:m, s, :], reads=[bSTG[s]], writes=[bOUT])

    from_tok(vecs_d, 64, lambda h: VT[:, 4 * h:4 * h + 4, :], writes=[bVT])

    def vt(c, idx):
        return VT[:, c, idx:idx + 1]

    def G(li, k):
        return li * 4 + k

    def stats(src_half, n, rs_ap, src_bufs, eps=RMS_EPS, bank=6):
        sqb = [bSIG[0], bSIG[1], bSQ2]
        for h in range(2):
            act(I("activation", out=SQ8[:, 4 * h:4 * h + 4, :n], in_=src_half(h), func=AF.Square), reads=src_bufs, writes=sqb)
        for c in range(KC):
            pe(I("matmul", out=PS[:, bank, :n], lhsT=ONB[:], rhs=SQ8[:, c, :n], start=(c == 0), stop=(c == KC - 1)), reads=sqb + [bONB], writes=[PB[bank]])
        dve(I("tensor_scalar", out=rs_ap, in0=PS[:, bank, :n], scalar1=1.0 / D, scalar2=eps, op0=ALU.mult, op1=ALU.add), reads=[PB[bank]], writes=[bRS])
        act(I("activation", out=rs_ap, in_=rs_ap, func=AF.Ln), reads=[bRS], writes=[bRS])
        act(I("activation", out=rs_ap, in_=rs_ap, func=AF.Exp, scale=-0.5), reads=[bRS], writes=[bRS])

    def prenorm(gi, c0, c1, dst, dst_buf_):
        for ti_, (a, n) in enumerate(ntiles(c0, c1)):
            dst_buf = dst_buf_[ti_] if isinstance(dst_buf_, list) else dst_buf_
            la = a - c0
            xb = xtb(a, n)
            stats(lambda h: XT[:, 4 * h:4 * h + 4, a:a + n], n, RSTD[:, la:la + n], xb)
            for c in range(KC):
                dve(I("scalar_tensor_tensor", out=dst(c, la, n), in0=XT[:, c, a:a + n], scalar=vt(c, gi), in1=RSTD[:, la:la + n], op0=ALU.mult, op1=ALU.mult),
                    reads=xb + [bVT, bRS], writes=[dst_buf])

    def postnorm_residual(gi, c0, c1, yT, ybufs):
        for (a, n) in ntiles(c0, c1):
            la = a - c0
            xb = xtb(a, n)
            stats(lambda h: yT("h%d" % h, la, n), n, RSTD[:, la:la + n], ybufs)
            for c in range(KC):
                dve(I("tensor_tensor", out=yT(c, la, n), in0=yT(c, la, n), in1=RSTD[:, la:la + n], op=ALU.mult), reads=ybufs + [bRS], writes=ybufs)
            for c in range(KC):
                dve(I("scalar_tensor_tensor", out=XT[:, c, a:a + n], in0=yT(c, la, n), scalar=vt(c, gi), in1=XT[:, c, a:a + n], op0=ALU.mult, op1=ALU.add),
                    reads=ybufs + xb + [bVT], writes=xb)

    R1, R2, R3, RW = 0, 18496, 36992, 36992 + 50864
    assert RW + 23552 <= 112000

    def yT_view():
        lo = arv(R1, 18496, F32, "p (c n) -> p c n", c=4)
        hi = arv(R2, 18496, F32, "p (c n) -> p c n", c=4)
        def f(c, la, n):
            if isinstance(c, str):
                return (lo if c == "h0" else hi)[:, :, la:la + n]
            return (lo if c < 4 else hi)[:, c % 4, la:la + n]
        return f

    def ffn(li, c0, c1):
        if skip_ffn:
            return
        P.new_phase()
        bR1, bR2, bA = P.buf("R1"), P.buf("R2"), P.buf("actT")
        bWA = [P.buf(f"wa{i}") for i in range(2)]
        hT = arv(R1, 18496, BF16, "p (c n) -> p c n", c=KC)
        aT = arv(R3, 50864, BF16, "p (c n) -> p c n", c=FC)
        WA = [arv(RW + 8192 * i, 8192, BF16, "p (k t n) -> p k t n", k=KC, t=2) for i in range(2)]
        WB = [arv(RW + 16384, 5632, BF16, "p (j n) -> p j n", j=FC),
              STG[:].rearrange("p s n -> p (s n)")[:, 0:1408].bitcast(BF16).rearrange("p (j n) -> p j n", j=FC)]
        bWB = [[P.buf("wb0")], bSTG]
        yT = yT_view()
        tiles = ntiles(c0, c1)
        bH = [P.buf(f"hT{i}") for i in range(len(tiles))]
        prenorm(G(li, 2), c0, c1, lambda c, la, n: hT[:, c, la:la + n], bH)
        wsrc = wb_in[li].rearrange("(k p) n -> p k n", p=128)
        cnt = 0
        for j in range(FC):
            s = (j // 2) % 2
            jj = j % 2
            if jj == 0:
                jp = j // 2
                P.dma("sp", WA[s][:, :, 0, :], wsrc[:, :, jp * 256:(jp + 1) * 256], reads=[bWIN[li]], writes=[bWA[s]])
                P.dma("sp", WA[s][:, :, 1, :], wsrc[:, :, DFF + jp * 256:DFF + (jp + 1) * 256], reads=[bWIN[li]], writes=[bWA[s]])
            for ti_, (a, n) in enumerate(tiles):
                la = a - c0
                t = cnt % 2
                cnt += 1
                bg, bu = 2 * t, 2 * t + 1
                for k in range(KC):
                    pe(I("matmul", out=PS[:, bg, :n], lhsT=WA[s][:, k, 0, jj * 128:(jj + 1) * 128], rhs=hT[:, k, la:la + n], start=(k == 0), stop=(k == KC - 1)),
                       reads=[bWA[s], bH[ti_]], writes=[PB[bg]])
                for k in range(KC):
                    pe(I("matmul", out=PS[:, bu, :n], lhsT=WA[s][:, k, 1, jj * 128:(jj + 1) * 128], rhs=hT[:, k, la:la + n], start=(k == 0), stop=(k == KC - 1)),
                       reads=[bWA[s], bH[ti_]], writes=[PB[bu]])
                act(I("activation", out=SIG[:, t, :n], in_=PS[:, bg, :n], func=AF.Silu), reads=[PB[bg]], writes=[bSIG[t]])
                dve(I("tensor_tensor", out=aT[:, j, la:la + n], in0=SIG[:, t, :n], in1=PS[:, bu, :n], op=ALU.mult), reads=[bSIG[t], PB[bu]], writes=[bA])
        osrc = wb_out[li].rearrange("(j p) n -> p j n", p=128)
        cnt = 0
        for c in range(KC):
            s = c % 2
            P.dma("sp", WB[s][:, 0:11, :], osrc[:, 0:11, c * 128:(c + 1) * 128], reads=[bWOUT[li]], writes=bWB[s])
            P.dma("sp", WB[s][:, 11:22, :], osrc[:, 11:22, c * 128:(c + 1) * 128], reads=[bWOUT[li]], writes=bWB[s])
            yb = bR1 if c < 4 else bR2
            for (a, n) in tiles:
                la = a - c0
                bk = (4, 5, 7)[cnt % 3]
                cnt += 1
                for j in range(FC):
                    pe(I("matmul", out=PS[:, bk, :n], lhsT=WB[s][:, j, :], rhs=aT[:, j, la:la + n], start=(j == 0), stop=(j == FC - 1)),
                       reads=bWB[s] + [bA], writes=[PB[bk]])
                act(I("activation", out=yT(c, la, n), in_=PS[:, bk, :n], func=AF.Copy), reads=[PB[bk]], writes=[yb] + (bH if c < 4 else []))
        postnorm_residual(G(li, 3), c0, c1, yT, [bR1, bR2])

    def load_tiles(row0, col0, nt):
        for t in range(nt):
            col = col0 + t * 128
            from_tok(xin[row0 + t * 128:row0 + (t + 1) * 128, :], 128, lambda h: XT[:, 4 * h:4 * h + 4, col:col + 128], writes=xtb(col, 128))

    def load_samples():
        from_tok(xsm, 4, lambda h: XT[:, 4 * h:4 * h + 4, SMP0:SMP0 + 4], writes=[XB[17]])

    def pool_layer(li, j, c0, c1, left, fix, flag_halo, save_tail, emit_out, samples):
        P.new_phase()
        S = c1 - c0
        W = 15 + S
        HF = arv(R3, 8 * 1171 * 4, F32, "p (c n) -> p c n", c=KC)
        TA = arv(R3 + 8 * 1171 * 4, 1171 * 4, F32)
        TB_ = arv(R3 + 9 * 1171 * 4, 1171 * 4, F32)
        DG = [arv(RW + 4096 + 4624 * i, 4624, BF16, "p (k n) -> p k n", k=2) for i in range(2)]
        PW = arv(RW, 4096, BF16, "p (g k n) -> p g k n", g=4, k=2)
        bHF, bTA, bTB, bPW = P.buf("HF"), P.buf("TA"), P.buf("TB"), P.buf("PW")
        bDG = [P.buf("dg0"), P.buf("dg1")]
        bR1, bR2 = P.buf("R1"), P.buf("R2")
        yT = yT_view()
        P.dma("sp", PW, wb_pool[j].rearrange("g (k p) n -> p g k n", p=128), reads=[bWPOOL[j]], writes=[bPW])
        prenorm(G(li, 0), c0, c1, lambda c, la, n: HF[:, c, 15 + la:15 + la + n], bHF)
        if left == "zero":
            pool(I("memset", HF[:, :, 0:15], 0.0), writes=[bHF])
        else:
            pool(I("tensor_copy", out=HF[:, :, 0:15], in_=HT15[:]), reads=[bHT15], writes=[bHF])
        if flag_halo:
            pool(I("tensor_scalar", out=HF[:, :, 15:15 + 128], in0=HF[:, :, 15:15 + 128], scalar1=FLG[:, 0:1], scalar2=None, op0=ALU.mult), reads=[bHF, bFLG], writes=[bHF])
        ntok = S - (4 if samples else 0)
        if save_tail:
            pool(I("tensor_copy", out=HT15[:], in_=HF[:, :, ntok:15 + ntok]), reads=[bHF], writes=[bHT15])
        if emit_out:
            to_tok(lambda c: HF[:, c, ntok:15 + ntok], 15, poolp_d[j], reads=[bHF])
        if samples:
            HS = arv(RW + 13344, 8 * 64 * 4, F32, "p (c n) -> p c n", c=KC)
            SA = arv(RW + 13344 + 2048, 256, F32)
            SB_ = arv(RW + 13344 + 2304, 256, F32)
            bHS = P.buf("HS")
            from_tok(spool_d[j], 60, lambda h: HS[:, 4 * h:4 * h + 4, :].rearrange("p c (b t) -> p c b t", t=16)[:, :, :, 0:15], writes=[bHS])
            dve(I("tensor_copy", out=HS[:].rearrange("p c (b t) -> p c b t", t=16)[:, :, :, 15], in_=HF[:, :, 15 + ntok:15 + S]), reads=[bHF], writes=[bHS])
            P.dma("sp", pools_d[j][:, 0:14, :], spool_d[j].rearrange("(b t) d -> b t d", t=15)[:, 1:15, :], writes=[bOUT])
            to_tok(lambda c: HF[:, c, 15 + ntok:15 + S], 4, pools_d[j][:, 14, :], reads=[bHF])
        for g in range(4):
            w = 2 << g
            for kk in range(2):
                c = 2 * g + kk
                src = HF[:, c, :]
                bufs = [bHF]
                shift = 1
                cur, curb = src, bHF
                for lv in range(g + 1):
                    dst, dstb = (TA, bTA) if lv % 2 == 0 else (TB_, bTB)
                    lo = 2 * shift - 1
                    dve(I("tensor_tensor", out=dst[:, lo:W], in0=cur[:, lo:W], in1=cur[:, lo - shift:W - shift], op=ALU.add),
                        reads=[curb], writes=[dstb])
                    cur, curb = dst, dstb
                    shift *= 2
                dve(I("scalar_tensor_tensor", out=DG[g % 2][:, kk, 0:S], in0=cur[:, 15:W], scalar=1.0 / w, in1=HF[:, c, 15:W], op0=ALU.mult, op1=ALU.subtract),
                    reads=[curb, bHF], writes=[bDG[g % 2]])
                for (fc, pos) in fix:
                    dve(I("tensor_tensor", out=SMALL[:, 0:15], in0=cur[:, 15 + fc:30 + fc], in1=INVC[:, pos, g, 0:15], op=ALU.mult),
                        reads=[curb, bINVC], writes=[bSMALL])
                    dve(I("tensor_tensor", out=DG[g % 2][:, kk, fc:fc + 15], in0=SMALL[:, 0:15], in1=HF[:, c, 15 + fc:30 + fc], op=ALU.subtract),
                        reads=[bSMALL, bHF], writes=[bDG[g % 2]])
                if samples:
                    cur, curb = HS[:, c, :], bHS
                    shift = 1
                    for lv in range(g + 1):
                        dst = SA if lv % 2 == 0 else SB_
                        lo = 2 * shift - 1
                        dve(I("tensor_tensor", out=dst[:, lo:64], in0=cur[:, lo:64], in1=cur[:, lo - shift:64 - shift], op=ALU.add),
                            reads=[curb], writes=[bHS])
                        cur = dst
                        shift *= 2
                    dve(I("scalar_tensor_tensor", out=DG[g % 2][:, kk, ntok:S], in0=cur.rearrange("p (b t) -> p b t", t=16)[:, :, 15], scalar=1.0 / w,
                                                                              in1=HF[:, c, 15 + ntok:15 + S], op0=ALU.mult, op1=ALU.subtract),
                        reads=[bHS, bHF], writes=[bDG[g % 2]])
            cnt = 0
            for co in range(2):
                cp = 2 * g + co
                yb = bR1 if cp < 4 else bR2
                for (a, n) in ntiles(c0, c1):
                    la = a - c0
                    bk = 4 + cnt % 2
                    cnt += 1
                    for k in range(2):
                        pe(I("matmul", out=PS[:, bk, :n], lhsT=PW[:, g, k, co * 128:(co + 1) * 128], rhs=DG[g % 2][:, k, la:la + n], start=(k == 0), stop=(k == 1)),
                           reads=[bPW, bDG[g % 2]], writes=[PB[bk]])
                    act(I("activation", out=yT(cp, la, n), in_=PS[:, bk, :n], func=AF.Copy, scale=vt(cp, 16 + j)), reads=[PB[bk], bVT], writes=[yb])
        postnorm_residual(G(li, 1), c0, c1, yT, [bR1, bR2])

    def h1_to_scratch(c0, ncols, dcol0):
        P.new_phase()
        hT = arv(R1, 18496, BF16, "p (c n) -> p c n", c=KC)
        b = P.buf("R1")
        prenorm(G(1, 0), c0, c0 + ncols, lambda c, la, n: hT[:, c, la:la + n], b)
        P.dma("sp", h1c_d[:, :, dcol0:dcol0 + ncols], hT[:, :, 0:ncols], reads=[b], writes=[bH1C])

    if 0 in layers:
        load_tiles(0, 128, 8)
        cast_group(0)
        pool_layer(0, 0, 128, 1152, "zero", [(0, 0)], False, True, False, False)
        anchor(0)
        ffn(0, 128, 1152)
        h1_to_scratch(128, 1024, 0)
        load_tiles(1024, 128, 8)
        pool_layer(0, 0, 128, 1152, "tail", [], False, True, False, False)
        ffn(0, 128, 1152)
        h1_to_scratch(128, 1024, 1024)
        for c in range(KC):
            act(I("activation", out=XT[:, c, 0:128], in_=XT[:, c, 1024:1152], func=AF.Copy), reads=[XB[8]], writes=[XB[0]])
        load_tiles(2048, 128, 8)
        pool_layer(0, 0, 128, 1152, "tail", [(0, 1)], False, True, False, False)
        anchor(1)
        ffn(0, 128, 1152)
        load_tiles(3072, 1152, 8)
        load_samples()
        pool_layer(0, 0, 1152, NE, "tail", [], False, False, True, True)
        ffn(0, 1152, NE)
    else:
        cast_group(0)
        load_tiles(1920, 0, 17)
        load_samples()

    def conv_layer(c0, c1, first):
        P.new_phase()
        S = c1 - c0
        samples = (c1 == NE)
        ntok = S - (4 if samples else 0)
        tiles = ntiles(c0, c1)
        hT = arv(R1, 18496, BF16, "p (c n) -> p c n", c=KC)
        UT = arv(R3, 8 * 1186 * 4, F32, "p (c n) -> p c n", c=KC)
        sT = arv(R3, 18496, BF16, "p (c n) -> p c n", c=KC)
        cT = yT_view()
        WA = [arv(RW + 4096 * i, 4096, BF16, "p (k n) -> p k n", k=KC) for i in range(3)]
        WO = [arv(RW + 12288 + 2048 * i, 2048, BF16, "p (k n) -> p k n", k=KC) for i in range(2)]
        US = arv(RW + 16384, 8 * 4 * 31 * 4, F32, "p (c b k) -> p c b k", c=KC, b=4)
        bR1, bR2, bUT = P.buf("R1"), P.buf("R2"), P.buf("UT")
        bWA = [P.buf(f"wa{i}") for i in range(3)]
        bWO = [P.buf(f"wo{i}") for i in range(2)]
        bUS = P.buf("US")
        prenorm(G(2, 0), c0, c1, lambda c, la, n: hT[:, c, la:la + n], bR1)
        if first:
            pool(I("memset", UT[:, :, 0:30], 0.0), writes=[bUT])
        else:
            pool(I("tensor_copy", out=UT[:, :, 0:30], in_=UT30[:]), reads=[bUT30], writes=[bUT])
        wsrc = wb_cin.rearrange("(k p) n -> p k n", p=128)
        cnt = 0
        for c in range(KC):
            s = c % 3
            P.dma("sp", WA[s][:, :, 0:128], wsrc[:, :, c * 128:(c + 1) * 128], reads=[bWCIN], writes=[bWA[s]])
            P.dma("sp", WA[s][:, :, 128:256], wsrc[:, :, D + c * 128:D + (c + 1) * 128], reads=[bWCIN], writes=[bWA[s]])
            for (a, n) in tiles:
                la = a - c0
                t = cnt % 2
                cnt += 1
                bg, bu = 2 * t, 2 * t + 1
                for k in range(KC):
                    pe(I("matmul", out=PS[:, bg, :n], lhsT=WA[s][:, k, 0:128], rhs=hT[:, k, la:la + n], start=(k == 0), stop=(k == KC - 1)), reads=[bWA[s], bR1], writes=[PB[bg]])
                for k in range(KC):
                    pe(I("matmul", out=PS[:, bu, :n], lhsT=WA[s][:, k, 128:256], rhs=hT[:, k, la:la + n], start=(k == 0), stop=(k == KC - 1)), reads=[bWA[s], bR1], writes=[PB[bu]])
                act(I("activation", out=SIG[:, t, :n], in_=PS[:, bu, :n], func=AF.Sigmoid, bias=vt(c, 19)), reads=[PB[bu], bVT], writes=[bSIG[t]])
                dve(I("scalar_tensor_tensor", out=UT[:, c, 30 + la:30 + la + n], in0=PS[:, bg, :n], scalar=vt(c, 18), in1=SIG[:, t, :n], op0=ALU.add, op1=ALU.mult),
                    reads=[PB[bg], bSIG[t], bVT], writes=[bUT])
        if first:
            pool(I("tensor_scalar", out=UT[:, :, 30:158], in0=UT[:, :, 30:158], scalar1=FLG[:, 0:1], scalar2=None, op0=ALU.mult), reads=[bUT, bFLG], writes=[bUT])
        if not samples:
            pool(I("tensor_copy", out=UT30[:], in_=UT[:, :, ntok:ntok + 30]), reads=[bUT], writes=[bUT30])
        else:
            to_tok(lambda c: UT[:, c, ntok:ntok + 30], 30, convp_d, reads=[bUT])
            from_tok(sconv_d, 120, lambda h: US[:, 4 * h:4 * h + 4, :, 0:30], writes=[bUS])
            dve(I("tensor_copy", out=US[:, :, :, 30], in_=UT[:, :, 30 + ntok:30 + S]), reads=[bUT], writes=[bUS])
            P.dma("sp", convs_d[:, 0:29, :], sconv_d.rearrange("(b t) d -> b t d", t=30)[:, 1:30, :], writes=[bOUT])
            to_tok(lambda c: UT[:, c, 30 + ntok:30 + S], 4, convs_d[:, 29, :], reads=[bUT])
            for c in range(KC):
                dve(I("tensor_tensor", out=US[:, c], in0=US[:, c], in1=VT[:, c, 24:55].unsqueeze(1).broadcast_to([128, 4, 31]), op=ALU.mult), reads=[bUS, bVT], writes=[bUS])
            dve(I("tensor_reduce", out=SMALL[:, 0:32], in_=US[:].rearrange("p c b k -> p (c b) k"), axis=mybir.AxisListType.X, op=ALU.add), reads=[bUS], writes=[bSMALL])
            for h in range(2):
                reg = arv(R1 if h == 0 else R2, 18496, F32, "p (c n) -> p c n", c=4)
                dve(I("tensor_tensor", out=reg[:, :, ntok:S], in0=SMALL[:, 16 * h:16 * h + 16].rearrange("p (c b) -> p c b", b=4),
                      in1=VT[:, 4 * h:4 * h + 4, 20:21].broadcast_to([128, 4, 4]), op=ALU.add), reads=[bSMALL, bVT], writes=[bR1 if h == 0 else bR2])
        UTB = [arv(RW, 2372, BF16), arv(RW + 20352, 2372, BF16)]
        DIAG = arv(RW + 2372, 7936, BF16, "p (k n) -> p k n", k=31)
        bUTB = [P.buf("utb0"), P.buf("utb1")]
        bDIAG = P.buf("diag")
        cnt = 0
        for c in range(KC):
            s = c % 2
            yb = bR1 if c < 4 else bR2
            act(I("activation", out=UTB[s][:, 0:30 + S], in_=UT[:, c, 0:30 + S], func=AF.Copy), reads=[bUT], writes=[bUTB[s]] + (bWA if s == 0 else []))
            dve(I("tensor_tensor", out=DIAG, in0=IDB[:].unsqueeze(1).broadcast_to([128, 31, 128]), in1=VT[:, c, 24:55].unsqueeze(2).broadcast_to([128, 31, 128]), op=ALU.mult),
                reads=[bIDB, bVT], writes=[bDIAG] + bWA)
            for (a, n) in tiles:
                if a >= SMP0:
                    continue
                la = a - c0
                bk = 4 + cnt % 2
                cnt += 1
                for k in range(31):
                    pe(I("matmul", out=PS[:, bk, :n], lhsT=DIAG[:, k, :], rhs=UTB[s][:, la + k:la + k + n], start=(k == 0), stop=(k == 30)), reads=[bDIAG, bUTB[s]], writes=[PB[bk]])
                act(I("activation", out=cT(c, la, n), in_=PS[:, bk, :n], func=AF.Identity, bias=vt(c, 20)), reads=[PB[bk], bVT], writes=[yb])
        for (a, n) in tiles:
            la = a - c0
            for c in range(KC):
                yb = bR1 if c < 4 else bR2
                pe(I("matmul", out=PS[:, 6, :n], lhsT=ONF[:], rhs=cT(c, la, n), start=(c == 0), stop=(c == KC - 1)), reads=[yb, bONF], writes=[PB[6]])
            for c in range(KC):
                yb = bR1 if c < 4 else bR2
                s = c % 2
                act(I("activation", out=STG[:, s, :n], in_=cT(c, la, n), func=AF.Square), reads=[yb], writes=[bSTG[s]])
                pe(I("matmul", out=PS[:, 7, :n], lhsT=ONF[:], rhs=STG[:, s, :n], start=(c == 0), stop=(c == KC - 1)), reads=[bSTG[s], bONF], writes=[PB[7]])
            MU = SIG[:, 0, :n]
            T2 = SIG[:, 1, :n]
            RS = RSTD[:, la:la + n]
            dve(I("tensor_scalar", out=MU, in0=PS[:, 6, :n], scalar1=1.0 / D, scalar2=None, op0=ALU.mult), reads=[PB[6]], writes=[bSIG[0]])
            dve(I("tensor_scalar", out=RS, in0=PS[:, 7, :n], scalar1=1.0 / D, scalar2=LN_EPS, op0=ALU.mult, op1=ALU.add), reads=[PB[7]], writes=[bRS])
            dve(I("tensor_tensor", out=T2, in0=MU, in1=MU, op=ALU.mult), reads=[bSIG[0]], writes=[bSIG[1]])
            dve(I("tensor_tensor", out=RS, in0=RS, in1=T2, op=ALU.subtract), reads=[bRS, bSIG[1]], writes=[bRS])
            act(I("activation", out=RS, in_=RS, func=AF.Sqrt), reads=[bRS], writes=[bRS])
            dve(I("reciprocal", out=RS, in_=RS), reads=[bRS], writes=[bRS])
            for c in range(KC):
                yb = bR1 if c < 4 else bR2
                dve(I("tensor_tensor", out=cT(c, la, n), in0=cT(c, la, n), in1=MU, op=ALU.subtract), reads=[yb, bSIG[0]], writes=[yb])
            for c in range(KC):
                yb = bR1 if c < 4 else bR2
                dve(I("tensor_tensor", out=cT(c, la, n), in0=cT(c, la, n), in1=RS, op=ALU.mult), reads=[yb, bRS], writes=[yb])
            for c in range(KC):
                yb = bR1 if c < 4 else bR2
                act(I("activation", out=sT[:, c, la:la + n], in_=cT(c, la, n), func=AF.Silu, scale=vt(c, 21), bias=vt(c, 22)), reads=[yb, bVT], writes=[bUT])
        osrc = wb_cout.rearrange("(k p) n -> p k n", p=128)
        cnt = 0
        for c in range(KC):
            s = c % 2
            P.dma("sp", WO[s], osrc[:, :, c * 128:(c + 1) * 128], reads=[bWCOUT], writes=[bWO[s]])
            yb = bR1 if c < 4 else bR2
            for (a, n) in tiles:
                la = a - c0
                bk = 4 + cnt % 2
                cnt += 1
                for k in range(KC):
                    pe(I("matmul", out=PS[:, bk, :n], lhsT=WO[s][:, k, :], rhs=sT[:, k, la:la + n], start=(k == 0), stop=(k == KC - 1)), reads=[bWO[s], bUT], writes=[PB[bk]])
                act(I("activation", out=cT(c, la, n), in_=PS[:, bk, :n], func=AF.Identity, bias=vt(c, 23)), reads=[PB[bk], bVT], writes=[yb])
        postnorm_residual(G(2, 1), c0, c1, cT, [bR1, bR2])

    def PSB(b):
        return PS[:, b, :].bitcast(BF16)

    def attn_layer():
        P.new_phase()
        H1E = arv(0, 34880, BF16, "p (c n) -> p c n", c=KC)
        H1C = arv(34880, 32768, BF16, "p (c n) -> p c n", c=KC)
        WQ = [arv(67648 + 16384 * i, 16384, BF16, "p (k n) -> p k n", k=KC) for i in range(2)]
        QS = arv(100416, 4096, BF16, "p (s n) -> p s n", s=2)
        VS = [arv(104512 + 2080 * i, 2080, BF16, "p (h d) -> p h d", h=16) for i in range(3)]
        bH1E, bH1Cs = P.buf("H1E"), P.buf("H1C")
        bWQ = [P.buf("wq0"), P.buf("wq1")]
        bQS = [P.buf("qs0"), P.buf("qs1")]
        bVS = [P.buf(f"vs{i}") for i in range(3)]
        prenorm(G(1, 0), 0, 1152, lambda c, la, n: H1E[:, c, la:la + n], bH1E)
        prenorm(G(1, 0), 1152, NE, lambda c, la, n: H1E[:, c, 1152 + la:1152 + la + n], bH1E)
        P.dma("sp", H1C, h1c_d, reads=[bH1C], writes=[bH1Cs])
        pool(I("tensor_copy", out=VS[0][:, :, 64], in_=FLG[:, 0:16]), reads=[bFLG], writes=[bVS[0]])
        for i in (1, 2):
            pool(I("memset", VS[i][:, :, 64:65], 1.0), writes=[bVS[i]])
        wsrc = wb_qkv.rearrange("(k p) n -> p k n", p=128)
        tl_ctx = [("c", t, H1C, t * 128, 128, t * 128) for t in range(15)]
        tl_e = [("e", t, H1E, t * 128, 128, 1920 + t * 128) for t in range(17)]
        tl_s = [("s", 0, H1E, SMP0, 4, 4096)]
        cntb = cq = cv = 0
        for cg in range(9):
            g, typ = cg // 3, cg % 3
            s = cg % 2
            P.dma("sp", WQ[s], wsrc[:, :, cg * 1024:(cg + 1) * 1024], reads=[bWQKV], writes=[bWQ[s]])
            tls = (tl_ctx if typ > 0 else []) + tl_e + tl_s
            for (kind, t, H, col, M, row0) in tls:
                hb = bH1Cs if kind == "c" else bH1E
                b0 = 2 * (cntb % 2)
                cntb += 1
                for hf in range(2):
                    for k in range(KC):
                        pe(I("matmul", out=PS[0:M, b0 + hf, :], lhsT=H[:, k, col:col + M], rhs=WQ[s][:, k, hf * 512:(hf + 1) * 512], start=(k == 0), stop=(k == KC - 1)),
                           reads=[hb, bWQ[s]], writes=[PB[b0 + hf]])
                if typ < 2:
                    q = cq % 2
                    cq += 1
                    act(I("activation", out=QS[0:M, q, 0:512], in_=PS[0:M, b0, :], func=AF.Copy), reads=[PB[b0]], writes=[bQS[q]])
                    dve(I("tensor_copy", out=QS[0:M, q, 512:1024], in_=PS[0:M, b0 + 1, :]), reads=[PB[b0 + 1]], writes=[bQS[q]])
                    o0 = g * 2048 + typ * 1024
                    P.dma("pool", qk_d[row0:row0 + M, o0:o0 + 1024], QS[0:M, q, :], reads=[bQS[q]], writes=[bQK])
                else:
                    isctx = kind == "c" or (kind == "e" and t == 0)
                    vi = 0 if isctx else 1 + cv % 2
                    cv += 1
                    act(I("activation", out=VS[vi][0:M, 0:8, 0:64], in_=PS[0:M, b0, :].rearrange("p (h d) -> p h d", h=8), func=AF.Copy), reads=[PB[b0]], writes=[bVS[vi]])
                    dve(I("tensor_copy", out=VS[vi][0:M, 8:16, 0:64], in_=PS[0:M, b0 + 1, :].rearrange("p (h d) -> p h d", h=8)), reads=[PB[b0 + 1]], writes=[bVS[vi]])
                    P.dma("pool", v_d[row0:row0 + M, g * 1040:(g + 1) * 1040], VS[vi][0:M].rearrange("p h d -> p (h d)"), reads=[bVS[vi]], writes=[bV])
                if typ > 0 and (kind == "s" or (kind == "e" and t >= 1)):
                    if kind == "s":
                        dst = sws_d[g][0:4, typ - 1, :]
                    else:
                        orow = (t - 1) * 128 - (2048 - WINS[g])
                        dst = None if orow < 0 else swp_d[g][orow:orow + 128, typ - 1, :]
                    if dst is not None:
                        f = stg_ctr[0] % 2
                        stg_ctr[0] += 1
                        act(I("activation", out=STG[0:M, f, 0:512], in_=PS[0:M, b0, :], func=AF.Copy), reads=[PB[b0]], writes=[bSTG[f]])
                        dve(I("tensor_copy", out=STG[0:M, f, 512:1024], in_=PS[0:M, b0 + 1, :]), reads=[PB[b0 + 1]], writes=[bSTG[f]])
                        P.dma("act", dst, STG[0:M, f, :], reads=[bSTG[f]], writes=[bOUT])

        if attn_stop == 'A':
            return
        P.new_phase()
        TB = [arv(8192 * g, 8192, BF16, "p (h t q) -> p h t q", h=16, t=2) for g in range(3)]
        TBf = [arv(8192 * g, 8192, BF16) for g in range(3)]
        HK = arv(24576, 16384, F32)
        VEC = arv(40960, 2048, F32)
        RB = arv(43008, 192, F32)
        OH = arv(43264, 2048, F32)
        NR = arv(45312, 2048, F32)
        bTB = [P.buf(f"tb{g}") for g in range(3)]
        bHK, bVECs, bRB, bOH, bNR = P.buf("HK"), P.buf("VEC"), P.buf("RB"), P.buf("OH"), P.buf("NR")
        P.dma("sp", RB[0:32, :], relb_d, writes=[bRB])
        for g in range(3):
            P.dma("sp", OH[0:32, :], oh_d[g], writes=[bOH])
            P.dma("sp", NR[0:1, :], negr_d[g], writes=[bNR])
            pe(I("matmul", out=PS[0:16, 0, :], lhsT=RB[0:32, g * 16:(g + 1) * 16], rhs=OH[0:32, :], start=True, stop=False), reads=[bRB, bOH], writes=[PB[0]])
            pe(I("matmul", out=PS[0:16, 0, :], lhsT=ONF[0:1, 0:16], rhs=NR[0:1, :], start=False, stop=True), reads=[bONF, bNR], writes=[PB[0]])
            act(I("activation", out=VEC[0:16, :], in_=PS[0:16, 0, :], func=AF.Copy), reads=[PB[0]], writes=[bVECs])
            P.dma("sp", vec_d[g], VEC[0:16, :], reads=[bVECs], writes=[bVEC])
            for h4 in range(4):
                hsrc = bass.AP(vec_d.tensor, g * 16 * 512 + h4 * 4 * 512, [[1, 128], [512, 4], [256, 2], [1, 128]])
                P.dma("sp", HK.rearrange("p (h t q) -> p h t q", h=16, t=2)[:, 4 * h4:4 * h4 + 4], hsrc, reads=[bVEC], writes=[bHK])
            for i in range(8):
                bk = 2 + i % 2
                pe(I("matmul", out=PS[:, bk, :], lhsT=JF[:], rhs=HK[:, i * 512:(i + 1) * 512], start=True, stop=True), reads=[bJF, bHK], writes=[PB[bk]])
                act(I("activation", out=TBf[g][:, i * 512:(i + 1) * 512], in_=PS[:, bk, :], func=AF.Exp), reads=[PB[bk]], writes=[bTB[g]])

        if attn_stop == 'B':
            return
        P.new_phase()
        QTOK = [arv(24576 + 2048 * i, 2048, BF16) for i in range(2)]
        KTOK = [arv(28672 + 2048 * i, 2048, BF16) for i in range(2)]
        QT = [arv(32768 + 2048 * i, 2048, BF16, "p (c n) -> p c n", c=KC) for i in range(2)]
        KT = [arv(36864 + 2048 * i, 2048, BF16, "p (c n) -> p c n", c=KC) for i in range(3)]
        VA = [arv(43008 + 2080 * i, 2080, BF16) for i in range(3)]
        PT = [arv(49248 + 1024 * i, 1024, BF16, "p (h t q) -> p h t q", h=2, t=2) for i in range(3)]
        UZS = [arv(52320 + 4160 * i, 4160, F32) for i in range(2)]
        bQTOK = [P.buf("qtok0"), P.buf("qtok1")]
        bKTOK = [P.buf("ktok0"), P.buf("ktok1")]
        bQT = [P.buf("qt0"), P.buf("qt1")]
        bKT = [P.buf(f"kt{i}") for i in range(3)]
        bVA = [P.buf(f"va{i}") for i in range(3)]
        bPT = [P.buf(f"pt{i}") for i in range(3)]
        bUZS = [P.buf("uzs0"), P.buf("uzs1")]
        blocks = []
        qi = 0
        for g, dil in enumerate(DILS):
            NB = 32 // dil
            QB0 = 16 // dil - 1
            for r in range(dil):
                first = True
                for kb in range(max(QB0 - 1, 0), NB):
                    isq = kb >= QB0
                    blocks.append(dict(g=g, dil=dil, r=r, kb=kb, isq=isq, first=first, qb=(qi % 2) if isq else None,
                                       qlo=((128 - 128 // dil) if kb == QB0 else 0)))
                    if isq:
                        qi += 1
                    first = False

        def issue_loads(i):
            B_ = blocks[i]
            g, dil, r, kb = B_["g"], B_["dil"], B_["r"], B_["kb"]
            sl, kti = i % 3, i % 2
            row0 = kb * 128 * dil + r
            P.dma("act", KTOK[kti], bass.AP(qk_d.tensor, row0 * QROW + g * 2048 + 1024, [[dil * QROW, 128], [1, 1024]]), reads=[bQK], writes=[bKTOK[kti]])
            P.dma("act", VA[sl], bass.AP(v_d.tensor, row0 * VROW + g * 1040, [[dil * VROW, 128], [1, 1040]]), reads=[bV], writes=[bVA[sl]])
            if B_["isq"]:
                qb, qlo = B_["qb"], B_["qlo"]
                P.dma("act", QTOK[qb][qlo:128, :], bass.AP(qk_d.tensor, (row0 + qlo * dil) * QROW + g * 2048, [[dil * QROW, 128 - qlo], [1, 1024]]), reads=[bQK], writes=[bQTOK[qb]])

        ui = 0
        issue_loads(0)
        for i, B_ in enumerate(blocks):
            g, dil, r, kb = B_["g"], B_["dil"], B_["r"], B_["kb"]
            if i + 1 < len(blocks):
                issue_loads(i + 1)
            sl, kti = i % 3, i % 2
            for c in range(KC):
                pe(I("transpose", out=PSB(0)[:, c * 128:(c + 1) * 128], in_=KTOK[kti][:, c * 128:(c + 1) * 128], identity=IDB[:]), reads=[bKTOK[kti], bIDB], writes=[PB[0]])
            dve(I("tensor_copy", out=KT[sl], in_=PSB(0).rearrange("p (c n) -> p c n", c=KC)), reads=[PB[0]], writes=[bKT[sl]])
            if not B_["isq"]:
                continue
            qlo, qb = B_["qlo"], B_["qb"]
            nq = 128 - qlo
            for c in range(KC):
                pe(I("transpose", out=PSB(1)[:, c * 128:(c + 1) * 128], in_=QTOK[qb][:, c * 128:(c + 1) * 128], identity=IDB[:]), reads=[bQTOK[qb], bIDB], writes=[PB[1]])
            act(I("activation", out=QT[qb], in_=PSB(1).rearrange("p (c n) -> p c n", c=KC), func=AF.Copy), reads=[PB[1]], writes=[bQT[qb]])
            has_prev = (kb >= 1) and not B_["first"]
            pc0 = 0 if has_prev else 1
            kts = ([(0, (i - 1) % 3)] if has_prev else []) + [(1, sl)]

            def heads_of(hp):
                base = 4 * (hp // 2) + (hp % 2)
                return (base, base + 2)

            def s_stage(hp):
                bank = 2 + hp % 3
                hA, hB = heads_of(hp)
                p0 = (hA % 2) * 64
                for hh, h in enumerate((hA, hB)):
                    for (pc, ks) in kts:
                        o0 = hh * 256 + pc * 128
                        pe(I("matmul", out=PS[:, bank, o0 + qlo:o0 + 128], lhsT=KT[ks][p0:p0 + 64, h // 2, :], rhs=QT[qb][p0:p0 + 64, h // 2, qlo:128], start=True, stop=True),
                           reads=[bKT[ks], bQT[qb]], writes=[PB[bank]])
                pt = hp % 3
                act(I("activation", out=PT[pt][:, :, pc0:2, qlo:128], in_=PS[:, bank, :].rearrange("p (h t q) -> p h t q", h=2, t=2)[:, :, pc0:2, qlo:128], func=AF.Exp, scale=0.125),
                    reads=[PB[bank]], writes=[bPT[pt]])
                dve(I("tensor_tensor", out=PT[pt][:, :, pc0:2, qlo:128], in0=PT[pt][:, :, pc0:2, qlo:128], in1=TB[g][:, hA:hB + 1:2, pc0:2, qlo:128], op=ALU.mult),
                    reads=[bPT[pt], bTB[g]], writes=[bPT[pt]])

            def pv_stage(hp):
                pt = hp % 3
                for hh, h in enumerate(heads_of(hp)):
                    bank = 5 + h // 7
                    off = (h % 7) * 65
                    for jx, (pc, ks) in enumerate(kts):
                        pe(I("matmul", out=PS[0:nq, bank, off:off + 65], lhsT=PT[pt][:, hh, pc, qlo:128], rhs=VA[ks][:, h * 65:(h + 1) * 65], start=(jx == 0), stop=(jx == len(kts) - 1)),
                           reads=[bPT[pt], bVA[ks]], writes=[PB[bank]])

            for ii in range(10):
                if ii < 8:
                    s_stage(ii)
                if ii >= 2:
                    pv_stage(ii - 2)
            u = ui % 2
            ui += 1
            act(I("activation", out=UZS[u][0:nq, 0:455], in_=PS[0:nq, 5, 0:455], func=AF.Copy), reads=[PB[5]], writes=[bUZS[u]])
            dve(I("tensor_copy", out=UZS[u][0:nq, 455:910], in_=PS[0:nq, 6, 0:455]), reads=[PB[6]], writes=[bUZS[u]])
            act(I("activation", out=UZS[u][0:nq, 910:1040], in_=PS[0:nq, 7, 0:130], func=AF.Copy), reads=[PB[7]], writes=[bUZS[u]])
            e0 = (kb * 128 + qlo) * dil + r - 1920
            P.dma("act", bass.AP(uz_d.tensor, (g * 2176 + e0) * UROW, [[dil * UROW, nq], [1, 1040]]), UZS[u][0:nq, :], reads=[bUZS[u]], writes=[bUZ])

        if attn_stop == 'C':
            return
        P.new_phase()
        SQK = arv(24576, 12288, BF16)
        SV = arv(36864, 6240, BF16)
        CK = [arv(43104 + 4096 * i, 4096, F32) for i in range(2)]
        CV = [arv(51296 + 4096 * i, 4096, F32) for i in range(2)]
        CVA = arv(59488, 2080, BF16, "p (h d) -> p h d", h=16)
        CVAf = arv(59488, 2080, BF16)
        PR = arv(61568, 4096, F32)
        LG = arv(65664, 64, F32)
        PEX = arv(65728, 64, F32)
        PM = arv(65792, 32, BF16)
        MSK = arv(65920, 4160, F32)
        UZA = arv(70080, 4160, F32)
        SEL = arv(74240, 1024, BF16, "p (b m) -> p b m", b=4)
        OHB = arv(75264, 64, F32, "p (b m) -> p b m", b=4)
        BD = arv(75328, 4160, F32, "p (h d) -> p h d", h=16)
        BDf = arv(75328, 4160, F32)
        RB0 = arv(79488, 192, F32)
        PRS = arv(79680, 4160, F32)
        LGS = arv(83840, 64, F32)
        PSS = arv(83904, 64, F32)
        bS = {n: P.buf(n) for n in "SQK SV CVA PR LG PEX PM MSK UZA SEL OHB BD RB0 PRS LGS PSS".split()}
        bCK = [P.buf("ck0"), P.buf("ck1")]
        bCV = [P.buf("cv0"), P.buf("cv1")]
        pool(I("memset", SEL[0:4], 1.0), writes=[bS["SEL"]])
        pool(I("affine_select", out=SEL[0:4], in_=SEL[0:4], pattern=[[-1, 4], [0, 128]], compare_op=ALU.is_equal, fill=0.0, base=0, channel_multiplier=1), reads=[bS["SEL"]], writes=[bS["SEL"]])
        pool(I("memset", OHB[0:16], 1.0), writes=[bS["OHB"]])
        pool(I("affine_select", out=OHB[0:16], in_=OHB[0:16], pattern=[[1, 4], [-1, 4]], compare_op=ALU.is_equal, fill=0.0, base=0, channel_multiplier=0), reads=[bS["OHB"]], writes=[bS["OHB"]])
        pool(I("memset", BD[0:16], 1.0), writes=[bS["BD"]])
        pool(I("affine_select", out=BD[0:16], in_=BD[0:16], pattern=[[-1, 16], [0, 65]], compare_op=ALU.is_equal, fill=0.0, base=0, channel_multiplier=1), reads=[bS["BD"]], writes=[bS["BD"]])
        pool(I("memset", CVA[:, :, 64:65], 1.0), writes=[bS["CVA"]])
        pool(I("memset", UZA[0:4, :], 0.0), writes=[bS["UZA"]])
        P.dma("sp", RB0[0:4, :], bass.AP(relb_d.tensor, 0, [[0, 4], [1, 48]]), writes=[bS["RB0"]])
        P.dma("sp", SQK[0:4, :], qk_d[4096:4100, :], reads=[bQK], writes=[bS["SQK"]])
        P.dma("sp", SV[0:4, :], v_d[4096:4100, :], reads=[bV], writes=[bS["SV"]])
        chunks = [(0, 455), (455, 455), (910, 130)]
        ci = 0
        for b in range(4):
            for g, dil in enumerate(DILS):
                cc = ci % 2
                ci += 1
                P.dma("sp", CK[cc], bass.AP(cache_d[g].tensor, b * WINS[g] * 2048, [[dil * 2048, 128], [1, 1024]]), writes=[bCK[cc]])
                P.dma("sp", CV[cc], bass.AP(cache_d[g].tensor, b * WINS[g] * 2048 + 1024, [[dil * 2048, 128], [1, 1024]]), writes=[bCV[cc]])
                for hf in range(2):
                    pe(I("matmul", out=PS[:, hf, :], lhsT=SEL[0:4, b, :], rhs=SQK[0:4, g * 2048 + hf * 512:g * 2048 + (hf + 1) * 512], start=True, stop=True), reads=[bS["SEL"], bS["SQK"]], writes=[PB[hf]])
                    dve(I("tensor_tensor", out=PR[:, hf * 512:(hf + 1) * 512], in0=CK[cc][:, hf * 512:(hf + 1) * 512], in1=PS[:, hf, :], op=ALU.mult), reads=[bCK[cc], PB[hf]], writes=[bS["PR"]])
                dve(I("tensor_reduce", out=LG[:, 0:16], in_=PR.rearrange("p (h d) -> p h d", h=16), axis=mybir.AxisListType.X, op=ALU.add), reads=[bS["PR"]], writes=[bS["LG"]])
                act(I("activation", out=PEX[:, 0:16], in_=LG[:, 0:16], func=AF.Exp, scale=0.125), reads=[bS["LG"]], writes=[bS["PEX"]])
                dve(I("tensor_tensor", out=PM[:, 0:16], in0=PEX[:, 0:16], in1=TB[g][:, :, 0, 0], op=ALU.mult), reads=[bS["PEX"], bTB[g]], writes=[bS["PM"]])
                dve(I("tensor_copy", out=CVA[:, :, 0:64], in_=CV[cc].rearrange("p (h d) -> p h d", h=16)), reads=[bCV[cc]], writes=[bS["CVA"]])
                for i3, (off, ncol) in enumerate(chunks):
                    pe(I("matmul", out=PS[0:16, 2 + i3, 0:ncol], lhsT=PM[:, 0:16], rhs=CVAf[:, off:off + ncol], start=True, stop=True), reads=[bS["PM"], bS["CVA"]], writes=[PB[2 + i3]])
                    dve(I("tensor_tensor", out=MSK[0:16, off:off + ncol], in0=PS[0:16, 2 + i3, 0:ncol], in1=BDf[0:16, off:off + ncol], op=ALU.mult), reads=[PB[2 + i3], bS["BD"]], writes=[bS["MSK"]])
                for i3, (off, ncol) in enumerate(chunks):
                    pe(I("matmul", out=PS[0:4, 5 + i3, 0:ncol], lhsT=OHB[0:16, b, :], rhs=MSK[0:16, off:off + ncol], start=True, stop=True), reads=[bS["OHB"], bS["MSK"]], writes=[PB[5 + i3]])
                    dve(I("tensor_tensor", out=UZA[0:4, off:off + ncol], in0=UZA[0:4, off:off + ncol], in1=PS[0:4, 5 + i3, 0:ncol], op=ALU.add), reads=[bS["UZA"], PB[5 + i3]], writes=[bS["UZA"]])
        for g in range(3):
            dve(I("tensor_tensor", out=PRS[0:4, 0:1024], in0=SQK[0:4, g * 2048:g * 2048 + 1024], in1=SQK[0:4, g * 2048 + 1024:g * 2048 + 2048], op=ALU.mult), reads=[bS["SQK"]], writes=[bS["PRS"]])
            dve(I("tensor_reduce", out=LGS[0:4, 0:16], in_=PRS[0:4, 0:1024].rearrange("p (h d) -> p h d", h=16), axis=mybir.AxisListType.X, op=ALU.add), reads=[bS["PRS"]], writes=[bS["LGS"]])
            dve(I("scalar_tensor_tensor", out=LGS[0:4, 0:16], in0=LGS[0:4, 0:16], scalar=0.125, in1=RB0[0:4, g * 16:(g + 1) * 16], op0=ALU.mult, op1=ALU.add), reads=[bS["LGS"], bS["RB0"]], writes=[bS["LGS"]])
            act(I("activation", out=PSS[0:4, 0:16], in_=LGS[0:4, 0:16], func=AF.Exp), reads=[bS["LGS"]], writes=[bS["PSS"]])
            prs3 = PRS[0:4, 0:1040].rearrange("p (h d) -> p h d", h=16)
            sv3 = SV[0:4, g * 1040:(g + 1) * 1040].rearrange("p (h d) -> p h d", h=16)
            dve(I("tensor_tensor", out=prs3[:, :, 0:64], in0=sv3[:, :, 0:64], in1=PSS[0:4, 0:16].unsqueeze(2).broadcast_to([4, 16, 64]), op=ALU.mult), reads=[bS["SV"], bS["PSS"], bS["PRS"]], writes=[bS["PRS"]])
            dve(I("tensor_copy", out=prs3[:, :, 64], in_=PSS[0:4, 0:16]), reads=[bS["PSS"]], writes=[bS["PRS"]])
            dve(I("tensor_tensor", out=UZA[0:4, :], in0=UZA[0:4, :], in1=PRS[0:4, 0:1040], op=ALU.add), reads=[bS["UZA"], bS["PRS"]], writes=[bS["UZA"]])

        if attn_stop == 'D1':
            return
        P.new_phase()
        ATT = arv(0, 34880, BF16, "p (c n) -> p c n", c=KC)
        UZ3 = [arv(74240 + 12480 * i, 12480, F32, "p (g n) -> p g n", g=3) for i in range(2)]
        ATK = [arv(99200 + 2048 * i, 2048, BF16) for i in range(2)]
        ZR = arv(103296, 64, F32)
        ATS = arv(103360, 2048, BF16)
        bATT, bZR, bATS = P.buf("ATT"), P.buf("ZR"), P.buf("ATS")
        bUZ3 = [P.buf("uz30"), P.buf("uz31")]
        bATK = [P.buf("atk0"), P.buf("atk1")]
        bUZA2 = P.buf("UZA2")
        uza3 = UZA[0:4, :].rearrange("p (h d) -> p h d", h=16)
        dve(I("tensor_scalar", out=ZR[0:4, 0:16], in0=uza3[:, :, 64], scalar1=1e-30, scalar2=None, op0=ALU.add), reads=[bS["UZA"]], writes=[bZR])
        dve(I("reciprocal", out=ZR[0:4, 0:16], in_=ZR[0:4, 0:16]), reads=[bZR], writes=[bZR])
        dve(I("tensor_tensor", out=ATS[0:4, 0:1024].rearrange("p (h d) -> p h d", h=16), in0=uza3[:, :, 0:64], in1=ZR[0:4, 0:16].unsqueeze(2).broadcast_to([4, 16, 64]), op=ALU.mult),
            reads=[bS["UZA"], bZR], writes=[bATS])
        for c in range(KC):
            pe(I("transpose", out=PSB(0)[:, c * 4:(c + 1) * 4], in_=ATS[0:4, c * 128:(c + 1) * 128], identity=IDB[0:4, 0:4]), reads=[bATS, bIDB], writes=[PB[0]])
        act(I("activation", out=ATT[:, :, SMP0:NE], in_=PSB(0)[:, 0:32].rearrange("p (c n) -> p c n", c=KC), func=AF.Copy), reads=[PB[0]], writes=[bATT])
        def merge_load(e):
            for gg in range(3):
                P.dma("act", UZ3[e % 2][:, gg, :], uz_d[gg, e * 128:(e + 1) * 128, :], reads=[bUZ], writes=[bUZ3[e % 2]])

        merge_load(0)
        for e in range(17):
            u3 = e % 2
            if e + 1 < 17:
                merge_load(e + 1)
            dve(I("tensor_tensor", out=UZ3[u3][:, 0, :], in0=UZ3[u3][:, 0, :], in1=UZ3[u3][:, 1, :], op=ALU.add), reads=[bUZ3[u3]], writes=[bUZ3[u3]])
            dve(I("tensor_tensor", out=UZ3[u3][:, 0, :], in0=UZ3[u3][:, 0, :], in1=UZ3[u3][:, 2, :], op=ALU.add), reads=[bUZ3[u3]], writes=[bUZ3[u3]])
            s3 = UZ3[u3][:, 0, :].rearrange("p (h d) -> p h d", h=16)
            dve(I("tensor_scalar", out=ZR[:, 0:16], in0=s3[:, :, 64], scalar1=1e-30, scalar2=None, op0=ALU.add), reads=[bUZ3[u3]], writes=[bZR])
            dve(I("reciprocal", out=ZR[:, 0:16], in_=ZR[:, 0:16]), reads=[bZR], writes=[bZR])
            dve(I("tensor_tensor", out=ATK[u3].rearrange("p (h d) -> p h d", h=16), in0=s3[:, :, 0:64], in1=ZR[:, 0:16].unsqueeze(2).broadcast_to([128, 16, 64]), op=ALU.mult),
                reads=[bUZ3[u3], bZR], writes=[bATK[u3]])
            for c in range(KC):
                pe(I("transpose", out=PSB(1)[:, c * 128:(c + 1) * 128], in_=ATK[u3][:, c * 128:(c + 1) * 128], identity=IDB[:]), reads=[bATK[u3], bIDB], writes=[PB[1]])
            act(I("activation", out=ATT[:, :, e * 128:(e + 1) * 128], in_=PSB(1).rearrange("p (c n) -> p c n", c=KC), func=AF.Copy), reads=[PB[1]], writes=[bATT])

        if attn_stop == 'D2':
            return
        P.new_phase()
        WO = [arv(107904 + 2048 * i, 2048, BF16, "p (k n) -> p k n", k=KC) for i in range(2)]
        ylo = arv(34880, 18496, F32, "p (c n) -> p c n", c=4)
        yhi = arv(53376, 18496, F32, "p (c n) -> p c n", c=4)
        bATT2, bYL, bYH = P.buf("ATT2"), P.buf("YL"), P.buf("YH")
        bWO = [P.buf("wo0"), P.buf("wo1")]
        osrc = wb_o.rearrange("(k p) n -> p k n", p=128)
        passes = [(0, 1152), (1152, NE)]

        def Y(c, col, n):
            p0 = 0 if col < 1152 else 1152
            if isinstance(c, str):
                return (ylo if c == "h0" else yhi)[:, :, col - p0:col - p0 + n]
            return (ylo if c < 4 else yhi)[:, c % 4, col - p0:col - p0 + n]

        for (c0, c1) in passes:
            cnt = 0
            for c in range(KC):
                s = c % 2
                P.dma("sp", WO[s], osrc[:, :, c * 128:(c + 1) * 128], reads=[bWO_], writes=[bWO[s]])
                yb = bYL if c < 4 else bYH
                for (a, n) in ntiles(c0, c1):
                    bk = 4 + cnt % 2
                    cnt += 1
                    for k in range(KC):
                        pe(I("matmul", out=PS[:, bk, :n], lhsT=WO[s][:, k, :], rhs=ATT[:, k, a:a + n], start=(k == 0), stop=(k == KC - 1)), reads=[bWO[s], bATT], writes=[PB[bk]])
                    act(I("activation", out=Y(c, a, n), in_=PS[:, bk, :n], func=AF.Copy), reads=[PB[bk]], writes=[yb])
            postnorm_residual(G(1, 1), c0, c1, lambda c, la, n, c0=c0: Y(c, c0 + la, n), [bYL, bYH])

    if 0 not in layers:
        anchor(0)
        anchor(1)
    if 1 in layers:
        attn_layer()
        anchor(2)
        ffn(1, 0, 1152)
        ffn(1, 1152, NE)
    if 2 in layers:
        conv_layer(0, 1152, True)
        ffn(2, 0, 1152)
        conv_layer(1152, NE, False)
        ffn(2, 1152, NE)
    if 3 in layers:
        pool_layer(3, 1, 0, 1152, "zero", [(128, 1)], True, True, False, False)
        ffn(3, 0, 1152)
        pool_layer(3, 1, 1152, NE, "tail", [], False, False, True, True)
        ffn(3, 1152, NE)

    P.new_phase()
    for t in range(16):
        col = 128 + t * 128
        to_tok(lambda c, col=col: XT[:, c, col:col + 128], 128, y_d[t * 128:(t + 1) * 128, :], reads=xtb(col, 128))
    to_tok(lambda c: XT[:, c, SMP0:NE], 4, ys_d, reads=[XB[17]])
    global LASTP
    LASTP = P
    P.finalize(st)
    st.close()
    return nc


def _in_maps(inp):
    oh, negr = static_tables()
    f32 = np.float32
    xp = np.asarray(inp["x_prompt"], f32)
    vecs = np.zeros((64, D), f32)
    vecs[0:16] = np.asarray(inp["norm_g"], f32).reshape(16, D)
    vecs[16:18] = np.asarray(inp["pool_scale"], f32)
    vecs[18:20] = np.asarray(inp["conv_b_in"], f32).reshape(2, D)
    vecs[20] = np.asarray(inp["conv_b_dw"], f32)[0]
    vecs[21] = np.asarray(inp["conv_ln_g"], f32)[0]
    vecs[22] = np.asarray(inp["conv_ln_b"], f32)[0]
    vecs[23] = np.asarray(inp["conv_b_out"], f32)[0]
    vecs[24:55] = np.asarray(inp["conv_w_dw"], f32)[0]
    shared = {
        "vecs": vecs,
        "w_ffn_in": np.ascontiguousarray(inp["w_ffn_in"], f32),
        "w_ffn_out": np.ascontiguousarray(inp["w_ffn_out"], f32),
        "pool_w": np.ascontiguousarray(inp["pool_w"], f32),
        "w_qkv": np.ascontiguousarray(np.asarray(inp["w_qkv"], f32)[0]),
        "w_o": np.ascontiguousarray(np.asarray(inp["w_o"], f32)[0]),
        "rel_bias": np.ascontiguousarray(np.asarray(inp["rel_bias"], f32).reshape(32, 48)),
        "conv_w_in": np.ascontiguousarray(np.asarray(inp["conv_w_in"], f32)[0]),
        "conv_w_out": np.ascontiguousarray(np.asarray(inp["conv_w_out"], f32)[0]),
        "oh": oh, "negr": negr,
    }
    maps = []
    for i in range(8):
        seq, half = i // 2, i % 2
        b0 = 4 * i
        if half == 0:
            xin = np.concatenate([np.zeros((2048, D), f32), xp[seq, 0:2048]], axis=0)
        else:
            xin = np.ascontiguousarray(xp[seq])
        invc = np.zeros((2, 4, 16), f32)
        for pos in range(2):
            start = (half == 1) if pos == 0 else (half == 0)
            for g in range(4):
                w = 2 << g
                for j in range(16):
                    invc[pos, g, j] = 1.0 / (min(j + 1, w) if start else w)
        m = dict(shared)
        m["xin"] = xin
        m["xsm"] = np.ascontiguousarray(np.asarray(inp["x_sample"], f32)[b0:b0 + 4, 0])
        m["flag"] = np.full((128, 16), float(half), f32)
        m["invc"] = np.ascontiguousarray(np.broadcast_to(invc.reshape(1, 128), (128, 128)))
        m["spool"] = np.ascontiguousarray(np.asarray(inp["state_pool"], f32)[:, b0:b0 + 4].reshape(2, 60, D))
        m["sconv"] = np.ascontiguousarray(np.asarray(inp["state_conv"], f32)[0, b0:b0 + 4].reshape(120, D))
        for g in range(3):
            m[f"cache{g}"] = np.ascontiguousarray(np.asarray(inp[f"cache_swa_g{g}"], f32)[0, b0:b0 + 4].reshape(4, WINS[g], 2, D))
        maps.append(m)
    return maps


def _assemble(res):
    f32 = np.float32
    y = np.zeros((4, 4096, D), f32)
    ys = np.zeros((32, 1, D), f32)
    poolp = np.zeros((2, 4, 15, D), f32)
    pools = np.zeros((2, 32, 15, D), f32)
    swp = [np.zeros((1, 4, WINS[g], 2, 16, 64), f32) for g in range(3)]
    sws = [np.zeros((1, 32, 1, 2, 16, 64), f32) for g in range(3)]
    convp = np.zeros((1, 4, 30, D), f32)
    convs = np.zeros((1, 32, 30, D), f32)
    for i in range(8):
        r = res[i]
        seq, half = i // 2, i % 2
        b0 = 4 * i
        y[seq, half * 2048:(half + 1) * 2048] = r["y"]
        ys[b0:b0 + 4, 0] = r["ys"]
        pools[:, b0:b0 + 4] = r["pools"]
        convs[0, b0:b0 + 4] = r["convs"]
        for g in range(3):
            sws[g][0, b0:b0 + 4, 0] = r[f"sws{g}"].reshape(4, 2, 16, 64)
        if half == 1:
            poolp[:, seq] = r["poolp"]
            convp[0, seq] = r["convp"]
            for g in range(3):
                swp[g][0, seq] = r[f"swp{g}"].reshape(WINS[g], 2, 16, 64)
    return (y, ys, poolp, pools, swp[0], swp[1], swp[2], sws[0], sws[1], sws[2], convp, convs)


def kernel(**inputs):
    nc = build()
    res = run_bass_kernel_spmd(nc, _in_maps(inputs), core_ids=list(range(8)))
    return _assemble(res.results)
```
